# Optimizing a Trainium2 kernel written in Bass

```python
import math
import jax, jax.numpy as jnp
from jax import lax
import numpy as np

D_MODEL = 1024
BATCH = 4
SEQ = 4096
DEPTH = 2

N_A_LAYERS = DEPTH // 2
N_B_LAYERS = DEPTH - N_A_LAYERS
CONV_WIDTH = 31
N_HEADS = 16
N_KV_HEADS = 4
HEAD_DIM = 64
GROUP = N_HEADS // N_KV_HEADS
WINDOW = 128
BLOCK = 128
ROPE_DIM = HEAD_DIM // 4
ROPE_THETA = 500000.0
PEER_HEADS = 8
PEER_NKEYS = 128
PEER_EXPERTS = PEER_NKEYS * PEER_NKEYS
PEER_KEY_HALF = 128
PEER_TOPK = 16
PEER_CHUNK = 128
LN_EPS = 1e-5
DEEPNORM_ALPHA = (2.0 * DEPTH) ** 0.25
DEEPNORM_BETA = (8.0 * DEPTH) ** -0.25
NEG_INF = -1e30

kernel_name = "yoco_conformer_swa_sink_peer_deepnorm"


def layer_norm(x, g, b):
    xf = x.astype(jnp.float32)
    mu = jnp.mean(xf, axis=-1, keepdims=True)
    xc = xf - mu
    var = jnp.mean(xc * xc, axis=-1, keepdims=True)
    y = xc * lax.rsqrt(var + LN_EPS) * g.astype(jnp.float32) + b.astype(jnp.float32)
    return y.astype(x.dtype)


def apply_partial_rope(t, positions):
    half = ROPE_DIM // 2
    inv_freq = ROPE_THETA ** (-(jnp.arange(half, dtype=jnp.float32) * 2.0 / ROPE_DIM))
    ang = positions.astype(jnp.float32)[..., None] * inv_freq
    cos = jnp.cos(ang)[:, :, None, :]
    sin = jnp.sin(ang)[:, :, None, :]
    tr = t[..., :ROPE_DIM].astype(jnp.float32)
    t1, t2 = tr[..., :half], tr[..., half:]
    rot = jnp.concatenate([t1 * cos - t2 * sin, t2 * cos + t1 * sin], axis=-1).astype(t.dtype)
    return jnp.concatenate([rot, t[..., ROPE_DIM:]], axis=-1)


def conformer_conv(x, w_in, b_in, dw, dw_b, ln_g, ln_b, w_out, b_out):
    h = x @ w_in + b_in
    a, gate = jnp.split(h, 2, axis=-1)
    h = a * jax.nn.sigmoid(gate)
    h = lax.conv_general_dilated(
        h, dw[:, None, :].astype(h.dtype), window_strides=(1,),
        padding=[(CONV_WIDTH - 1, 0)],
        dimension_numbers=("NWC", "WIO", "NWC"),
        feature_group_count=D_MODEL) + dw_b
    h = jax.nn.silu(layer_norm(h, ln_g, ln_b))
    return h @ w_out + b_out


def shared_kv(x, kv_w, positions):
    b, s, _ = x.shape
    kv = (x @ kv_w).reshape(b, s, 2, N_KV_HEADS, HEAD_DIM)
    k = apply_partial_rope(kv[:, :, 0], positions)
    v = kv[:, :, 1]
    return k, v


def sliding_window_sink_attention(q, k, v, sinks):
    b, s, _, _ = q.shape
    nb = s // BLOCK
    qb = q.reshape(b, nb, BLOCK, N_KV_HEADS, GROUP, HEAD_DIM)
    pad = jnp.zeros((b, BLOCK, N_KV_HEADS, HEAD_DIM), k.dtype)
    kb = jnp.concatenate([pad, k], axis=1).reshape(b, nb + 1, BLOCK, N_KV_HEADS, HEAD_DIM)
    vb = jnp.concatenate([pad, v], axis=1).reshape(b, nb + 1, BLOCK, N_KV_HEADS, HEAD_DIM)
    kwin = jnp.concatenate([kb[:, :-1], kb[:, 1:]], axis=2)
    vwin = jnp.concatenate([vb[:, :-1], vb[:, 1:]], axis=2)
    scores = jnp.einsum("bnqkgd,bnjkd->bnkgqj", qb, kwin).astype(jnp.float32) * (HEAD_DIM ** -0.5)
    qi = jnp.arange(BLOCK)[:, None] + BLOCK
    kj = jnp.arange(2 * BLOCK)[None, :]
    diff = qi - kj
    band = (diff >= 0) & (diff < WINDOW)
    first = (jnp.arange(nb)[:, None, None] == 0) & (kj[None] < BLOCK)
    valid = band[None] & jnp.logical_not(first)
    scores = jnp.where(valid[None, :, None, None], scores, NEG_INF)
    sink = sinks.astype(jnp.float32).reshape(1, 1, N_KV_HEADS, GROUP, 1, 1)
    m = jnp.maximum(jnp.max(scores, axis=-1, keepdims=True), sink)
    p = jnp.exp(scores - m)
    denom = jnp.sum(p, axis=-1, keepdims=True) + jnp.exp(sink - m)
    w = (p / denom).astype(v.dtype)
    out = jnp.einsum("bnkgqj,bnjkd->bnqkgd", w, vwin)
    return out.reshape(b, s, N_HEADS * HEAD_DIM)


def peer(x, w_q, sub_keys, w_u, w_v):
    b, s, d = x.shape
    xt = x.reshape(-1, d)
    t = xt.shape[0]
    q = (xt @ w_q).reshape(t, PEER_HEADS, 2, PEER_KEY_HALF)
    sc = jnp.einsum("thpc,pnc->thpn", q, sub_keys).astype(jnp.float32)
    sv, si = lax.top_k(sc, PEER_TOPK)
    cand = (sv[:, :, 0, :, None] + sv[:, :, 1, None, :]).reshape(t, PEER_HEADS, PEER_TOPK * PEER_TOPK)
    cidx = (si[:, :, 0, :, None] * PEER_NKEYS + si[:, :, 1, None, :]).reshape(t, PEER_HEADS, PEER_TOPK * PEER_TOPK)
    best, pos = lax.top_k(cand, PEER_TOPK)
    experts = jnp.take_along_axis(cidx, pos, axis=-1)
    gates = jax.nn.softmax(best, axis=-1).astype(x.dtype)
    nc = t // PEER_CHUNK

    def chunk(args):
        xc, ec, gc = args
        h = jnp.einsum("cd,chkd->chk", xc, w_u[ec])
        a = gc * jax.nn.gelu(h, approximate=False)
        return jnp.einsum("chk,chkd->cd", a, w_v[ec])

    out = lax.map(chunk, (xt.reshape(nc, PEER_CHUNK, d),
                          experts.reshape(nc, PEER_CHUNK, PEER_HEADS, PEER_TOPK),
                          gates.reshape(nc, PEER_CHUNK, PEER_HEADS, PEER_TOPK)))
    return out.reshape(b, s, d)


def setup_inputs(seed: int = 0) -> dict:
    key = jax.random.key(seed)
    ks = jax.random.split(key, 26)
    f32 = jnp.float32
    D = D_MODEL
    nrm = lambda k, shape, scale: jax.random.normal(k, shape, f32) * scale
    x = jax.random.normal(ks[0], (BATCH, SEQ, D), f32)
    offs = jax.random.randint(ks[1], (BATCH, 1), 0, 4096, dtype=jnp.int32)
    positions = (offs + jnp.arange(SEQ, dtype=jnp.int32)[None, :]).astype(jnp.int32)
    NA, NB = N_A_LAYERS, N_B_LAYERS
    kv_w = jnp.concatenate([
        nrm(ks[2], (D, N_KV_HEADS * HEAD_DIM), D ** -0.5),
        nrm(ks[3], (D, N_KV_HEADS * HEAD_DIM), D ** -0.5 * DEEPNORM_BETA)], axis=1)
    return {
        "x": x,
        "positions": positions,
        "conv_w_in": nrm(ks[4], (NA, D, 2 * D), D ** -0.5),
        "conv_b_in": nrm(ks[5], (NA, 2 * D), 0.02),
        "conv_dw": nrm(ks[6], (NA, CONV_WIDTH, D), CONV_WIDTH ** -0.5),
        "conv_dw_b": nrm(ks[7], (NA, D), 0.02),
        "conv_ln_g": 1.0 + nrm(ks[8], (NA, D), 0.02),
        "conv_ln_b": nrm(ks[9], (NA, D), 0.02),
        "conv_w_out": nrm(ks[10], (NA, D, D), D ** -0.5 * DEEPNORM_BETA),
        "conv_b_out": nrm(ks[11], (NA, D), 0.02),
        "kv_w": kv_w,
        "attn_w_q": nrm(ks[12], (NB, D, N_HEADS * HEAD_DIM), D ** -0.5),
        "attn_sinks": nrm(ks[13], (NB, N_HEADS), 0.5),
        "attn_w_o": nrm(ks[14], (NB, N_HEADS * HEAD_DIM, D), (N_HEADS * HEAD_DIM) ** -0.5 * DEEPNORM_BETA),
        "peer_w_q": nrm(ks[15], (DEPTH, D, PEER_HEADS * 2 * PEER_KEY_HALF), D ** -0.5),
        "peer_sub_keys": nrm(ks[16], (DEPTH, 2, PEER_NKEYS, PEER_KEY_HALF), PEER_KEY_HALF ** -0.5),
        "peer_u": nrm(ks[17], (DEPTH, PEER_EXPERTS, D), D ** -0.5),
        "peer_v": nrm(ks[18], (DEPTH, PEER_EXPERTS, D), DEEPNORM_BETA * PEER_HEADS ** -0.5),
        "ln_mix_g": 1.0 + nrm(ks[19], (DEPTH, D), 0.02),
        "ln_mix_b": nrm(ks[20], (DEPTH, D), 0.02),
        "ln_ffn_g": 1.0 + nrm(ks[21], (DEPTH, D), 0.02),
        "ln_ffn_b": nrm(ks[22], (DEPTH, D), 0.02),
    }


def reference(x, positions, conv_w_in, conv_b_in, conv_dw, conv_dw_b, conv_ln_g, conv_ln_b,
              conv_w_out, conv_b_out, kv_w, attn_w_q, attn_sinks, attn_w_o,
              peer_w_q, peer_sub_keys, peer_u, peer_v,
              ln_mix_g, ln_mix_b, ln_ffn_g, ln_ffn_b):
    b, s, _ = x.shape
    k_sh = None
    v_sh = None
    for layer in range(DEPTH):
        if layer < N_A_LAYERS:
            i = layer
            mix = conformer_conv(x, conv_w_in[i], conv_b_in[i], conv_dw[i], conv_dw_b[i],
                                 conv_ln_g[i], conv_ln_b[i], conv_w_out[i], conv_b_out[i])
        else:
            j = layer - N_A_LAYERS
            q = (x @ attn_w_q[j]).reshape(b, s, N_HEADS, HEAD_DIM)
            q = apply_partial_rope(q, positions)
            o = sliding_window_sink_attention(q, k_sh, v_sh, attn_sinks[j])
            mix = o @ attn_w_o[j]
        x = layer_norm(DEEPNORM_ALPHA * x + mix, ln_mix_g[layer], ln_mix_b[layer])
        ffn = peer(x, peer_w_q[layer], peer_sub_keys[layer], peer_u[layer], peer_v[layer])
        x = layer_norm(DEEPNORM_ALPHA * x + ffn, ln_ffn_g[layer], ln_ffn_b[layer])
        if layer == N_A_LAYERS - 1:
            k_sh, v_sh = shared_kv(x, kv_w, positions)
    return x
```

```python
import contextlib
import math
import numpy as np
import concourse.bass as bass
import concourse.mybir as mybir
from concourse.bass_utils import run_bass_kernel_spmd

F32 = mybir.dt.float32
BF16 = mybir.dt.bfloat16
I32 = mybir.dt.int32
U32 = mybir.dt.uint32
ALU = mybir.AluOpType
AF = mybir.ActivationFunctionType
AX = mybir.AxisListType

D = 1024
DC = 8
ALPHA = 4.0 ** 0.25
EPS = 1e-5
NJ = 128
NSLOT = 6
PI = math.pi


class KB:
    def __init__(self, nc):
        self.nc = nc
        self.es = contextlib.ExitStack()
        self.eng = {"pe": nc.tensor, "act": nc.scalar, "dve": nc.vector,
                    "pool": nc.gpsimd, "sp": nc.sync}
        self.sem = {}
        self.cnt = {}
        for e in ("pe", "act", "dve", "pool"):
            self.sem[e] = self.es.enter_context(nc.semaphore("s_" + e))
            self.cnt[e] = 0
        self.known = {e: {} for e in self.eng}
        self.res = {}
        self.ninstr = 0
        import os
        self.limit = int(os.environ.get("KB_LIMIT", "1000000000"))

    def dsem(self, name):
        if name not in self.sem:
            self.sem[name] = self.es.enter_context(self.nc.semaphore("d_" + name))
            self.cnt[name] = 0
        return name

    def _need(self, e, reads, writes):
        need = {}

        def add(sv):
            if sv is None:
                return
            s, v = sv
            if need.get(s, 0) < v:
                need[s] = v
        for k in reads:
            r = self.res.get(k)
            if r:
                add(r["w"])
        for k in writes:
            r = self.res.get(k)
            if r:
                add(r["w"])
                for s, v in r["r"].items():
                    add((s, v))
        kn = self.known[e]
        for s, v in need.items():
            if e == "pe" and s == "pe":
                continue
            if s not in ("pe", "act", "dve", "pool"):
                v = self.cnt[s]
            if kn.get(s, 0) < v:
                self.eng[e].wait_ge(self.sem[s], v)
                kn[s] = v

    def _mark(self, s, v, reads, writes):
        for k in writes:
            self.res[k] = {"w": (s, v), "r": {}}
        for k in reads:
            r = self.res.setdefault(k, {"w": None, "r": {}})
            if r["r"].get(s, 0) < v:
                r["r"][s] = v

    def op(self, e, fn, reads=(), writes=(), sig=True):
        if self.ninstr >= self.limit:
            self.ninstr += 1
            return None
        self._need(e, reads, writes)
        ins = fn(self.eng[e])
        self.ninstr += 1
        if sig:
            self.cnt[e] += 1
            ins.then_inc(self.sem[e], 1)
            v = self.cnt[e]
        else:
            v = self.cnt[e] + 1
        self._mark(e, v, reads, writes)
        return ins

    def dma(self, q, semname, out, in_, reads=(), writes=(), **kw):
        if self.ninstr >= self.limit:
            self.ninstr += 1
            return None
        self.dsem(semname)
        self._need(q, reads, writes)
        ins = self.eng[q].dma_start(out=out, in_=in_, **kw)
        self.cnt[semname] += 16
        ins.then_inc(self.sem[semname], 16)
        self._mark(semname, self.cnt[semname], reads, writes)
        self.ninstr += 1
        return ins

    def barrier(self):
        for e in ("pe", "act", "dve", "pool", "sp"):
            kn = self.known[e]
            for s, v in self.cnt.items():
                if v > 0 and kn.get(s, 0) < v:
                    self.eng[e].wait_ge(self.sem[s], v)
                    kn[s] = v
        self.res = {}

    def finish(self, e="sp"):
        kn = self.known[e]
        for s, v in self.cnt.items():
            if v > 0 and kn.get(s, 0) < v:
                self.eng[e].wait_ge(self.sem[s], v)
                kn[s] = v


def build(S_core, dbg=False, phases=None):
    NTL = S_core // 128 + 1
    NTOK = NTL * 128
    XW = NTOK + 32
    nc = bass.Bass("TRN2", target_bir_lowering=False)

    def din(name, shape, dt=F32):
        return nc.dram_tensor(name, list(shape), dt, kind="ExternalInput").ap()

    xT_d = din("xT", [D, XW])
    xtm_d = din("xtm", [NTOK, D])
    hm_d = din("hm", [XW])
    pos_d = din("pos_pn", [128, NTL], I32)
    amask_d = din("amask", [2, 128, 256])
    invf_d = din("invf", [8])
    w_in_d = din("w_in", [D, 2 * D])
    b_in_d = din("b_in_pc", [128, 16])
    dw_d = din("dw_pc", [128, 8, 31])
    dwb_d = din("dwb_pc", [128, 8])
    clng_d = din("clng_pc", [128, 8])
    clnb_d = din("clnb_pc", [128, 8])
    w_out_d = din("w_out", [D, D])
    b_out_d = din("b_out", [D])
    kvw_d = din("kv_w", [D, 512])
    awq_d = din("attn_w_q", [D, D])
    sinks_d = din("sinks", [16])
    awo_d = din("attn_w_o", [D, D])
    pwq_d = din("peer_w_q", [2, D, 2 * D])
    keysT_d = din("keysT", [2, 128, 2, 128])
    UT_d = din("UT", [2, NJ, 128, 8, 128])
    V_d = din("peer_v", [2, 128 * NJ, D])
    lnmg_d = din("ln_mix_g", [2, D])
    lnmb_d = din("ln_mix_b", [2, D])
    lnfg_d = din("ln_ffn_g", [2, D])
    lnfb_d = din("ln_ffn_b", [2, D])
    out_d = nc.dram_tensor("out", [S_core, D], F32, kind="ExternalOutput").ap()
    skind = "ExternalOutput" if dbg else "Internal"
    xs_d = [nc.dram_tensor("xs%d" % i, [NTOK, D], F32, kind=skind).ap() for i in range(3)]

    UTb_d = nc.dram_tensor("UTb", [2, NJ, 128, D], BF16).ap()
    Vb_d = nc.dram_tensor("Vb", [2, NJ, 128, D], BF16).ap()

    k = KB(nc)
    top = k.es

    uid = [0]

    def sb(st, name, shape, dt):
        uid[0] += 1
        return st.enter_context(nc.sbuf_tensor("sb%d_%s" % (uid[0], name), list(shape), dt))

    def ps(st, name, shape, dt=F32):
        uid[0] += 1
        return st.enter_context(nc.psum_tensor("ps%d_%s" % (uid[0], name), list(shape), dt))

    xT_res = sb(top, "xT_res", [128, DC, XW], BF16)
    IDXI = sb(top, "IDXI", [128, NTOK], F32)
    IDXJ = sb(top, "IDXJ", [128, NTOK], F32)
    GATE = sb(top, "GATE", [128, NTOK], F32)
    ident = sb(top, "ident", [128, 128], F32)
    identb = sb(top, "identb", [128, 128], BF16)
    iota_f = sb(top, "iota_f", [128, 128], F32)
    iota_b = sb(top, "iota_b", [128, 128], BF16)
    iota_p = sb(top, "iota_p", [128, 1], F32)
    ones_f = sb(top, "ones_f", [128, 128], F32)
    g_b = sb(top, "g_b", [128, D], F32)
    b_b = sb(top, "b_b", [128, D], F32)
    junk = sb(top, "junk", [128, D], BF16)
    st = sb(top, "st", [128, 16], F32)

    k.op("pool", lambda e: e.iota(iota_f[:], pattern=[[1, 128]], base=0, channel_multiplier=0,
                                  allow_small_or_imprecise_dtypes=True), writes=["iota_f"])
    k.op("pool", lambda e: e.iota(iota_p[:], pattern=[[0, 1]], base=0, channel_multiplier=1,
                                  allow_small_or_imprecise_dtypes=True), writes=["iota_p"])
    k.op("dve", lambda e: e.tensor_copy(out=iota_b[:], in_=iota_f[:]), reads=["iota_f"], writes=["iota_b"])
    k.op("dve", lambda e: e.tensor_scalar(out=ident[:], in0=iota_f[:], scalar1=iota_p[:, 0:1], scalar2=None,
                                          op0=ALU.is_equal), reads=["iota_f", "iota_p"], writes=["ident"])
    k.op("dve", lambda e: e.tensor_copy(out=identb[:], in_=ident[:]), reads=["ident"], writes=["identb"])
    k.op("dve", lambda e: e.memset(ones_f[:], 1.0), writes=["ones_f"])

    xT_v = xT_d.rearrange("(c p) w -> p c w", p=128)
    for c in range(DC):
        k.dma("pool", "xTload", xT_res[:, c, :], xT_v[:, c, :], writes=["xTall"])
    xkeys_all = ["xTall"]

    def xkey(tile):
        return "xT%d" % tile

    def load_ln(g_d, b_d, layer):
        k.dma("sp", "lnp", g_b[:], g_d[layer].partition_broadcast(128), writes=["g_b"])
        k.dma("sp", "lnp", b_b[:], b_d[layer].partition_broadcast(128), writes=["b_b"])

    def layer_norm_tm(r, rkey, o, okey):
        k.op("act", lambda e: e.activation(out=junk[:], in_=r, func=AF.Identity, accum_out=st[:, 0:1]),
             reads=[rkey], writes=["junk", "st0"])
        k.op("act", lambda e: e.activation(out=junk[:], in_=r, func=AF.Square, accum_out=st[:, 1:2]),
             reads=[rkey], writes=["junk", "st1"])
        k.op("dve", lambda e: e.tensor_scalar(out=st[:, 2:3], in0=st[:, 0:1], scalar1=1.0 / D, scalar2=None,
                                              op0=ALU.mult), reads=["st0"], writes=["st2"])
        k.op("dve", lambda e: e.tensor_tensor(out=st[:, 3:4], in0=st[:, 2:3], in1=st[:, 2:3], op=ALU.mult),
             reads=["st2"], writes=["st3"])
        k.op("dve", lambda e: e.scalar_tensor_tensor(out=st[:, 4:5], in0=st[:, 1:2], scalar=1.0 / D,
                                                     in1=st[:, 3:4], op0=ALU.mult, op1=ALU.subtract),
             reads=["st1", "st3"], writes=["st4"])
        k.op("dve", lambda e: e.tensor_scalar(out=st[:, 6:7], in0=st[:, 4:5], scalar1=EPS, scalar2=None,
                                              op0=ALU.add), reads=["st4"], writes=["st6"])
        k.op("act", lambda e: e.activation(out=st[:, 7:8], in_=st[:, 6:7], func=AF.Sqrt), reads=["st6"], writes=["st7"])
        k.op("dve", lambda e: e.reciprocal(out=st[:, 5:6], in_=st[:, 7:8]), reads=["st7"], writes=["st5"])
        k.op("dve", lambda e: e.tensor_scalar(out=o, in0=r, scalar1=st[:, 2:3], scalar2=st[:, 5:6],
                                              op0=ALU.subtract, op1=ALU.mult),
             reads=[rkey, "st2", "st5"], writes=[okey])
        k.op("dve", lambda e: e.tensor_tensor(out=o, in0=o, in1=g_b[:], op=ALU.mult),
             reads=[okey, "g_b"], writes=[okey])
        k.op("dve", lambda e: e.tensor_tensor(out=o, in0=o, in1=b_b[:], op=ALU.add),
             reads=[okey, "b_b"], writes=[okey])

    def to_feature_major(src, skey, tile, tp_ps, tpkey, pkeys=None, coarse=True):
        col0 = 32 + tile * 128
        xall = xkeys_all if coarse else []
        for half in range(2):
            pk = pkeys[half] if pkeys else tpkey + str(half)
            for q in range(4):
                c = half * 4 + q
                k.op("pe", lambda e, c=c, q=q: e.transpose(out=tp_ps[half][:, q * 128:(q + 1) * 128],
                                                           in_=src[:, c * 128:(c + 1) * 128], identity=ident[:]),
                     reads=[skey, "ident"], writes=[pk], sig=(q == 3))
            eng = "act" if half == 0 else "dve"
            if eng == "act":
                k.op("act", lambda e: e.copy(out=xT_res[:, half * 4:(half + 1) * 4, col0:col0 + 128],
                                             in_=tp_ps[half][:].rearrange("p (a b) -> p a b", a=4)),
                     reads=[pk], writes=[xkey(tile)] + xall)
            else:
                k.op("dve", lambda e: e.tensor_copy(out=xT_res[:, half * 4:(half + 1) * 4, col0:col0 + 128],
                                                    in_=tp_ps[half][:].rearrange("p (a b) -> p a b", a=4)),
                     reads=[pk], writes=[xkey(tile)] + xall)

    def phase_w():
        with contextlib.ExitStack() as ph:
            NB = 3
            su = [sb(ph, "su%d" % i, [128, 2, D], F32) for i in range(NB)]
            sv_ = [sb(ph, "sv%d" % i, [128, 2, D], F32) for i in range(NB)]
            bu = [sb(ph, "bu%d" % i, [128, 2, D], BF16) for i in range(NB)]
            bv = [sb(ph, "bv%d" % i, [128, 2, D], BF16) for i in range(NB)]
            cnt = 0
            for layer in range(2):
                V_v = V_d[layer].rearrange("(i j) d -> j i d", j=NJ)
                for step in range(NJ // 2):
                    b = cnt % NB
                    cnt += 1
                    j0 = 2 * step
                    k.dma("sp", "pw_in%d" % b, su[b][:], UT_d[layer, j0:j0 + 2].rearrange("j p c i -> p j (c i)"),
                          writes=["su%d" % b])
                    k.dma("sp", "pw_in%d" % b, sv_[b][:], V_v[j0:j0 + 2].rearrange("j i d -> i j d"),
                          writes=["sv%d" % b])
                    ue = "dve" if step % 2 == 0 else "pool"
                    k.op(ue, lambda e: e.tensor_copy(out=bu[b][:], in_=su[b][:]), reads=["su%d" % b], writes=["bu%d" % b])
                    k.op("act", lambda e: e.copy(out=bv[b][:], in_=sv_[b][:]), reads=["sv%d" % b], writes=["bv%d" % b])
                    k.dma("sp", "pw_out%d" % b, UTb_d[layer, j0:j0 + 2].rearrange("j p x -> p j x"), bu[b][:],
                          reads=["bu%d" % b])
                    k.dma("sp", "pw_out%d" % b, Vb_d[layer, j0:j0 + 2].rearrange("j p x -> p j x"), bv[b][:],
                          reads=["bv%d" % b])
        k.barrier()

    def table_steps(layer, ph):
        NB = 3
        su = [sb(ph, "tsu%d" % i, [128, D], F32) for i in range(NB)]
        sv_ = [sb(ph, "tsv%d" % i, [128, D], F32) for i in range(NB)]
        bu = [sb(ph, "tbu%d" % i, [128, D], BF16) for i in range(NB)]
        bv = [sb(ph, "tbv%d" % i, [128, D], BF16) for i in range(NB)]
        V_v = V_d[layer].rearrange("(i j) d -> j i d", j=NJ)

        def load(j):
            b = j % NB
            k.dma("sp", "tw_in%d" % b, su[b][:], UT_d[layer, j].rearrange("p c i -> p (c i)"), writes=["tsu%d" % b])
            k.dma("sp", "tw_in%d" % b, sv_[b][:], V_v[j], writes=["tsv%d" % b])

        load(0)
        load(1)
        for j in range(NJ):
            b = j % NB
            if j + 2 < NJ:
                load(j + 2)
            k.op("act", lambda e: e.copy(out=bu[b][:], in_=su[b][:]), reads=["tsu%d" % b], writes=["tbu%d" % b])
            k.op("act", lambda e: e.copy(out=bv[b][:], in_=sv_[b][:]), reads=["tsv%d" % b], writes=["tbv%d" % b])
            k.dma("pool", "tw_out%d" % b, UTb_d[layer, j], bu[b][:], reads=["tbu%d" % b])
            k.dma("pool", "tw_out%d" % b, Vb_d[layer, j], bv[b][:], reads=["tbv%d" % b])
            yield j

    def phase_m0():
        with contextlib.ExitStack() as ph:
            xtile = sb(ph, "xtile", [128, 1, D], F32)
            rtile = sb(ph, "rtile", [128, 1, D], F32)
            otile = sb(ph, "otile", [128, 1, D], F32)
            wi = sb(ph, "wi", [128, DC, 2 * D], BF16)
            wo = sb(ph, "wo", [128, DC, D], BF16)
            hch = sb(ph, "hch", [128, 8, 544], BF16)
            dg = sb(ph, "dg", [128, 2, 31, 128], BF16)
            acc = sb(ph, "acc", [128, 8, 512], F32)
            hs = sb(ph, "hs", [128, 8, 512], BF16)
            sq = [sb(ph, "sq%d" % i, [128, 512], F32) for i in range(2)]
            sig = [sb(ph, "sig%d" % i, [128, 512], F32) for i in range(2)]
            gtmp = [sb(ph, "gtmp%d" % i, [128, 512], F32) for i in range(2)]
            mean_t = sb(ph, "mean_t", [128, 512], F32)
            msq_t = sb(ph, "msq_t", [128, 512], F32)
            rstd_t = sb(ph, "rstd_t", [128, 512], F32)
            hm_b = sb(ph, "hm_b", [128, XW], BF16)
            bout_b = sb(ph, "bout_b", [128, D], F32)
            b_in = sb(ph, "b_in", [128, 16], F32)
            dw = sb(ph, "dw", [128, 8, 31], F32)
            dwb = sb(ph, "dwb", [128, 8], F32)
            clng = sb(ph, "clng", [128, 8], F32)
            clnb = sb(ph, "clnb", [128, 8], F32)
            ag = [ps(ph, "ag%d" % i, [128, 512]) for i in range(4)]
            s1 = ps(ph, "s1", [128, 512])
            s2 = ps(ph, "s2", [128, 512])
            mix = ps(ph, "mix", [128, D])

            wi_v = w_in_d.rearrange("(c p) n -> p c n", p=128)
            wo_v = w_out_d.rearrange("(c p) n -> p c n", p=128)
            for c in range(DC):
                k.dma("pool", "wload", wi[:, c, :], wi_v[:, c, :], writes=["wi"])
            for c in range(DC):
                k.dma("pool", "wload", wo[:, c, :], wo_v[:, c, :], writes=["wo"])
            k.dma("pool", "wload", hm_b[:], hm_d.partition_broadcast(128), writes=["hm_b"])
            k.dma("sp", "cload", bout_b[:], b_out_d.partition_broadcast(128), writes=["bout_b"])
            k.dma("sp", "cload", b_in[:], b_in_d[:, :], writes=["cst"])
            k.dma("sp", "cload", dw[:], dw_d[:, :, :], writes=["cst"])
            k.dma("sp", "cload", dwb[:], dwb_d[:, :], writes=["cst"])
            k.dma("sp", "cload", clng[:], clng_d[:, :], writes=["cst"])
            k.dma("sp", "cload", clnb[:], clnb_d[:, :], writes=["cst"])
            load_ln(lnmg_d, lnmb_d, 0)

            def glu(col0, n, dcol0, rkeys):
                for fc in range(8):
                    b = fc % 2
                    a_ps, g_ps = ag[2 * b], ag[2 * b + 1]
                    for (pt, colsel, pk) in ((a_ps, fc, "ag%d" % (2 * b)), (g_ps, fc + 8, "ag%d" % (2 * b + 1))):
                        for dc in range(DC):
                            k.op("pe", lambda e, pt=pt, colsel=colsel, dc=dc: e.matmul(
                                pt[:, :n], lhsT=wi[:, dc, colsel * 128:(colsel + 1) * 128],
                                rhs=xT_res[:, dc, col0:col0 + n], start=(dc == 0), stop=(dc == DC - 1)),
                                reads=["wi"] + rkeys, writes=[pk], sig=(dc == DC - 1))
                    k.op("act", lambda e: e.activation(out=sig[b][:, :n], in_=g_ps[:, :n], func=AF.Sigmoid,
                                                       bias=b_in[:, fc + 8:fc + 9], scale=1.0),
                         reads=["ag%d" % (2 * b + 1), "cst"], writes=["sig%d" % b])
                    k.op("dve", lambda e: e.scalar_tensor_tensor(out=gtmp[b][:, :n], in0=a_ps[:, :n],
                                                                 scalar=b_in[:, fc:fc + 1], in1=sig[b][:, :n],
                                                                 op0=ALU.add, op1=ALU.mult),
                         reads=["ag%d" % (2 * b), "sig%d" % b, "cst"], writes=["gtmp%d" % b])
                    k.op("dve", lambda e: e.tensor_tensor(out=hch[:, fc, dcol0:dcol0 + n], in0=gtmp[b][:, :n],
                                                          in1=hm_b[:, col0:col0 + n], op=ALU.mult),
                         reads=["gtmp%d" % b, "hm_b"], writes=["hch"])

            glu(0, 32, 0, ["xTall"])
            tok0 = 0
            while tok0 < NTOK:
                n = min(512, NTOK - tok0)
                tiles = list(range(tok0 // 128, (tok0 + n) // 128))
                glu(32 + tok0, n, 32, ["xTall"] + [xkey(t) for t in tiles])
                for fc in range(8):
                    b = fc % 2
                    ak = "acc%d" % fc
                    for kk in range(31):
                        if kk % 3 == 0:
                            k.op("act", lambda e, kk=kk: e.activation(out=dg[:, b, kk, :], in_=identb[:], func=AF.Copy,
                                                                      scale=dw[:, fc, kk:kk + 1]),
                                 reads=["identb", "cst"], writes=["dgA%d" % b])
                        else:
                            k.op("dve", lambda e, kk=kk: e.tensor_scalar(out=dg[:, b, kk, :], in0=identb[:],
                                                                         scalar1=dw[:, fc, kk:kk + 1], scalar2=None,
                                                                         op0=ALU.mult),
                                 reads=["identb", "cst"], writes=["dgD%d" % b])
                    for kk in range(31):
                        k.op("pe", lambda e, kk=kk: e.matmul(ag[2 + b][:, :n], lhsT=dg[:, b, kk, :],
                                                             rhs=hch[:, fc, 2 + kk:2 + kk + n],
                                                             start=(kk == 0), stop=(kk == 30)),
                             reads=["dgA%d" % b, "dgD%d" % b, "hch"], writes=["ag%d" % (2 + b)], sig=(kk == 30))
                    k.op("act", lambda e: e.activation(out=acc[:, fc, :n], in_=ag[2 + b][:, :n], func=AF.Identity,
                                                       bias=dwb[:, fc:fc + 1], scale=1.0),
                         reads=["ag%d" % (2 + b), "cst"], writes=[ak])
                if tok0 + n < NTOK:
                    k.op("dve", lambda e: e.tensor_copy(out=hch[:, :, 0:32], in_=hch[:, :, n:n + 32]),
                         reads=["hch"] + ["acc%d" % f for f in range(8)], writes=["hch"])
                for fc in range(8):
                    b = fc % 2
                    k.op("act", lambda e: e.activation(out=sq[b][:, :n], in_=acc[:, fc, :n], func=AF.Square),
                         reads=["acc%d" % fc], writes=["sq%d" % b])
                    k.op("pe", lambda e: e.matmul(s1[:, :n], lhsT=ones_f[:], rhs=acc[:, fc, :n],
                                                  start=(fc == 0), stop=(fc == 7)),
                         reads=["acc%d" % fc, "ones_f"], writes=["s1"])
                    k.op("pe", lambda e: e.matmul(s2[:, :n], lhsT=ones_f[:], rhs=sq[b][:, :n],
                                                  start=(fc == 0), stop=(fc == 7)),
                         reads=["sq%d" % b, "ones_f"], writes=["s2"])
                k.op("dve", lambda e: e.tensor_scalar(out=mean_t[:, :n], in0=s1[:, :n], scalar1=1.0 / D, scalar2=None,
                                                      op0=ALU.mult), reads=["s1"], writes=["mean_t"])
                k.op("dve", lambda e: e.tensor_tensor(out=msq_t[:, :n], in0=mean_t[:, :n], in1=mean_t[:, :n],
                                                      op=ALU.mult), reads=["mean_t"], writes=["msq_t"])
                k.op("dve", lambda e: e.scalar_tensor_tensor(out=rstd_t[:, :n], in0=s2[:, :n], scalar=1.0 / D,
                                                             in1=msq_t[:, :n], op0=ALU.mult, op1=ALU.subtract),
                     reads=["s2", "msq_t"], writes=["rstd_t"])
                k.op("dve", lambda e: e.tensor_scalar(out=rstd_t[:, :n], in0=rstd_t[:, :n], scalar1=EPS, scalar2=None,
                                                      op0=ALU.add), reads=["rstd_t"], writes=["rstd_t"])
                k.op("act", lambda e: e.activation(out=rstd_t[:, :n], in_=rstd_t[:, :n], func=AF.Sqrt),
                     reads=["rstd_t"], writes=["rstd_t"])
                k.op("dve", lambda e: e.reciprocal(out=rstd_t[:, :n], in_=rstd_t[:, :n]),
                     reads=["rstd_t"], writes=["rstd_t"])
                for m in range(4):
                    pr = (2 * m, 2 * m + 1)
                    for fc in pr:
                        k.op("dve", lambda e, fc=fc: e.tensor_tensor(out=acc[:, fc, :n], in0=acc[:, fc, :n], in1=mean_t[:, :n],
                                                                     op=ALU.subtract),
                             reads=["acc%d" % fc, "mean_t"], writes=["acc%d" % fc])
                    for fc in pr:
                        k.op("dve", lambda e, fc=fc: e.tensor_tensor(out=acc[:, fc, :n], in0=acc[:, fc, :n], in1=rstd_t[:, :n],
                                                                     op=ALU.mult),
                             reads=["acc%d" % fc, "rstd_t"], writes=["acc%d" % fc])
                    for fc in pr:
                        k.op("act", lambda e, fc=fc: e.activation(out=hs[:, fc, :n], in_=acc[:, fc, :n], func=AF.Silu,
                                                                  bias=clnb[:, fc:fc + 1], scale=clng[:, fc:fc + 1]),
                             reads=["acc%d" % fc, "cst"], writes=["hs"])
                for ti, tile in enumerate(tiles):
                    for half in range(2):
                        for fc in range(8):
                            k.op("pe", lambda e, fc=fc: e.matmul(
                                mix[:, half * 512:(half + 1) * 512], lhsT=hs[:, fc, ti * 128:(ti + 1) * 128],
                                rhs=wo[:, fc, half * 512:(half + 1) * 512], start=(fc == 0), stop=(fc == 7)),
                                reads=["hs", "wo"], writes=["mix"], sig=(fc == 7))
                    k.dma("sp", "xt", xtile[:, 0, :], xtm_d[tile * 128:(tile + 1) * 128, :], writes=["xtile"])
                    r = rtile[:, 0, :]
                    k.op("dve", lambda e: e.scalar_tensor_tensor(out=r, in0=xtile[:, 0, :], scalar=ALPHA, in1=mix[:],
                                                                 op0=ALU.mult, op1=ALU.add),
                         reads=["xtile", "mix"], writes=["rtile"])
                    k.op("dve", lambda e: e.tensor_tensor(out=r, in0=r, in1=bout_b[:], op=ALU.add),
                         reads=["rtile", "bout_b"], writes=["rtile"])
                    o = otile[:, 0, :]
                    layer_norm_tm(r, "rtile", o, "otile")
                    k.dma("sp", "xst", xs_d[0][tile * 128:(tile + 1) * 128, :], o, reads=["otile"])
                    to_feature_major(o, "otile", tile, [ag[0], ag[1]], "ag")
                tok0 += n
        k.barrier()

    def phase_s(layer):
        with contextlib.ExitStack() as ph:
            wq = sb(ph, "wq", [128, DC, 2 * D], BF16)
            keysT = sb(ph, "keysT", [128, 2, 128], F32)
            qT = sb(ph, "qT", [128, 16, 256], F32)
            scA = sb(ph, "scA", [128, 2048], F32)
            scB = sb(ph, "scB", [128, 2048], F32)

            class _V:
                def __init__(self, t, a):
                    self.t, self.a = t, a

                def __getitem__(self, idx):
                    return self.t[:].rearrange("p (a b) -> p a b", a=self.a)[idx]
            sc = _V(scA, 16)
            scr = _V(scB, 16)
            sv = sb(ph, "sv", [128, 16, 16], F32)
            si = sb(ph, "si", [128, 16, 16], U32)
            sif = sb(ph, "sif", [128, 16, 16], F32)
            cand = _V(scA, 8)
            cscr = _V(scB, 8)
            best = sb(ph, "best", [128, 8, 16], F32)
            posu = sb(ph, "posu", [128, 8, 16], U32)
            k1u = sb(ph, "k1u", [128, 8, 16], U32)
            k2u = sb(ph, "k2u", [128, 8, 16], U32)
            k1f = sb(ph, "k1f", [128, 8, 16], F32)
            k2f = sb(ph, "k2f", [128, 8, 16], F32)
            class _V4:
                def __getitem__(self, idx):
                    return scB[:].rearrange("p (h a b) -> p h a b", h=8, a=16)[idx]
            E1 = _V4()
            E2 = E1
            idi = sb(ph, "idi", [128, 128], F32)
            idj = sb(ph, "idj", [128, 128], F32)
            gt = sb(ph, "gt", [128, 8, 16], F32)
            zs = sb(ph, "zs", [128, 8], F32)
            q_ps = [ps(ph, "q_ps%d" % i, [128, 512]) for i in range(2)]
            sc_ps = ps(ph, "sc_ps", [128, 2048])
            tp = ps(ph, "tp", [128, 512])

            wq_v = pwq_d[layer].rearrange("(c p) n -> p c n", p=128)
            for c in range(DC):
                k.dma("pool", "wload", wq[:, c, :], wq_v[:, c, :], writes=["wq"])
            k.dma("sp", "cload", keysT[:], keysT_d[layer], writes=["keysT"])
            tsteps = table_steps(0, ph) if (layer == 0 and want_tables0) else iter(())
            per_tile = (NJ + NTL - 1) // NTL

            tok0 = 128 if layer == 1 else 0
            while tok0 < NTOK:
                n = min(256, NTOK - tok0)
                tiles = list(range(tok0 // 128, (tok0 + n) // 128))
                col0 = 32 + tok0
                for hp in range(16):
                    b = hp % 2
                    for dc in range(DC):
                        k.op("pe", lambda e, dc=dc: e.matmul(q_ps[b][:, :n], lhsT=wq[:, dc, hp * 128:(hp + 1) * 128],
                                                             rhs=xT_res[:, dc, col0:col0 + n],
                                                             start=(dc == 0), stop=(dc == DC - 1)),
                             reads=["wq", "xTall"] + [xkey(t) for t in tiles], writes=["q_ps%d" % b],
                             sig=(dc == DC - 1))
                    if b == 0:
                        k.op("act", lambda e: e.copy(out=qT[:, hp, :n], in_=q_ps[b][:, :n]),
                             reads=["q_ps%d" % b], writes=["qT"])
                    else:
                        k.op("dve", lambda e: e.tensor_copy(out=qT[:, hp, :n], in_=q_ps[b][:, :n]),
                             reads=["q_ps%d" % b], writes=["qT"])
                for ti, tile in enumerate(tiles):
                    for hp in range(16):
                        k.op("pe", lambda e: e.matmul(sc_ps[:, hp * 128:(hp + 1) * 128],
                                                      lhsT=qT[:, hp, ti * 128:(ti + 1) * 128],
                                                      rhs=keysT[:, hp % 2, :], start=True, stop=True),
                             reads=["qT", "keysT"], writes=["sc_ps"], sig=(hp == 15))
                    k.op("act", lambda e: e.copy(out=sc[:].rearrange("p a b -> p (a b)"), in_=sc_ps[:]),
                         reads=["sc_ps"], writes=["sc"])
                    SCRK = ["scr%d" % i for i in range(16)]
                    for m in range(8):
                        for _ in range((per_tile + 7) // 8):
                            next(tsteps, None)
                        pr = (2 * m, 2 * m + 1)
                        for hp in pr:
                            k.op("dve", lambda e, hp=hp: e.max(out=sv[:, hp, 0:8], in_=sc[:, hp, :]),
                                 reads=["sc"], writes=["svA%d" % hp])
                        for hp in pr:
                            k.op("dve", lambda e, hp=hp: e.max_index(out=si[:, hp, 0:8], in_max=sv[:, hp, 0:8],
                                                                     in_values=sc[:, hp, :]),
                                 reads=["sc", "svA%d" % hp], writes=["siA%d" % hp])
                        for hp in pr:
                            k.op("dve", lambda e, hp=hp: e.match_replace(out=scr[:, hp, :], in_to_replace=sv[:, hp, 0:8],
                                                                         in_values=sc[:, hp, :], imm_value=-1e30),
                                 reads=["sc", "svA%d" % hp], writes=["scr%d" % hp])
                        for hp in pr:
                            k.op("dve", lambda e, hp=hp: e.max(out=sv[:, hp, 8:16], in_=scr[:, hp, :]),
                                 reads=["scr%d" % hp], writes=["svB%d" % hp])
                        for hp in pr:
                            k.op("dve", lambda e, hp=hp: e.max_index(out=si[:, hp, 8:16], in_max=sv[:, hp, 8:16],
                                                                     in_values=scr[:, hp, :]),
                                 reads=["scr%d" % hp, "svB%d" % hp], writes=["siB%d" % hp])
                    SVK = ["svA%d" % i for i in range(16)] + ["svB%d" % i for i in range(16)]
                    SIK = ["siA%d" % i for i in range(16)] + ["siB%d" % i for i in range(16)]
                    k.op("dve", lambda e: e.tensor_copy(out=sif[:], in_=si[:]), reads=SIK, writes=["sif"])
                    sv4 = sv[:].rearrange("p (h s) k -> p h s k", s=2)
                    sif4 = sif[:].rearrange("p (h s) k -> p h s k", s=2)
                    k.op("dve", lambda e: e.tensor_tensor(
                        out=cand[:].rearrange("p h (a b) -> p h a b", a=16),
                        in0=sv4[:, :, 0, :].unsqueeze(3).to_broadcast([128, 8, 16, 16]),
                        in1=sv4[:, :, 1, :].unsqueeze(2).to_broadcast([128, 8, 16, 16]), op=ALU.add),
                        reads=SVK, writes=["sc"])
                    for m in range(4):
                        pr = (2 * m, 2 * m + 1)
                        ck = lambda h: ["scr%d" % (2 * h), "scr%d" % (2 * h + 1)]
                        for h in pr:
                            k.op("dve", lambda e, h=h: e.max(out=best[:, h, 0:8], in_=cand[:, h, :]),
                                 reads=["sc"], writes=["bA%d" % h])
                        for h in pr:
                            k.op("dve", lambda e, h=h: e.max_index(out=posu[:, h, 0:8], in_max=best[:, h, 0:8],
                                                                   in_values=cand[:, h, :]),
                                 reads=["sc", "bA%d" % h], writes=["pA%d" % h])
                        for h in pr:
                            k.op("dve", lambda e, h=h: e.match_replace(out=cscr[:, h, :], in_to_replace=best[:, h, 0:8],
                                                                       in_values=cand[:, h, :], imm_value=-1e30),
                                 reads=["sc", "bA%d" % h], writes=ck(h))
                        for h in pr:
                            k.op("dve", lambda e, h=h: e.max(out=best[:, h, 8:16], in_=cscr[:, h, :]),
                                 reads=ck(h), writes=["bB%d" % h])
                        for h in pr:
                            k.op("dve", lambda e, h=h: e.max_index(out=posu[:, h, 8:16], in_max=best[:, h, 8:16],
                                                                   in_values=cscr[:, h, :]),
                                 reads=ck(h) + ["bB%d" % h], writes=["pB%d" % h])
                    BK = ["bA%d" % i for i in range(8)] + ["bB%d" % i for i in range(8)]
                    PK = ["pA%d" % i for i in range(8)] + ["pB%d" % i for i in range(8)]
                    k.op("dve", lambda e: e.tensor_single_scalar(out=k1u[:], in_=posu[:], scalar=4,
                                                                 op=ALU.logical_shift_right), reads=PK, writes=["k1u"])
                    k.op("dve", lambda e: e.tensor_single_scalar(out=k2u[:], in_=posu[:], scalar=15,
                                                                 op=ALU.bitwise_and), reads=PK, writes=["k2u"])
                    k.op("dve", lambda e: e.tensor_copy(out=k1f[:], in_=k1u[:]), reads=["k1u"], writes=["k1f"])
                    k.op("dve", lambda e: e.tensor_copy(out=k2f[:], in_=k2u[:]), reads=["k2u"], writes=["k2f"])
                    io16 = iota_f[:, 0:16].unsqueeze(1).unsqueeze(1).to_broadcast([128, 8, 16, 16])
                    for (kf, kfk, Et, Ek, side, dst, dk) in ((k1f, "k1f", E1, SCRK, 0, idi, "idi"),
                                                           (k2f, "k2f", E2, SCRK, 1, idj, "idj")):
                        k.op("dve", lambda e: e.tensor_tensor(out=Et[:], in0=kf[:].unsqueeze(3).to_broadcast([128, 8, 16, 16]),
                                                              in1=io16, op=ALU.is_equal),
                             reads=[kfk, "iota_f"], writes=Ek)
                        k.op("dve", lambda e: e.tensor_tensor(
                            out=Et[:], in0=Et[:], in1=sif4[:, :, side, :].unsqueeze(2).to_broadcast([128, 8, 16, 16]),
                            op=ALU.mult), reads=Ek + ["sif"], writes=Ek)
                        k.op("dve", lambda e: e.tensor_reduce(out=dst[:].rearrange("p (h k) -> p h k", h=8), in_=Et[:],
                                                              axis=AX.X, op=ALU.add), reads=Ek, writes=[dk])
                    k.op("dve", lambda e: e.tensor_tensor(out=gt[:], in0=best[:],
                                                          in1=best[:, :, 0:1].to_broadcast([128, 8, 16]), op=ALU.subtract),
                         reads=BK, writes=["gt"])
                    k.op("act", lambda e: e.activation(out=gt[:], in_=gt[:], func=AF.Exp), reads=["gt"], writes=["gt"])
                    k.op("dve", lambda e: e.tensor_reduce(out=zs[:], in_=gt[:], axis=AX.X, op=ALU.add),
                         reads=["gt"], writes=["zs"])
                    k.op("dve", lambda e: e.reciprocal(out=zs[:], in_=zs[:]), reads=["zs"], writes=["zs"])
                    k.op("dve", lambda e: e.tensor_tensor(out=gt[:], in0=gt[:],
                                                          in1=zs[:].unsqueeze(2).to_broadcast([128, 8, 16]), op=ALU.mult),
                         reads=["gt", "zs"], writes=["gt"])
                    srcs = ((idi[:], "idi", IDXI, "IDXI"), (idj[:], "idj", IDXJ, "IDXJ"),
                            (gt[:].rearrange("p h k -> p (h k)"), "gt", GATE, "GATE"))
                    for qi, (s_ap, s_k, dst, d_k) in enumerate(srcs):
                        k.op("pe", lambda e: e.transpose(out=tp[:, qi * 128:(qi + 1) * 128], in_=s_ap, identity=ident[:]),
                             reads=[s_k, "ident"], writes=["tp"])
                    for qi, (s_ap, s_k, dst, d_k) in enumerate(srcs):
                        k.op("act", lambda e: e.copy(out=dst[:, tile * 128:(tile + 1) * 128],
                                                     in_=tp[:, qi * 128:(qi + 1) * 128]),
                             reads=["tp"], writes=[d_k])
                tok0 += n
            for _ in tsteps:
                pass
        k.barrier()

    def phase_d(layer, xin_d, xout_d, final):
        with contextlib.ExitStack() as ph:
            xtile = sb(ph, "xtile", [128, 2, D], F32)
            rtile = sb(ph, "rtile", [128, 2, D], F32)
            otile = sb(ph, "otile", [128, 2, D], F32)
            G = sb(ph, "G", [128, 256, 128], BF16)
            ub = [sb(ph, "ub%d" % i, [128, DC, 128], BF16) for i in range(NSLOT)]
            vb = [sb(ph, "vb%d" % i, [128, D], BF16) for i in range(NSLOT)]
            TB = 16
            Pb = [sb(ph, "Pb%d" % i, [128, TB, 128], BF16) for i in range(2)]
            Qb = [sb(ph, "Qb%d" % i, [128, TB, 128], BF16) for i in range(2)]
            gl = [sb(ph, "gl%d" % i, [128, 256], F32) for i in range(3)]
            ab = [sb(ph, "ab%d" % i, [128, 256], BF16) for i in range(3)]
            out_ps = [ps(ph, "out_ps%d" % i, [128, D]) for i in range(2)]
            h_ps = [ps(ph, "h_ps%d" % i, [128, 512]) for i in range(2)]
            g_ps = [ps(ph, "g_ps%d" % i, [128, 512]) for i in range(2)]
            hbuf = [(h_ps[0], "h_ps0"), (h_ps[1], "h_ps1"), (g_ps[1], "g_ps1")]
            load_ln(lnfg_d, lnfb_d, layer)

            def make_group(tile0):
                nsub = min(2, NTL - tile0)
                T = nsub * 128
                tiles = list(range(tile0, tile0 + nsub))
                t0 = tile0 * 128
                col0 = 32 + t0
                xk = [xkey(t) for t in tiles]
                def gbuild():
                    io_bc = iota_b[:].unsqueeze(1).to_broadcast([128, TB, 128])
                    for tb in range(T // TB):
                        b = tb % 2
                        ta = t0 + tb * TB
                        k.op("dve", lambda e: e.tensor_tensor(
                            out=Pb[b][:], in0=io_bc, in1=IDXI[:, ta:ta + TB].unsqueeze(2).to_broadcast([128, TB, 128]),
                            op=ALU.is_equal), reads=["iota_b", "IDXI"], writes=["Pb%d" % b])
                        k.op("dve", lambda e: e.tensor_tensor(
                            out=Qb[b][:], in0=io_bc, in1=IDXJ[:, ta:ta + TB].unsqueeze(2).to_broadcast([128, TB, 128]),
                            op=ALU.is_equal), reads=["iota_b", "IDXJ"], writes=["Qb%d" % b])
                        k.op("dve", lambda e: e.tensor_tensor(
                            out=Pb[b][:], in0=Pb[b][:], in1=GATE[:, ta:ta + TB].unsqueeze(2).to_broadcast([128, TB, 128]),
                            op=ALU.mult), reads=["Pb%d" % b, "GATE"], writes=["Pb%d" % b])
                        for q4 in range(TB // 4):
                            gb = (tb * (TB // 4) + q4) % 2
                            for u in range(4):
                                uu = q4 * 4 + u
                                k.op("pe", lambda e, u=u, uu=uu: e.matmul(
                                    g_ps[gb][:, u * 128:(u + 1) * 128], lhsT=Pb[b][:, uu, :], rhs=Qb[b][:, uu, :],
                                    start=True, stop=True),
                                    reads=["Pb%d" % b, "Qb%d" % b], writes=["g_ps%d" % gb], sig=(u == 3))
                            tq = tb * TB + q4 * 4
                            k.op("act", lambda e: e.copy(out=G[:, tq:tq + 4, :],
                                                         in_=g_ps[gb][:].rearrange("p (a b) -> p a b", a=4)),
                                 reads=["g_ps%d" % gb], writes=["G"])
                def jloop(hook=()):
                    def mm1(j):
                        s_ = j % NSLOT
                        hp_, hk = hbuf[j % 3]
                        k.dma("sp", "wu%d" % s_, ub[s_][:].rearrange("p c i -> p (c i)"), UTb_d[layer, j], writes=["wslotU%d" % s_])
                        k.dma("sp", "wv%d" % s_, vb[s_][:], Vb_d[layer, j], writes=["wslotV%d" % s_])
                        for dc in range(DC):
                            k.op("pe", lambda e, dc=dc: e.matmul(hp_[:, :T], lhsT=ub[s_][:, dc, :],
                                                                 rhs=xT_res[:, dc, col0:col0 + T],
                                                                 start=(dc == 0), stop=(dc == DC - 1)),
                                 reads=["wslotU%d" % s_] + xk, writes=[hk], sig=(dc == DC - 1))

                    def mm2(j):
                        s_ = j % NSLOT
                        b = j % 3
                        hp_, hk = hbuf[j % 3]
                        k.op("act", lambda e: e.activation(out=gl[b][:, :T], in_=hp_[:, :T], func=AF.Gelu),
                             reads=[hk], writes=["gl%d" % b])
                        k.op("dve", lambda e: e.tensor_tensor(out=ab[b][:, :T], in0=gl[b][:, :T], in1=G[:, 0:T, j],
                                                              op=ALU.mult), reads=["gl%d" % b, "G"], writes=["ab%d" % b])
                        for ts in range(nsub):
                            for dh in range(2):
                                k.op("pe", lambda e, ts=ts, dh=dh: e.matmul(
                                    out_ps[ts][:, dh * 512:(dh + 1) * 512], lhsT=ab[b][:, ts * 128:(ts + 1) * 128],
                                    rhs=vb[s_][:, dh * 512:(dh + 1) * 512], start=(j == 0), stop=(j == NJ - 1)),
                                    reads=["ab%d" % b, "wslotV%d" % s_], writes=["out_ps"],
                                    sig=(ts == nsub - 1 and dh == 1))

                    mm1(0)
                    mm1(1)
                    for j in range(NJ):
                        if j + 2 < NJ:
                            mm1(j + 2)
                        mm2(j)
                        if j == 16:
                            for f in hook:
                                f()
                def epilogue():
                    deferred = []
                    for ts, tile in enumerate(tiles):
                        k.dma("sp", "xt", xtile[:, ts, :], xin_d[tile * 128:(tile + 1) * 128, :], writes=["xtile%d" % ts])
                        r = rtile[:, ts, :]
                        k.op("dve", lambda e: e.scalar_tensor_tensor(out=r, in0=xtile[:, ts, :], scalar=ALPHA,
                                                                     in1=out_ps[ts][:], op0=ALU.mult, op1=ALU.add),
                             reads=["xtile%d" % ts, "out_ps"], writes=["rtile%d" % ts])
                        o = otile[:, ts, :]
                        layer_norm_tm(r, "rtile%d" % ts, o, "otile%d" % ts)
                        if final:
                            if tile >= 1:
                                k.dma("sp", "ost", out_d[(tile - 1) * 128:tile * 128, :], o, reads=["otile%d" % ts])
                        else:
                            k.dma("sp", "xst", xout_d[tile * 128:(tile + 1) * 128, :], o, reads=["otile%d" % ts])
                            deferred.append(lambda o=o, ts=ts, tile=tile: to_feature_major(
                                o, "otile%d" % ts, tile, [g_ps[0], g_ps[0]], "g_ps", pkeys=["g_ps0", "g_ps0"], coarse=False))
                    return deferred

                return gbuild, jloop, epilogue, nsub

            groups = []
            tile0 = 1 if final else 0
            while tile0 < NTL:
                grp = make_group(tile0)
                groups.append(grp)
                tile0 += grp[3]
            groups[0][0]()
            pend = []
            for gi, grp in enumerate(groups):
                grp[1](pend)
                if gi + 1 < len(groups):
                    groups[gi + 1][0]()
                pend = grp[2]()
            for f in pend:
                f()
        k.barrier()

    def phase_m1(xin_d, xout_d):
        with contextlib.ExitStack() as ph:
            xtile = sb(ph, "xtile", [128, 1, D], F32)
            rtile = sb(ph, "rtile", [128, 1, D], F32)
            otile = sb(ph, "otile", [128, 1, D], F32)
            wq = sb(ph, "awq", [128, DC, D], BF16)
            wo = sb(ph, "awo", [128, DC, D], BF16)
            KT = sb(ph, "KT", [128, 4, NTOK], BF16)
            VS = sb(ph, "VS", [128, NTL, 256], BF16)
            qt = sb(ph, "qt", [128, D], F32)
            qTs = sb(ph, "qTs", [128, 8, 128], BF16)
            rt = [sb(ph, "rt%d" % i, [128, 16, 8], F32) for i in range(4)]
            posi = sb(ph, "posi", [128, NTL], I32)
            posf = sb(ph, "posf", [128, NTL], F32)
            invf = sb(ph, "invf", [128, 8], F32)
            ang = sb(ph, "ang", [128, NTL, 8], F32)
            sin_t = sb(ph, "sin_t", [128, NTL, 8], F32)
            cos_t = sb(ph, "cos_t", [128, NTL, 8], F32)
            amask = sb(ph, "amask", [128, 2, 256], F32)
            sink_b = sb(ph, "sink_b", [128, 16], F32)
            nsink_b = sb(ph, "nsink_b", [128, 16], F32)
            sm = [sb(ph, "sm%d" % i, [128, 256], F32) for i in range(2)]
            pb = [sb(ph, "pb%d" % i, [128, 256], BF16) for i in range(2)]
            pT = [sb(ph, "pT%d" % i, [128, 2, 128], BF16) for i in range(2)]
            sst = [sb(ph, "sst%d" % i, [128, 8], F32) for i in range(2)]
            osb = sb(ph, "osb", [128, D], F32)
            rden = sb(ph, "rden", [128, 16], F32)
            oT = sb(ph, "oT", [128, 8, 128], BF16)
            pj = [ps(ph, "pj%d" % i, [128, 512]) for i in range(2)]
            s_ps = [ps(ph, "s_ps%d" % i, [128, 512]) for i in range(2)]
            pT_ps = [ps(ph, "pT_ps%d" % i, [128, 1024], BF16) for i in range(2)]
            o_ps = ps(ph, "o_ps", [128, D])

            for (wt, wd, key, ncols) in ((wq, awq_d, "awq", D), (wo, awo_d, "awo", D)):
                wv = wd.rearrange("(c p) n -> p c n", p=128)
                for c in range(DC):
                    k.dma("pool", "wload", wt[:, c, :], wv[:, c, :], writes=[key])
            k.dma("sp", "cload", posi[:], pos_d[:, :], writes=["posi"])
            k.dma("sp", "cload", invf[:], invf_d.partition_broadcast(128), writes=["invf"])
            k.dma("sp", "cload", amask[:], amask_d.rearrange("a p k -> p a k"), writes=["amask"])
            k.dma("sp", "cload", sink_b[:], sinks_d.partition_broadcast(128), writes=["sink_b"])
            load_ln(lnmg_d, lnmb_d, 1)
            k.op("dve", lambda e: e.tensor_scalar(out=nsink_b[:], in0=sink_b[:], scalar1=-1.0, scalar2=None, op0=ALU.mult),
                 reads=["sink_b"], writes=["nsink_b"])
            k.op("dve", lambda e: e.tensor_copy(out=posf[:], in_=posi[:]), reads=["posi"], writes=["posf"])
            k.op("dve", lambda e: e.tensor_tensor(out=ang[:], in0=posf[:].unsqueeze(2).to_broadcast([128, NTL, 8]),
                                                  in1=invf[:].unsqueeze(1).to_broadcast([128, NTL, 8]), op=ALU.mult),
                 reads=["posf", "invf"], writes=["ang"])
            angs = sb(ph, "angs", [128, NTL, 8], F32)
            angk = sb(ph, "angk", [128, NTL, 8], F32)
            angi = sb(ph, "angi", [128, NTL, 8], I32)
            for (dst, dk, shift) in ((sin_t, "sin_t", 0.0), (cos_t, "cos_t", 0.5 * PI)):
                k.op("dve", lambda e: e.tensor_scalar(out=angs[:], in0=ang[:], scalar1=shift, scalar2=None, op0=ALU.add),
                     reads=["ang"], writes=["angs"])
                k.op("dve", lambda e: e.tensor_scalar(out=angk[:], in0=angs[:], scalar1=1.0 / (2.0 * PI), scalar2=None,
                                                      op0=ALU.mult), reads=["angs"], writes=["angk"])
                k.op("dve", lambda e: e.tensor_copy(out=angi[:], in_=angk[:]), reads=["angk"], writes=["angi"])
                k.op("dve", lambda e: e.tensor_copy(out=angk[:], in_=angi[:]), reads=["angi"], writes=["angk"])
                k.op("dve", lambda e: e.scalar_tensor_tensor(out=dst[:], in0=angk[:], scalar=-2.0 * PI, in1=angs[:],
                                                             op0=ALU.mult, op1=ALU.add),
                     reads=["angk", "angs"], writes=[dk])
                k.op("dve", lambda e: e.tensor_scalar(out=angk[:], in0=dst[:], scalar1=PI, scalar2=2.0 * PI,
                                                      op0=ALU.is_gt, op1=ALU.mult), reads=[dk], writes=["angk"])
                k.op("dve", lambda e: e.tensor_tensor(out=dst[:], in0=dst[:], in1=angk[:], op=ALU.subtract),
                     reads=[dk, "angk"], writes=[dk])
                k.op("dve", lambda e: e.tensor_scalar(out=dst[:], in0=dst[:], scalar1=-3.1415925, scalar2=3.1415925,
                                                      op0=ALU.max, op1=ALU.min), reads=[dk], writes=[dk])
                k.op("act", lambda e: e.activation(out=dst[:], in_=dst[:], func=AF.Sin), reads=[dk], writes=[dk])

            def rope(v3, vkey, nh, tile):
                c_b = cos_t[:, tile, :].unsqueeze(1).to_broadcast([128, nh, 8])
                s_b = sin_t[:, tile, :].unsqueeze(1).to_broadcast([128, nh, 8])
                v1 = v3[:, :, 0:8]
                v2 = v3[:, :, 8:16]
                tt = [rt[i][:, 0:nh, :] for i in range(4)]
                k.op("dve", lambda e: e.tensor_tensor(out=tt[0], in0=v1, in1=c_b, op=ALU.mult), reads=[vkey, "cos_t"], writes=["rt0"])
                k.op("dve", lambda e: e.tensor_tensor(out=tt[1], in0=v2, in1=s_b, op=ALU.mult), reads=[vkey, "sin_t"], writes=["rt1"])
                k.op("dve", lambda e: e.tensor_tensor(out=tt[2], in0=v2, in1=c_b, op=ALU.mult), reads=[vkey, "cos_t"], writes=["rt2"])
                k.op("dve", lambda e: e.tensor_tensor(out=tt[3], in0=v1, in1=s_b, op=ALU.mult), reads=[vkey, "sin_t"], writes=["rt3"])
                k.op("dve", lambda e: e.tensor_tensor(out=v1, in0=tt[0], in1=tt[1], op=ALU.subtract), reads=["rt0", "rt1"], writes=[vkey])
                k.op("dve", lambda e: e.tensor_tensor(out=v2, in0=tt[2], in1=tt[3], op=ALU.add), reads=["rt2", "rt3"], writes=[vkey])

            kvst = contextlib.ExitStack()
            kvw = sb(kvst, "kvw", [128, DC, 512], BF16)
            kvt = sb(kvst, "kvt", [128, 512], F32)
            kd = sb(kvst, "kd", [128, 4, 2, 64], F32)
            wv_ = kvw_d.rearrange("(c p) n -> p c n", p=128)
            for c in range(DC):
                k.dma("pool", "wload", kvw[:, c, :], wv_[:, c, :], writes=["kvw"])
            for tile in range(NTL):
                col0 = 32 + tile * 128
                for dc in range(DC):
                    k.op("pe", lambda e, dc=dc: e.matmul(pj[0][:, :], lhsT=xT_res[:, dc, col0:col0 + 128], rhs=kvw[:, dc, :],
                                                         start=(dc == 0), stop=(dc == DC - 1)),
                         reads=["kvw", "xTall", xkey(tile)], writes=["pj0"], sig=(dc == DC - 1))
                k.op("act", lambda e: e.copy(out=kvt[:], in_=pj[0][:]), reads=["pj0"], writes=["kvt"])
                k.op("dve", lambda e: e.tensor_copy(out=VS[:, tile, :], in_=kvt[:, 256:512]), reads=["kvt"], writes=["VS"])
                rope(kvt[:, 0:256].rearrange("p (h d) -> p h d", h=4), "kvt", 4, tile)
                kv3 = kvt[:, 0:256].rearrange("p (h d) -> p h d", h=4)
                for dup in range(2):
                    k.op("dve", lambda e: e.tensor_copy(out=kd[:, :, dup, :], in_=kv3), reads=["kvt"], writes=["kd"])
                for g in range(4):
                    k.op("pe", lambda e: e.transpose(out=pj[1][:, g * 128:(g + 1) * 128],
                                                     in_=kd[:, g, :, :].rearrange("p a b -> p (a b)"), identity=ident[:]),
                         reads=["kd", "ident"], writes=["pj1"], sig=(g == 3))
                k.op("act", lambda e: e.copy(out=KT[:, :, tile * 128:(tile + 1) * 128],
                                             in_=pj[1][:].rearrange("p (a b) -> p a b", a=4)),
                     reads=["pj1"], writes=["KT"])

            cnt = 0
            k.barrier()
            kvst.close()
            tsteps = table_steps(1, ph) if want_tables1 else iter(())
            per_tile = (NJ + NTL - 2) // (NTL - 1)
            qTs2 = [qTs, sb(ph, "qTs1", [128, 8, 128], BF16)]

            def preA(tile):
                col0 = 32 + tile * 128
                for half in range(2):
                    for dc in range(DC):
                        k.op("pe", lambda e, dc=dc: e.matmul(pj[half][:, :], lhsT=xT_res[:, dc, col0:col0 + 128],
                                                             rhs=wq[:, dc, half * 512:(half + 1) * 512],
                                                             start=(dc == 0), stop=(dc == DC - 1)),
                             reads=["awq", "xTall", xkey(tile)], writes=["pj%d" % half], sig=(dc == DC - 1))
                    k.op("act", lambda e: e.copy(out=qt[:, half * 512:(half + 1) * 512], in_=pj[half][:]),
                         reads=["pj%d" % half], writes=["qt"])
                rope(qt[:].rearrange("p (h d) -> p h d", h=16), "qt", 16, tile)

            def preB(tile):
                qd = qTs2[tile % 2]
                for half in range(2):
                    for q in range(4):
                        c = half * 4 + q
                        k.op("pe", lambda e, c=c, q=q: e.transpose(out=pj[half][:, q * 128:(q + 1) * 128],
                                                                   in_=qt[:, c * 128:(c + 1) * 128], identity=ident[:]),
                             reads=["qt", "ident"], writes=["pj%d" % half], sig=(q == 3))
                    k.op("act", lambda e: e.copy(out=qd[:, half * 4:(half + 1) * 4, :],
                                                 in_=pj[half][:].rearrange("p (a b) -> p a b", a=4)),
                         reads=["pj%d" % half], writes=["qTs%d" % (tile % 2)])

            def heads(tile):
                qd = qTs2[tile % 2]
                qk = "qTs%d" % (tile % 2)
                mi = 0 if tile == 1 else 1

                def front(h):
                    b = h % 2
                    g = h // 4
                    hb = 64 * (h % 2)
                    k.op("pe", lambda e: e.matmul(s_ps[b][:, 0:256], lhsT=qd[hb:hb + 64, h // 2, :],
                                                  rhs=KT[hb:hb + 64, g, (tile - 1) * 128:(tile + 1) * 128],
                                                  start=True, stop=True),
                         reads=[qk, "KT"], writes=["s_ps%d" % b])
                    k.op("dve", lambda e: e.tensor_tensor(out=sm[b][:], in0=s_ps[b][:, 0:256], in1=amask[:, mi, :], op=ALU.add),
                         reads=["s_ps%d" % b, "amask"], writes=["sm%d" % b])
                    k.op("dve", lambda e: e.tensor_reduce(out=sst[b][:, 0:1], in_=sm[b][:], axis=AX.X, op=ALU.max),
                         reads=["sm%d" % b], writes=["sstA%d" % b])
                    k.op("dve", lambda e: e.tensor_scalar(out=sst[b][:, 1:2], in0=sst[b][:, 0:1], scalar1=-0.125,
                                                          scalar2=nsink_b[:, h:h + 1], op0=ALU.mult, op1=ALU.min),
                         reads=["sstA%d" % b, "nsink_b"], writes=["sstB%d" % b])
                    k.op("act", lambda e: e.activation(out=pb[b][:], in_=sm[b][:], func=AF.Exp, bias=sst[b][:, 1:2],
                                                       scale=0.125, accum_out=sst[b][:, 2:3]),
                         reads=["sm%d" % b, "sstB%d" % b], writes=["pb%d" % b, "sstC%d" % b])
                    k.op("act", lambda e: e.activation(out=sst[b][:, 3:4], in_=sink_b[:, h:h + 1], func=AF.Exp,
                                                       bias=sst[b][:, 1:2], scale=1.0),
                         reads=["sink_b", "sstB%d" % b], writes=["sstD%d" % b])

                def denom(h):
                    b = h % 2
                    k.op("dve", lambda e: e.tensor_tensor(out=sst[b][:, 4:5], in0=sst[b][:, 2:3], in1=sst[b][:, 3:4], op=ALU.add),
                         reads=["sstC%d" % b, "sstD%d" % b], writes=["sstE%d" % b])
                    k.op("dve", lambda e: e.reciprocal(out=rden[:, h:h + 1], in_=sst[b][:, 4:5]),
                         reads=["sstE%d" % b], writes=["rden"])

                def back(h):
                    b = h % 2
                    g = h // 4
                    for kb_ in range(2):
                        k.op("pe", lambda e, kb_=kb_: e.transpose(out=pT_ps[b][:, kb_ * 128:(kb_ + 1) * 128],
                                                                  in_=pb[b][:, kb_ * 128:(kb_ + 1) * 128], identity=identb[:]),
                             reads=["pb%d" % b, "identb"], writes=["pT_ps%d" % b], sig=(kb_ == 1))
                    k.op("dve", lambda e: e.tensor_copy(out=pT[b][:].rearrange("p a b -> p (a b)"), in_=pT_ps[b][:, 0:256]),
                         reads=["pT_ps%d" % b], writes=["pT%d" % b])
                    for kb_ in range(2):
                        k.op("pe", lambda e, kb_=kb_: e.matmul(o_ps[:, h * 64:(h + 1) * 64], lhsT=pT[b][:, kb_, :],
                                                               rhs=VS[:, tile - 1 + kb_, g * 64:(g + 1) * 64],
                                                               start=(kb_ == 0), stop=(kb_ == 1)),
                             reads=["pT%d" % b, "VS"], writes=["o_ps%d" % (h // 8)], sig=(kb_ == 1))

                def evac(h):
                    if h % 8 == 7:
                        hb8 = h // 8
                        k.op("dve", lambda e: e.tensor_tensor(
                            out=osb[:, hb8 * 512:(hb8 + 1) * 512].rearrange("p (h d) -> p h d", h=8),
                            in0=o_ps[:, hb8 * 512:(hb8 + 1) * 512].rearrange("p (h d) -> p h d", h=8),
                            in1=rden[:, hb8 * 8:(hb8 + 1) * 8].unsqueeze(2).to_broadcast([128, 8, 64]), op=ALU.mult),
                            reads=["o_ps%d" % hb8, "rden"], writes=["osb"])

                front(0)
                front(1)
                denom(0)
                for h in range(16):
                    back(h)
                    if h + 2 < 16:
                        front(h + 2)
                    if h + 1 < 16:
                        denom(h + 1)
                    evac(h)
                    yield h

            def post(tile):
                for half in range(2):
                    for q in range(4):
                        c = half * 4 + q
                        k.op("pe", lambda e, c=c, q=q: e.transpose(out=pj[half][:, q * 128:(q + 1) * 128],
                                                                   in_=osb[:, c * 128:(c + 1) * 128], identity=ident[:]),
                             reads=["osb", "ident"], writes=["pj%d" % half], sig=(q == 3))
                    k.op("act", lambda e: e.copy(out=oT[:, half * 4:(half + 1) * 4, :],
                                                 in_=pj[half][:].rearrange("p (a b) -> p a b", a=4)),
                         reads=["pj%d" % half], writes=["oT"])
                    yield
                for half in range(2):
                    for c in range(8):
                        k.op("pe", lambda e, c=c: e.matmul(pj[half][:, :], lhsT=oT[:, c, :],
                                                           rhs=wo[:, c, half * 512:(half + 1) * 512],
                                                           start=(c == 0), stop=(c == 7)),
                             reads=["oT", "awo"], writes=["pj%d" % half], sig=(c == 7))
                    yield
                k.dma("sp", "xt", xtile[:, 0, :], xin_d[tile * 128:(tile + 1) * 128, :], writes=["xtile"])
                r = rtile[:, 0, :]
                for half in range(2):
                    k.op("dve", lambda e: e.scalar_tensor_tensor(
                        out=r[:, half * 512:(half + 1) * 512], in0=xtile[:, 0, half * 512:(half + 1) * 512], scalar=ALPHA,
                        in1=pj[half][:], op0=ALU.mult, op1=ALU.add),
                        reads=["xtile", "pj%d" % half], writes=["rtile"])
                yield
                o = otile[:, 0, :]
                layer_norm_tm(r, "rtile", o, "otile")
                yield
                k.dma("sp", "xst", xout_d[tile * 128:(tile + 1) * 128, :], o, reads=["otile"])
                to_feature_major(o, "otile", tile, [pj[0], pj[1]], "pj")
                yield

            preA(1)
            preB(1)
            pending = iter(())
            for tile in range(1, NTL):
                if tile + 1 < NTL:
                    preA(tile + 1)
                for h in heads(tile):
                    if h % 2 == 1:
                        for _ in range((per_tile + 7) // 8):
                            next(tsteps, None)
                    next(pending, None)
                    if h == 8 and tile + 1 < NTL:
                        preB(tile + 1)
                for _ in pending:
                    pass
                pending = post(tile)
            for _ in pending:
                pass
            for _ in tsteps:
                pass
        k.barrier()

    allp = ("m0", "s0", "d0", "m1", "s1", "d1")
    phases = allp if phases is None else phases
    want_tables0 = "d0" in phases and "s0" in phases
    want_tables1 = "d1" in phases and "m1" in phases
    if ("d0" in phases and not want_tables0) or ("d1" in phases and not want_tables1):
        phase_w()
    if "m0" in phases:
        phase_m0()
    if "s0" in phases:
        phase_s(0)
    if "d0" in phases:
        phase_d(0, xs_d[0], xs_d[1], False)
    if "m1" in phases:
        phase_m1(xs_d[1], xs_d[2])
    if "s1" in phases:
        phase_s(1)
    if "d1" in phases:
        phase_d(1, xs_d[2], None, True)
    k.finish("sp")
    return nc, k


def _core_inputs(x, positions, S, S_core, core):
    per_seq = S // S_core
    b = core // per_seq
    start = (core % per_seq) * S_core
    NTOK = S_core + 128
    XW = NTOK + 32
    lo = start - 160
    xrows = np.zeros((XW, D), np.float32)
    hm = np.zeros((XW,), np.float32)
    pos = np.zeros((NTOK,), np.int32)
    src_lo = max(lo, 0)
    xrows[src_lo - lo:] = x[b, src_lo:start + S_core]
    hm[src_lo - lo:] = 1.0
    plo = max(start - 128, 0)
    pos[plo - (start - 128):] = positions[b, plo:start + S_core]
    tq = np.arange(128)[:, None]
    tk = np.arange(256)[None, :]
    dd = tq + 128 - tk
    band = (dd >= 0) & (dd < 128)
    first = band & (tk >= 128) if start == 0 else band
    amask = np.stack([np.where(first, 0.0, -1e30), np.where(band, 0.0, -1e30)]).astype(np.float32)
    return {
        "xT": np.ascontiguousarray(xrows.T),
        "xtm": np.ascontiguousarray(xrows[32:]),
        "hm": hm,
        "pos_pn": np.ascontiguousarray(pos.reshape(-1, 128).T),
        "amask": amask,
    }


def _pc(v):
    return np.ascontiguousarray(v.reshape(-1, 128).T)


def _shared_inputs(conv_w_in, conv_b_in, conv_dw, conv_dw_b, conv_ln_g, conv_ln_b, conv_w_out, conv_b_out,
                   kv_w, attn_w_q, attn_sinks, attn_w_o, peer_w_q, peer_sub_keys, peer_u, peer_v,
                   ln_mix_g, ln_mix_b, ln_ffn_g, ln_ffn_b):
    half = 8
    invf = (500000.0 ** (-(np.arange(half, dtype=np.float32) * 2.0 / 16.0))).astype(np.float32)
    UT = np.ascontiguousarray(peer_u.reshape(2, 128, 128, 8, 128).transpose(0, 2, 4, 3, 1))
    keysT = np.ascontiguousarray(peer_sub_keys.transpose(0, 3, 1, 2))
    return {
        "invf": invf,
        "w_in": np.ascontiguousarray(conv_w_in[0]),
        "b_in_pc": _pc(conv_b_in[0]),
        "dw_pc": np.ascontiguousarray(conv_dw[0].T.reshape(8, 128, 31).transpose(1, 0, 2)),
        "dwb_pc": _pc(conv_dw_b[0]),
        "clng_pc": _pc(conv_ln_g[0]),
        "clnb_pc": _pc(conv_ln_b[0]),
        "w_out": np.ascontiguousarray(conv_w_out[0]),
        "b_out": np.ascontiguousarray(conv_b_out[0]),
        "kv_w": np.ascontiguousarray(kv_w),
        "attn_w_q": np.ascontiguousarray(attn_w_q[0]),
        "sinks": np.ascontiguousarray(attn_sinks[0]),
        "attn_w_o": np.ascontiguousarray(attn_w_o[0]),
        "peer_w_q": np.ascontiguousarray(peer_w_q),
        "keysT": keysT,
        "UT": UT,
        "peer_v": np.ascontiguousarray(peer_v),
        "ln_mix_g": np.ascontiguousarray(ln_mix_g),
        "ln_mix_b": np.ascontiguousarray(ln_mix_b),
        "ln_ffn_g": np.ascontiguousarray(ln_ffn_g),
        "ln_ffn_b": np.ascontiguousarray(ln_ffn_b),
    }


def run(x, positions, weights, n_cores=8, dbg=False, phases=None, cores=None):
    x = np.asarray(x, np.float32)
    positions = np.asarray(positions, np.int32)
    B, S, _ = x.shape
    S_core = B * S // n_cores
    shared = _shared_inputs(**{kk: np.asarray(v) for kk, v in weights.items()})
    nc, kb = build(S_core, dbg=dbg, phases=phases)
    print('ninstr', kb.ninstr, {e: kb.cnt[e] for e in ('pe', 'act', 'dve', 'pool')}, flush=True)
    in_maps = []
    cores = list(range(n_cores)) if cores is None else cores
    for c in cores:
        m = dict(shared)
        m.update(_core_inputs(x, positions, S, S_core, c))
        in_maps.append(m)
    res = run_bass_kernel_spmd(nc, in_maps, core_ids=list(range(len(cores))))
    if len(cores) != n_cores:
        return None, res
    out = np.concatenate([r["out"] for r in res.results], axis=0).reshape(B, S, D)
    return out, res


def kernel(x, positions, **weights):
    out, _ = run(x, positions, weights)
    return out.astype(np.float32)
```

```python
import contextlib
import math
import numpy as np
import concourse.bass as bass
import concourse.mybir as mybir
from concourse.bass_utils import run_bass_kernel_spmd

F32 = mybir.dt.float32
BF16 = mybir.dt.bfloat16
I32 = mybir.dt.int32
U32 = mybir.dt.uint32
ALU = mybir.AluOpType
AF = mybir.ActivationFunctionType
AX = mybir.AxisListType

D = 1024
DC = 8
ALPHA = 4.0 ** 0.25
EPS = 1e-5
NJ = 128
NSLOT = 6
PI = math.pi


class KB:
    def __init__(self, nc):
        self.nc = nc
        self.es = contextlib.ExitStack()
        self.eng = {"pe": nc.tensor, "act": nc.scalar, "dve": nc.vector,
                    "pool": nc.gpsimd, "sp": nc.sync}
        self.sem = {}
        self.cnt = {}
        for e in ("pe", "act", "dve", "pool"):
            self.sem[e] = self.es.enter_context(nc.semaphore("s_" + e))
            self.cnt[e] = 0
        self.known = {e: {} for e in self.eng}
        self.res = {}
        self.ninstr = 0
        import os
        self.limit = int(os.environ.get("KB_LIMIT", "1000000000"))

    def dsem(self, name):
        if name not in self.sem:
            self.sem[name] = self.es.enter_context(self.nc.semaphore("d_" + name))
            self.cnt[name] = 0
        return name

    def _need(self, e, reads, writes):
        need = {}

        def add(sv):
            if sv is None:
                return
            s, v = sv
            if need.get(s, 0) < v:
                need[s] = v
        for k in reads:
            r = self.res.get(k)
            if r:
                add(r["w"])
        for k in writes:
            r = self.res.get(k)
            if r:
                add(r["w"])
                for s, v in r["r"].items():
                    add((s, v))
        kn = self.known[e]
        for s, v in need.items():
            if e == "pe" and s == "pe":
                continue
            if s not in ("pe", "act", "dve", "pool"):
                v = self.cnt[s]
            if kn.get(s, 0) < v:
                self.eng[e].wait_ge(self.sem[s], v)
                kn[s] = v

    def _mark(self, s, v, reads, writes):
        for k in writes:
            self.res[k] = {"w": (s, v), "r": {}}
        for k in reads:
            r = self.res.setdefault(k, {"w": None, "r": {}})
            if r["r"].get(s, 0) < v:
                r["r"][s] = v

    def op(self, e, fn, reads=(), writes=(), sig=True):
        if self.ninstr >= self.limit:
            self.ninstr += 1
            return None
        self._need(e, reads, writes)
        ins = fn(self.eng[e])
        self.ninstr += 1
        if sig:
            self.cnt[e] += 1
            ins.then_inc(self.sem[e], 1)
            v = self.cnt[e]
        else:
            v = self.cnt[e] + 1
        self._mark(e, v, reads, writes)
        return ins

    def dma(self, q, semname, out, in_, reads=(), writes=(), **kw):
        if self.ninstr >= self.limit:
            self.ninstr += 1
            return None
        self.dsem(semname)
        self._need(q, reads, writes)
        ins = self.eng[q].dma_start(out=out, in_=in_, **kw)
        self.cnt[semname] += 16
        ins.then_inc(self.sem[semname], 16)
        self._mark(semname, self.cnt[semname], reads, writes)
        self.ninstr += 1
        return ins

    def barrier(self):
        for e in ("pe", "act", "dve", "pool", "sp"):
            kn = self.known[e]
            for s, v in self.cnt.items():
                if v > 0 and kn.get(s, 0) < v:
                    self.eng[e].wait_ge(self.sem[s], v)
                    kn[s] = v
        self.res = {}

    def finish(self, e="sp"):
        kn = self.known[e]
        for s, v in self.cnt.items():
            if v > 0 and kn.get(s, 0) < v:
                self.eng[e].wait_ge(self.sem[s], v)
                kn[s] = v


def build(S_core, dbg=False, phases=None):
    NTL = S_core // 128 + 1
    NTOK = NTL * 128
    XW = NTOK + 32
    nc = bass.Bass("TRN2", target_bir_lowering=False)

    def din(name, shape, dt=F32):
        return nc.dram_tensor(name, list(shape), dt, kind="ExternalInput").ap()

    xT_d = din("xT", [D, XW])
    xtm_d = din("xtm", [NTOK, D])
    hm_d = din("hm", [XW])
    pos_d = din("pos_pn", [128, NTL], I32)
    amask_d = din("amask", [2, 128, 256])
    invf_d = din("invf", [8])
    w_in_d = din("w_in", [D, 2 * D])
    b_in_d = din("b_in_pc", [128, 16])
    dw_d = din("dw_pc", [128, 8, 31])
    dwb_d = din("dwb_pc", [128, 8])
    clng_d = din("clng_pc", [128, 8])
    clnb_d = din("clnb_pc", [128, 8])
    w_out_d = din("w_out", [D, D])
    b_out_d = din("b_out", [D])
    kvw_d = din("kv_w", [D, 512])
    awq_d = din("attn_w_q", [D, D])
    sinks_d = din("sinks", [16])
    awo_d = din("attn_w_o", [D, D])
    pwq_d = din("peer_w_q", [2, D, 2 * D])
    keysT_d = din("keysT", [2, 128, 2, 128])
    UT_d = din("UT", [2, NJ, 128, 8, 128])
    V_d = din("peer_v", [2, 128 * NJ, D])
    lnmg_d = din("ln_mix_g", [2, D])
    lnmb_d = din("ln_mix_b", [2, D])
    lnfg_d = din("ln_ffn_g", [2, D])
    lnfb_d = din("ln_ffn_b", [2, D])
    out_d = nc.dram_tensor("out", [S_core, D], F32, kind="ExternalOutput").ap()
    skind = "ExternalOutput" if dbg else "Internal"
    xs_d = [nc.dram_tensor("xs%d" % i, [NTOK, D], F32, kind=skind).ap() for i in range(3)]

    UTb_d = nc.dram_tensor("UTb", [2, NJ, 128, D], BF16).ap()
    Vb_d = nc.dram_tensor("Vb", [2, NJ, 128, D], BF16).ap()

    k = KB(nc)
    top = k.es

    uid = [0]

    def sb(st, name, shape, dt):
        uid[0] += 1
        return st.enter_context(nc.sbuf_tensor("sb%d_%s" % (uid[0], name), list(shape), dt))

    def ps(st, name, shape, dt=F32):
        uid[0] += 1
        return st.enter_context(nc.psum_tensor("ps%d_%s" % (uid[0], name), list(shape), dt))

    xT_res = sb(top, "xT_res", [128, DC, XW], BF16)
    IDXI = sb(top, "IDXI", [128, NTOK], F32)
    IDXJ = sb(top, "IDXJ", [128, NTOK], F32)
    GATE = sb(top, "GATE", [128, NTOK], F32)
    ident = sb(top, "ident", [128, 128], F32)
    identb = sb(top, "identb", [128, 128], BF16)
    iota_f = sb(top, "iota_f", [128, 128], F32)
    iota_b = sb(top, "iota_b", [128, 128], BF16)
    iota_p = sb(top, "iota_p", [128, 1], F32)
    ones_f = sb(top, "ones_f", [128, 128], F32)
    g_b = sb(top, "g_b", [128, D], F32)
    b_b = sb(top, "b_b", [128, D], F32)
    junk = sb(top, "junk", [128, D], BF16)
    st = sb(top, "st", [128, 16], F32)

    k.op("pool", lambda e: e.iota(iota_f[:], pattern=[[1, 128]], base=0, channel_multiplier=0,
                                  allow_small_or_imprecise_dtypes=True), writes=["iota_f"])
    k.op("pool", lambda e: e.iota(iota_p[:], pattern=[[0, 1]], base=0, channel_multiplier=1,
                                  allow_small_or_imprecise_dtypes=True), writes=["iota_p"])
    k.op("dve", lambda e: e.tensor_copy(out=iota_b[:], in_=iota_f[:]), reads=["iota_f"], writes=["iota_b"])
    k.op("dve", lambda e: e.tensor_scalar(out=ident[:], in0=iota_f[:], scalar1=iota_p[:, 0:1], scalar2=None,
                                          op0=ALU.is_equal), reads=["iota_f", "iota_p"], writes=["ident"])
    k.op("dve", lambda e: e.tensor_copy(out=identb[:], in_=ident[:]), reads=["ident"], writes=["identb"])
    k.op("dve", lambda e: e.memset(ones_f[:], 1.0), writes=["ones_f"])

    xT_v = xT_d.rearrange("(c p) w -> p c w", p=128)
    for c in range(DC):
        k.dma("pool", "xTload", xT_res[:, c, :], xT_v[:, c, :], writes=["xTall"])
    xkeys_all = ["xTall"]

    def xkey(tile):
        return "xT%d" % tile

    def load_ln(g_d, b_d, layer):
        k.dma("sp", "lnp", g_b[:], g_d[layer].partition_broadcast(128), writes=["g_b"])
        k.dma("sp", "lnp", b_b[:], b_d[layer].partition_broadcast(128), writes=["b_b"])

    def layer_norm_tm(r, rkey, o, okey):
        k.op("act", lambda e: e.activation(out=junk[:], in_=r, func=AF.Identity, accum_out=st[:, 0:1]),
             reads=[rkey], writes=["junk", "st0"])
        k.op("act", lambda e: e.activation(out=junk[:], in_=r, func=AF.Square, accum_out=st[:, 1:2]),
             reads=[rkey], writes=["junk", "st1"])
        k.op("dve", lambda e: e.tensor_scalar(out=st[:, 2:3], in0=st[:, 0:1], scalar1=1.0 / D, scalar2=None,
                                              op0=ALU.mult), reads=["st0"], writes=["st2"])
        k.op("dve", lambda e: e.tensor_tensor(out=st[:, 3:4], in0=st[:, 2:3], in1=st[:, 2:3], op=ALU.mult),
             reads=["st2"], writes=["st3"])
        k.op("dve", lambda e: e.scalar_tensor_tensor(out=st[:, 4:5], in0=st[:, 1:2], scalar=1.0 / D,
                                                     in1=st[:, 3:4], op0=ALU.mult, op1=ALU.subtract),
             reads=["st1", "st3"], writes=["st4"])
        k.op("dve", lambda e: e.tensor_scalar(out=st[:, 6:7], in0=st[:, 4:5], scalar1=EPS, scalar2=None,
                                              op0=ALU.add), reads=["st4"], writes=["st6"])
        k.op("act", lambda e: e.activation(out=st[:, 7:8], in_=st[:, 6:7], func=AF.Sqrt), reads=["st6"], writes=["st7"])
        k.op("dve", lambda e: e.reciprocal(out=st[:, 5:6], in_=st[:, 7:8]), reads=["st7"], writes=["st5"])
        k.op("dve", lambda e: e.tensor_scalar(out=o, in0=r, scalar1=st[:, 2:3], scalar2=st[:, 5:6],
                                              op0=ALU.subtract, op1=ALU.mult),
             reads=[rkey, "st2", "st5"], writes=[okey])
        k.op("dve", lambda e: e.tensor_tensor(out=o, in0=o, in1=g_b[:], op=ALU.mult),
             reads=[okey, "g_b"], writes=[okey])
        k.op("dve", lambda e: e.tensor_tensor(out=o, in0=o, in1=b_b[:], op=ALU.add),
             reads=[okey, "b_b"], writes=[okey])

    def to_feature_major(src, skey, tile, tp_ps, tpkey):
        col0 = 32 + tile * 128
        for half in range(2):
            pk = tpkey + str(half)
            for q in range(4):
                c = half * 4 + q
                k.op("pe", lambda e, c=c, q=q: e.transpose(out=tp_ps[half][:, q * 128:(q + 1) * 128],
                                                           in_=src[:, c * 128:(c + 1) * 128], identity=ident[:]),
                     reads=[skey, "ident"], writes=[pk], sig=(q == 3))
            eng = "act" if half == 0 else "dve"
            if eng == "act":
                k.op("act", lambda e: e.copy(out=xT_res[:, half * 4:(half + 1) * 4, col0:col0 + 128],
                                             in_=tp_ps[half][:].rearrange("p (a b) -> p a b", a=4)),
                     reads=[pk], writes=[xkey(tile)] + xkeys_all)
            else:
                k.op("dve", lambda e: e.tensor_copy(out=xT_res[:, half * 4:(half + 1) * 4, col0:col0 + 128],
                                                    in_=tp_ps[half][:].rearrange("p (a b) -> p a b", a=4)),
                     reads=[pk], writes=[xkey(tile)] + xkeys_all)

    def phase_w():
        with contextlib.ExitStack() as ph:
            NB = 3
            su = [sb(ph, "su%d" % i, [128, 2, D], F32) for i in range(NB)]
            sv_ = [sb(ph, "sv%d" % i, [128, 2, D], F32) for i in range(NB)]
            bu = [sb(ph, "bu%d" % i, [128, 2, D], BF16) for i in range(NB)]
            bv = [sb(ph, "bv%d" % i, [128, 2, D], BF16) for i in range(NB)]
            cnt = 0
            for layer in range(2):
                V_v = V_d[layer].rearrange("(i j) d -> j i d", j=NJ)
                for step in range(NJ // 2):
                    b = cnt % NB
                    cnt += 1
                    j0 = 2 * step
                    k.dma("sp", "pw_in%d" % b, su[b][:], UT_d[layer, j0:j0 + 2].rearrange("j p c i -> p j (c i)"),
                          writes=["su%d" % b])
                    k.dma("sp", "pw_in%d" % b, sv_[b][:], V_v[j0:j0 + 2].rearrange("j i d -> i j d"),
                          writes=["sv%d" % b])
                    ue = "dve" if step % 2 == 0 else "pool"
                    k.op(ue, lambda e: e.tensor_copy(out=bu[b][:], in_=su[b][:]), reads=["su%d" % b], writes=["bu%d" % b])
                    k.op("act", lambda e: e.copy(out=bv[b][:], in_=sv_[b][:]), reads=["sv%d" % b], writes=["bv%d" % b])
                    k.dma("sp", "pw_out%d" % b, UTb_d[layer, j0:j0 + 2].rearrange("j p x -> p j x"), bu[b][:],
                          reads=["bu%d" % b])
                    k.dma("sp", "pw_out%d" % b, Vb_d[layer, j0:j0 + 2].rearrange("j p x -> p j x"), bv[b][:],
                          reads=["bv%d" % b])
        k.barrier()

    def table_steps(layer, ph):
        NB = 3
        su = [sb(ph, "tsu%d" % i, [128, D], F32) for i in range(NB)]
        sv_ = [sb(ph, "tsv%d" % i, [128, D], F32) for i in range(NB)]
        bu = [sb(ph, "tbu%d" % i, [128, D], BF16) for i in range(NB)]
        bv = [sb(ph, "tbv%d" % i, [128, D], BF16) for i in range(NB)]
        V_v = V_d[layer].rearrange("(i j) d -> j i d", j=NJ)

        def load(j):
            b = j % NB
            k.dma("sp", "tw_in%d" % b, su[b][:], UT_d[layer, j].rearrange("p c i -> p (c i)"), writes=["tsu%d" % b])
            k.dma("sp", "tw_in%d" % b, sv_[b][:], V_v[j], writes=["tsv%d" % b])

        load(0)
        load(1)
        for j in range(NJ):
            b = j % NB
            if j + 2 < NJ:
                load(j + 2)
            k.op("act", lambda e: e.copy(out=bu[b][:], in_=su[b][:]), reads=["tsu%d" % b], writes=["tbu%d" % b])
            k.op("act", lambda e: e.copy(out=bv[b][:], in_=sv_[b][:]), reads=["tsv%d" % b], writes=["tbv%d" % b])
            k.dma("pool", "tw_out%d" % b, UTb_d[layer, j], bu[b][:], reads=["tbu%d" % b])
            k.dma("pool", "tw_out%d" % b, Vb_d[layer, j], bv[b][:], reads=["tbv%d" % b])
            yield j

    def phase_m0():
        with contextlib.ExitStack() as ph:
            xtile = sb(ph, "xtile", [128, 1, D], F32)
            rtile = sb(ph, "rtile", [128, 1, D], F32)
            otile = sb(ph, "otile", [128, 1, D], F32)
            wi = sb(ph, "wi", [128, DC, 2 * D], BF16)
            wo = sb(ph, "wo", [128, DC, D], BF16)
            hch = sb(ph, "hch", [128, 8, 544], BF16)
            dg = sb(ph, "dg", [128, 2, 31, 128], BF16)
            acc = sb(ph, "acc", [128, 8, 512], F32)
            hs = sb(ph, "hs", [128, 8, 512], BF16)
            sq = [sb(ph, "sq%d" % i, [128, 512], F32) for i in range(2)]
            sig = [sb(ph, "sig%d" % i, [128, 512], F32) for i in range(2)]
            gtmp = [sb(ph, "gtmp%d" % i, [128, 512], F32) for i in range(2)]
            mean_t = sb(ph, "mean_t", [128, 512], F32)
            msq_t = sb(ph, "msq_t", [128, 512], F32)
            rstd_t = sb(ph, "rstd_t", [128, 512], F32)
            hm_b = sb(ph, "hm_b", [128, XW], BF16)
            bout_b = sb(ph, "bout_b", [128, D], F32)
            b_in = sb(ph, "b_in", [128, 16], F32)
            dw = sb(ph, "dw", [128, 8, 31], F32)
            dwb = sb(ph, "dwb", [128, 8], F32)
            clng = sb(ph, "clng", [128, 8], F32)
            clnb = sb(ph, "clnb", [128, 8], F32)
            ag = [ps(ph, "ag%d" % i, [128, 512]) for i in range(4)]
            s1 = ps(ph, "s1", [128, 512])
            s2 = ps(ph, "s2", [128, 512])
            mix = ps(ph, "mix", [128, D])

            wi_v = w_in_d.rearrange("(c p) n -> p c n", p=128)
            wo_v = w_out_d.rearrange("(c p) n -> p c n", p=128)
            for c in range(DC):
                k.dma("pool", "wload", wi[:, c, :], wi_v[:, c, :], writes=["wi"])
            for c in range(DC):
                k.dma("pool", "wload", wo[:, c, :], wo_v[:, c, :], writes=["wo"])
            k.dma("pool", "wload", hm_b[:], hm_d.partition_broadcast(128), writes=["hm_b"])
            k.dma("sp", "cload", bout_b[:], b_out_d.partition_broadcast(128), writes=["bout_b"])
            k.dma("sp", "cload", b_in[:], b_in_d[:, :], writes=["cst"])
            k.dma("sp", "cload", dw[:], dw_d[:, :, :], writes=["cst"])
            k.dma("sp", "cload", dwb[:], dwb_d[:, :], writes=["cst"])
            k.dma("sp", "cload", clng[:], clng_d[:, :], writes=["cst"])
            k.dma("sp", "cload", clnb[:], clnb_d[:, :], writes=["cst"])
            load_ln(lnmg_d, lnmb_d, 0)

            def glu(col0, n, dcol0, rkeys):
                for fc in range(8):
                    b = fc % 2
                    a_ps, g_ps = ag[2 * b], ag[2 * b + 1]
                    for (pt, colsel, pk) in ((a_ps, fc, "ag%d" % (2 * b)), (g_ps, fc + 8, "ag%d" % (2 * b + 1))):
                        for dc in range(DC):
                            k.op("pe", lambda e, pt=pt, colsel=colsel, dc=dc: e.matmul(
                                pt[:, :n], lhsT=wi[:, dc, colsel * 128:(colsel + 1) * 128],
                                rhs=xT_res[:, dc, col0:col0 + n], start=(dc == 0), stop=(dc == DC - 1)),
                                reads=["wi"] + rkeys, writes=[pk], sig=(dc == DC - 1))
                    k.op("act", lambda e: e.activation(out=sig[b][:, :n], in_=g_ps[:, :n], func=AF.Sigmoid,
                                                       bias=b_in[:, fc + 8:fc + 9], scale=1.0),
                         reads=["ag%d" % (2 * b + 1), "cst"], writes=["sig%d" % b])
                    k.op("dve", lambda e: e.scalar_tensor_tensor(out=gtmp[b][:, :n], in0=a_ps[:, :n],
                                                                 scalar=b_in[:, fc:fc + 1], in1=sig[b][:, :n],
                                                                 op0=ALU.add, op1=ALU.mult),
                         reads=["ag%d" % (2 * b), "sig%d" % b, "cst"], writes=["gtmp%d" % b])
                    k.op("dve", lambda e: e.tensor_tensor(out=hch[:, fc, dcol0:dcol0 + n], in0=gtmp[b][:, :n],
                                                          in1=hm_b[:, col0:col0 + n], op=ALU.mult),
                         reads=["gtmp%d" % b, "hm_b"], writes=["hch"])

            glu(0, 32, 0, ["xTall"])
            tok0 = 0
            while tok0 < NTOK:
                n = min(512, NTOK - tok0)
                tiles = list(range(tok0 // 128, (tok0 + n) // 128))
                glu(32 + tok0, n, 32, ["xTall"] + [xkey(t) for t in tiles])
                for fc in range(8):
                    b = fc % 2
                    ak = "acc%d" % fc
                    for kk in range(31):
                        if kk % 3 == 0:
                            k.op("act", lambda e, kk=kk: e.activation(out=dg[:, b, kk, :], in_=identb[:], func=AF.Copy,
                                                                      scale=dw[:, fc, kk:kk + 1]),
                                 reads=["identb", "cst"], writes=["dgA%d" % b])
                        else:
                            k.op("dve", lambda e, kk=kk: e.tensor_scalar(out=dg[:, b, kk, :], in0=identb[:],
                                                                         scalar1=dw[:, fc, kk:kk + 1], scalar2=None,
                                                                         op0=ALU.mult),
                                 reads=["identb", "cst"], writes=["dgD%d" % b])
                    for kk in range(31):
                        k.op("pe", lambda e, kk=kk: e.matmul(ag[2 + b][:, :n], lhsT=dg[:, b, kk, :],
                                                             rhs=hch[:, fc, 2 + kk:2 + kk + n],
                                                             start=(kk == 0), stop=(kk == 30)),
                             reads=["dgA%d" % b, "dgD%d" % b, "hch"], writes=["ag%d" % (2 + b)], sig=(kk == 30))
                    k.op("act", lambda e: e.activation(out=acc[:, fc, :n], in_=ag[2 + b][:, :n], func=AF.Identity,
                                                       bias=dwb[:, fc:fc + 1], scale=1.0),
                         reads=["ag%d" % (2 + b), "cst"], writes=[ak])
                if tok0 + n < NTOK:
                    k.op("dve", lambda e: e.tensor_copy(out=hch[:, :, 0:32], in_=hch[:, :, n:n + 32]),
                         reads=["hch"] + ["acc%d" % f for f in range(8)], writes=["hch"])
                for fc in range(8):
                    b = fc % 2
                    k.op("act", lambda e: e.activation(out=sq[b][:, :n], in_=acc[:, fc, :n], func=AF.Square),
                         reads=["acc%d" % fc], writes=["sq%d" % b])
                    k.op("pe", lambda e: e.matmul(s1[:, :n], lhsT=ones_f[:], rhs=acc[:, fc, :n],
                                                  start=(fc == 0), stop=(fc == 7)),
                         reads=["acc%d" % fc, "ones_f"], writes=["s1"])
                    k.op("pe", lambda e: e.matmul(s2[:, :n], lhsT=ones_f[:], rhs=sq[b][:, :n],
                                                  start=(fc == 0), stop=(fc == 7)),
                         reads=["sq%d" % b, "ones_f"], writes=["s2"])
                k.op("dve", lambda e: e.tensor_scalar(out=mean_t[:, :n], in0=s1[:, :n], scalar1=1.0 / D, scalar2=None,
                                                      op0=ALU.mult), reads=["s1"], writes=["mean_t"])
                k.op("dve", lambda e: e.tensor_tensor(out=msq_t[:, :n], in0=mean_t[:, :n], in1=mean_t[:, :n],
                                                      op=ALU.mult), reads=["mean_t"], writes=["msq_t"])
                k.op("dve", lambda e: e.scalar_tensor_tensor(out=rstd_t[:, :n], in0=s2[:, :n], scalar=1.0 / D,
                                                             in1=msq_t[:, :n], op0=ALU.mult, op1=ALU.subtract),
                     reads=["s2", "msq_t"], writes=["rstd_t"])
                k.op("dve", lambda e: e.tensor_scalar(out=rstd_t[:, :n], in0=rstd_t[:, :n], scalar1=EPS, scalar2=None,
                                                      op0=ALU.add), reads=["rstd_t"], writes=["rstd_t"])
                k.op("act", lambda e: e.activation(out=rstd_t[:, :n], in_=rstd_t[:, :n], func=AF.Sqrt),
                     reads=["rstd_t"], writes=["rstd_t"])
                k.op("dve", lambda e: e.reciprocal(out=rstd_t[:, :n], in_=rstd_t[:, :n]),
                     reads=["rstd_t"], writes=["rstd_t"])
                for m in range(4):
                    pr = (2 * m, 2 * m + 1)
                    for fc in pr:
                        k.op("dve", lambda e, fc=fc: e.tensor_tensor(out=acc[:, fc, :n], in0=acc[:, fc, :n], in1=mean_t[:, :n],
                                                                     op=ALU.subtract),
                             reads=["acc%d" % fc, "mean_t"], writes=["acc%d" % fc])
                    for fc in pr:
                        k.op("dve", lambda e, fc=fc: e.tensor_tensor(out=acc[:, fc, :n], in0=acc[:, fc, :n], in1=rstd_t[:, :n],
                                                                     op=ALU.mult),
                             reads=["acc%d" % fc, "rstd_t"], writes=["acc%d" % fc])
                    for fc in pr:
                        k.op("act", lambda e, fc=fc: e.activation(out=hs[:, fc, :n], in_=acc[:, fc, :n], func=AF.Silu,
                                                                  bias=clnb[:, fc:fc + 1], scale=clng[:, fc:fc + 1]),
                             reads=["acc%d" % fc, "cst"], writes=["hs"])
                for ti, tile in enumerate(tiles):
                    for half in range(2):
                        for fc in range(8):
                            k.op("pe", lambda e, fc=fc: e.matmul(
                                mix[:, half * 512:(half + 1) * 512], lhsT=hs[:, fc, ti * 128:(ti + 1) * 128],
                                rhs=wo[:, fc, half * 512:(half + 1) * 512], start=(fc == 0), stop=(fc == 7)),
                                reads=["hs", "wo"], writes=["mix"], sig=(fc == 7))
                    k.dma("sp", "xt", xtile[:, 0, :], xtm_d[tile * 128:(tile + 1) * 128, :], writes=["xtile"])
                    r = rtile[:, 0, :]
                    k.op("dve", lambda e: e.scalar_tensor_tensor(out=r, in0=xtile[:, 0, :], scalar=ALPHA, in1=mix[:],
                                                                 op0=ALU.mult, op1=ALU.add),
                         reads=["xtile", "mix"], writes=["rtile"])
                    k.op("dve", lambda e: e.tensor_tensor(out=r, in0=r, in1=bout_b[:], op=ALU.add),
                         reads=["rtile", "bout_b"], writes=["rtile"])
                    o = otile[:, 0, :]
                    layer_norm_tm(r, "rtile", o, "otile")
                    k.dma("sp", "xst", xs_d[0][tile * 128:(tile + 1) * 128, :], o, reads=["otile"])
                    to_feature_major(o, "otile", tile, [ag[0], ag[1]], "ag")
                tok0 += n
        k.barrier()

    def phase_s(layer):
        with contextlib.ExitStack() as ph:
            wq = sb(ph, "wq", [128, DC, 2 * D], BF16)
            keysT = sb(ph, "keysT", [128, 2, 128], F32)
            qT = sb(ph, "qT", [128, 16, 256], F32)
            scA = sb(ph, "scA", [128, 2048], F32)
            scB = sb(ph, "scB", [128, 2048], F32)

            class _V:
                def __init__(self, t, a):
                    self.t, self.a = t, a

                def __getitem__(self, idx):
                    return self.t[:].rearrange("p (a b) -> p a b", a=self.a)[idx]
            sc = _V(scA, 16)
            scr = _V(scB, 16)
            sv = sb(ph, "sv", [128, 16, 16], F32)
            si = sb(ph, "si", [128, 16, 16], U32)
            sif = sb(ph, "sif", [128, 16, 16], F32)
            cand = _V(scA, 8)
            cscr = _V(scB, 8)
            best = sb(ph, "best", [128, 8, 16], F32)
            posu = sb(ph, "posu", [128, 8, 16], U32)
            k1u = sb(ph, "k1u", [128, 8, 16], U32)
            k2u = sb(ph, "k2u", [128, 8, 16], U32)
            k1f = sb(ph, "k1f", [128, 8, 16], F32)
            k2f = sb(ph, "k2f", [128, 8, 16], F32)
            class _V4:
                def __getitem__(self, idx):
                    return scB[:].rearrange("p (h a b) -> p h a b", h=8, a=16)[idx]
            E1 = _V4()
            E2 = E1
            idi = sb(ph, "idi", [128, 128], F32)
            idj = sb(ph, "idj", [128, 128], F32)
            gt = sb(ph, "gt", [128, 8, 16], F32)
            zs = sb(ph, "zs", [128, 8], F32)
            q_ps = [ps(ph, "q_ps%d" % i, [128, 512]) for i in range(2)]
            sc_ps = ps(ph, "sc_ps", [128, 2048])
            tp = ps(ph, "tp", [128, 512])

            wq_v = pwq_d[layer].rearrange("(c p) n -> p c n", p=128)
            for c in range(DC):
                k.dma("pool", "wload", wq[:, c, :], wq_v[:, c, :], writes=["wq"])
            k.dma("sp", "cload", keysT[:], keysT_d[layer], writes=["keysT"])
            tsteps = table_steps(0, ph) if (layer == 0 and want_tables0) else iter(())
            per_tile = (NJ + NTL - 1) // NTL

            tok0 = 128 if layer == 1 else 0
            while tok0 < NTOK:
                n = min(256, NTOK - tok0)
                tiles = list(range(tok0 // 128, (tok0 + n) // 128))
                col0 = 32 + tok0
                for hp in range(16):
                    b = hp % 2
                    for dc in range(DC):
                        k.op("pe", lambda e, dc=dc: e.matmul(q_ps[b][:, :n], lhsT=wq[:, dc, hp * 128:(hp + 1) * 128],
                                                             rhs=xT_res[:, dc, col0:col0 + n],
                                                             start=(dc == 0), stop=(dc == DC - 1)),
                             reads=["wq", "xTall"] + [xkey(t) for t in tiles], writes=["q_ps%d" % b],
                             sig=(dc == DC - 1))
                    if b == 0:
                        k.op("act", lambda e: e.copy(out=qT[:, hp, :n], in_=q_ps[b][:, :n]),
                             reads=["q_ps%d" % b], writes=["qT"])
                    else:
                        k.op("dve", lambda e: e.tensor_copy(out=qT[:, hp, :n], in_=q_ps[b][:, :n]),
                             reads=["q_ps%d" % b], writes=["qT"])
                for ti, tile in enumerate(tiles):
                    for hp in range(16):
                        k.op("pe", lambda e: e.matmul(sc_ps[:, hp * 128:(hp + 1) * 128],
                                                      lhsT=qT[:, hp, ti * 128:(ti + 1) * 128],
                                                      rhs=keysT[:, hp % 2, :], start=True, stop=True),
                             reads=["qT", "keysT"], writes=["sc_ps"], sig=(hp == 15))
                    k.op("act", lambda e: e.copy(out=sc[:].rearrange("p a b -> p (a b)"), in_=sc_ps[:]),
                         reads=["sc_ps"], writes=["sc"])
                    SCRK = ["scr%d" % i for i in range(16)]
                    for m in range(8):
                        for _ in range((per_tile + 7) // 8):
                            next(tsteps, None)
                        pr = (2 * m, 2 * m + 1)
                        for hp in pr:
                            k.op("dve", lambda e, hp=hp: e.max(out=sv[:, hp, 0:8], in_=sc[:, hp, :]),
                                 reads=["sc"], writes=["svA%d" % hp])
                        for hp in pr:
                            k.op("dve", lambda e, hp=hp: e.max_index(out=si[:, hp, 0:8], in_max=sv[:, hp, 0:8],
                                                                     in_values=sc[:, hp, :]),
                                 reads=["sc", "svA%d" % hp], writes=["siA%d" % hp])
                        for hp in pr:
                            k.op("dve", lambda e, hp=hp: e.match_replace(out=scr[:, hp, :], in_to_replace=sv[:, hp, 0:8],
                                                                         in_values=sc[:, hp, :], imm_value=-1e30),
                                 reads=["sc", "svA%d" % hp], writes=["scr%d" % hp])
                        for hp in pr:
                            k.op("dve", lambda e, hp=hp: e.max(out=sv[:, hp, 8:16], in_=scr[:, hp, :]),
                                 reads=["scr%d" % hp], writes=["svB%d" % hp])
                        for hp in pr:
                            k.op("dve", lambda e, hp=hp: e.max_index(out=si[:, hp, 8:16], in_max=sv[:, hp, 8:16],
                                                                     in_values=scr[:, hp, :]),
                                 reads=["scr%d" % hp, "svB%d" % hp], writes=["siB%d" % hp])
                    SVK = ["svA%d" % i for i in range(16)] + ["svB%d" % i for i in range(16)]
                    SIK = ["siA%d" % i for i in range(16)] + ["siB%d" % i for i in range(16)]
                    k.op("dve", lambda e: e.tensor_copy(out=sif[:], in_=si[:]), reads=SIK, writes=["sif"])
                    sv4 = sv[:].rearrange("p (h s) k -> p h s k", s=2)
                    sif4 = sif[:].rearrange("p (h s) k -> p h s k", s=2)
                    k.op("dve", lambda e: e.tensor_tensor(
                        out=cand[:].rearrange("p h (a b) -> p h a b", a=16),
                        in0=sv4[:, :, 0, :].unsqueeze(3).to_broadcast([128, 8, 16, 16]),
                        in1=sv4[:, :, 1, :].unsqueeze(2).to_broadcast([128, 8, 16, 16]), op=ALU.add),
                        reads=SVK, writes=["sc"])
                    for m in range(4):
                        pr = (2 * m, 2 * m + 1)
                        ck = lambda h: ["scr%d" % (2 * h), "scr%d" % (2 * h + 1)]
                        for h in pr:
                            k.op("dve", lambda e, h=h: e.max(out=best[:, h, 0:8], in_=cand[:, h, :]),
                                 reads=["sc"], writes=["bA%d" % h])
                        for h in pr:
                            k.op("dve", lambda e, h=h: e.max_index(out=posu[:, h, 0:8], in_max=best[:, h, 0:8],
                                                                   in_values=cand[:, h, :]),
                                 reads=["sc", "bA%d" % h], writes=["pA%d" % h])
                        for h in pr:
                            k.op("dve", lambda e, h=h: e.match_replace(out=cscr[:, h, :], in_to_replace=best[:, h, 0:8],
                                                                       in_values=cand[:, h, :], imm_value=-1e30),
                                 reads=["sc", "bA%d" % h], writes=ck(h))
                        for h in pr:
                            k.op("dve", lambda e, h=h: e.max(out=best[:, h, 8:16], in_=cscr[:, h, :]),
                                 reads=ck(h), writes=["bB%d" % h])
                        for h in pr:
                            k.op("dve", lambda e, h=h: e.max_index(out=posu[:, h, 8:16], in_max=best[:, h, 8:16],
                                                                   in_values=cscr[:, h, :]),
                                 reads=ck(h) + ["bB%d" % h], writes=["pB%d" % h])
                    BK = ["bA%d" % i for i in range(8)] + ["bB%d" % i for i in range(8)]
                    PK = ["pA%d" % i for i in range(8)] + ["pB%d" % i for i in range(8)]
                    k.op("dve", lambda e: e.tensor_single_scalar(out=k1u[:], in_=posu[:], scalar=4,
                                                                 op=ALU.logical_shift_right), reads=PK, writes=["k1u"])
                    k.op("dve", lambda e: e.tensor_single_scalar(out=k2u[:], in_=posu[:], scalar=15,
                                                                 op=ALU.bitwise_and), reads=PK, writes=["k2u"])
                    k.op("dve", lambda e: e.tensor_copy(out=k1f[:], in_=k1u[:]), reads=["k1u"], writes=["k1f"])
                    k.op("dve", lambda e: e.tensor_copy(out=k2f[:], in_=k2u[:]), reads=["k2u"], writes=["k2f"])
                    io16 = iota_f[:, 0:16].unsqueeze(1).unsqueeze(1).to_broadcast([128, 8, 16, 16])
                    for (kf, kfk, Et, Ek, side, dst, dk) in ((k1f, "k1f", E1, SCRK, 0, idi, "idi"),
                                                           (k2f, "k2f", E2, SCRK, 1, idj, "idj")):
                        k.op("dve", lambda e: e.tensor_tensor(out=Et[:], in0=kf[:].unsqueeze(3).to_broadcast([128, 8, 16, 16]),
                                                              in1=io16, op=ALU.is_equal),
                             reads=[kfk, "iota_f"], writes=Ek)
                        k.op("dve", lambda e: e.tensor_tensor(
                            out=Et[:], in0=Et[:], in1=sif4[:, :, side, :].unsqueeze(2).to_broadcast([128, 8, 16, 16]),
                            op=ALU.mult), reads=Ek + ["sif"], writes=Ek)
                        k.op("dve", lambda e: e.tensor_reduce(out=dst[:].rearrange("p (h k) -> p h k", h=8), in_=Et[:],
                                                              axis=AX.X, op=ALU.add), reads=Ek, writes=[dk])
                    k.op("dve", lambda e: e.tensor_tensor(out=gt[:], in0=best[:],
                                                          in1=best[:, :, 0:1].to_broadcast([128, 8, 16]), op=ALU.subtract),
                         reads=BK, writes=["gt"])
                    k.op("act", lambda e: e.activation(out=gt[:], in_=gt[:], func=AF.Exp), reads=["gt"], writes=["gt"])
                    k.op("dve", lambda e: e.tensor_reduce(out=zs[:], in_=gt[:], axis=AX.X, op=ALU.add),
                         reads=["gt"], writes=["zs"])
                    k.op("dve", lambda e: e.reciprocal(out=zs[:], in_=zs[:]), reads=["zs"], writes=["zs"])
                    k.op("dve", lambda e: e.tensor_tensor(out=gt[:], in0=gt[:],
                                                          in1=zs[:].unsqueeze(2).to_broadcast([128, 8, 16]), op=ALU.mult),
                         reads=["gt", "zs"], writes=["gt"])
                    srcs = ((idi[:], "idi", IDXI, "IDXI"), (idj[:], "idj", IDXJ, "IDXJ"),
                            (gt[:].rearrange("p h k -> p (h k)"), "gt", GATE, "GATE"))
                    for qi, (s_ap, s_k, dst, d_k) in enumerate(srcs):
                        k.op("pe", lambda e: e.transpose(out=tp[:, qi * 128:(qi + 1) * 128], in_=s_ap, identity=ident[:]),
                             reads=[s_k, "ident"], writes=["tp"])
                    for qi, (s_ap, s_k, dst, d_k) in enumerate(srcs):
                        k.op("act", lambda e: e.copy(out=dst[:, tile * 128:(tile + 1) * 128],
                                                     in_=tp[:, qi * 128:(qi + 1) * 128]),
                             reads=["tp"], writes=[d_k])
                tok0 += n
            for _ in tsteps:
                pass
        k.barrier()

    def phase_d(layer, xin_d, xout_d, final):
        with contextlib.ExitStack() as ph:
            xtile = sb(ph, "xtile", [128, 2, D], F32)
            rtile = sb(ph, "rtile", [128, 2, D], F32)
            otile = sb(ph, "otile", [128, 2, D], F32)
            G = sb(ph, "G", [128, 256, 128], BF16)
            ub = [sb(ph, "ub%d" % i, [128, DC, 128], BF16) for i in range(NSLOT)]
            vb = [sb(ph, "vb%d" % i, [128, D], BF16) for i in range(NSLOT)]
            TB = 16
            Pb = [sb(ph, "Pb%d" % i, [128, TB, 128], BF16) for i in range(2)]
            Qb = [sb(ph, "Qb%d" % i, [128, TB, 128], BF16) for i in range(2)]
            gl = [sb(ph, "gl%d" % i, [128, 256], F32) for i in range(3)]
            ab = [sb(ph, "ab%d" % i, [128, 256], BF16) for i in range(3)]
            out_ps = [ps(ph, "out_ps%d" % i, [128, D]) for i in range(2)]
            h_ps = [ps(ph, "h_ps%d" % i, [128, 512]) for i in range(2)]
            g_ps = [ps(ph, "g_ps%d" % i, [128, 512]) for i in range(2)]
            hbuf = [(h_ps[0], "h_ps0"), (h_ps[1], "h_ps1"), (g_ps[0], "g_ps0"), (g_ps[1], "g_ps1")]
            load_ln(lnfg_d, lnfb_d, layer)

            tile0 = 1 if final else 0
            while tile0 < NTL:
                nsub = min(2, NTL - tile0)
                T = nsub * 128
                tiles = list(range(tile0, tile0 + nsub))
                t0 = tile0 * 128
                col0 = 32 + t0
                xk = ["xTall"] + [xkey(t) for t in tiles]
                io_bc = iota_b[:].unsqueeze(1).to_broadcast([128, TB, 128])
                for tb in range(T // TB):
                    b = tb % 2
                    ta = t0 + tb * TB
                    k.op("dve", lambda e: e.tensor_tensor(
                        out=Pb[b][:], in0=io_bc, in1=IDXI[:, ta:ta + TB].unsqueeze(2).to_broadcast([128, TB, 128]),
                        op=ALU.is_equal), reads=["iota_b", "IDXI"], writes=["Pb%d" % b])
                    k.op("dve", lambda e: e.tensor_tensor(
                        out=Qb[b][:], in0=io_bc, in1=IDXJ[:, ta:ta + TB].unsqueeze(2).to_broadcast([128, TB, 128]),
                        op=ALU.is_equal), reads=["iota_b", "IDXJ"], writes=["Qb%d" % b])
                    k.op("dve", lambda e: e.tensor_tensor(
                        out=Pb[b][:], in0=Pb[b][:], in1=GATE[:, ta:ta + TB].unsqueeze(2).to_broadcast([128, TB, 128]),
                        op=ALU.mult), reads=["Pb%d" % b, "GATE"], writes=["Pb%d" % b])
                    for q4 in range(TB // 4):
                        gb = (tb * (TB // 4) + q4) % 2
                        for u in range(4):
                            uu = q4 * 4 + u
                            k.op("pe", lambda e, u=u, uu=uu: e.matmul(
                                g_ps[gb][:, u * 128:(u + 1) * 128], lhsT=Pb[b][:, uu, :], rhs=Qb[b][:, uu, :],
                                start=True, stop=True),
                                reads=["Pb%d" % b, "Qb%d" % b], writes=["g_ps%d" % gb], sig=(u == 3))
                        tq = tb * TB + q4 * 4
                        k.op("act", lambda e: e.copy(out=G[:, tq:tq + 4, :],
                                                     in_=g_ps[gb][:].rearrange("p (a b) -> p a b", a=4)),
                             reads=["g_ps%d" % gb], writes=["G"])
                def mm1(j):
                    s_ = j % NSLOT
                    hp_, hk = hbuf[j % 4]
                    k.dma("sp", "wu%d" % s_, ub[s_][:].rearrange("p c i -> p (c i)"), UTb_d[layer, j], writes=["wslotU%d" % s_])
                    k.dma("sp", "wv%d" % s_, vb[s_][:], Vb_d[layer, j], writes=["wslotV%d" % s_])
                    for dc in range(DC):
                        k.op("pe", lambda e, dc=dc: e.matmul(hp_[:, :T], lhsT=ub[s_][:, dc, :],
                                                             rhs=xT_res[:, dc, col0:col0 + T],
                                                             start=(dc == 0), stop=(dc == DC - 1)),
                             reads=["wslotU%d" % s_] + xk, writes=[hk], sig=(dc == DC - 1))

                def mm2(j):
                    s_ = j % NSLOT
                    b = j % 3
                    hp_, hk = hbuf[j % 4]
                    k.op("act", lambda e: e.activation(out=gl[b][:, :T], in_=hp_[:, :T], func=AF.Gelu),
                         reads=[hk], writes=["gl%d" % b])
                    k.op("dve", lambda e: e.tensor_tensor(out=ab[b][:, :T], in0=gl[b][:, :T], in1=G[:, 0:T, j],
                                                          op=ALU.mult), reads=["gl%d" % b, "G"], writes=["ab%d" % b])
                    for ts in range(nsub):
                        for dh in range(2):
                            k.op("pe", lambda e, ts=ts, dh=dh: e.matmul(
                                out_ps[ts][:, dh * 512:(dh + 1) * 512], lhsT=ab[b][:, ts * 128:(ts + 1) * 128],
                                rhs=vb[s_][:, dh * 512:(dh + 1) * 512], start=(j == 0), stop=(j == NJ - 1)),
                                reads=["ab%d" % b, "wslotV%d" % s_], writes=["out_ps"],
                                sig=(ts == nsub - 1 and dh == 1))

                for ts, tile in enumerate(tiles):
                    k.dma("sp", "xt", xtile[:, ts, :], xin_d[tile * 128:(tile + 1) * 128, :], writes=["xtile%d" % ts])
                mm1(0)
                mm1(1)
                for j in range(NJ):
                    if j + 2 < NJ:
                        mm1(j + 2)
                    mm2(j)
                for ts, tile in enumerate(tiles):
                    r = rtile[:, ts, :]
                    k.op("dve", lambda e: e.scalar_tensor_tensor(out=r, in0=xtile[:, ts, :], scalar=ALPHA,
                                                                 in1=out_ps[ts][:], op0=ALU.mult, op1=ALU.add),
                         reads=["xtile%d" % ts, "out_ps"], writes=["rtile%d" % ts])
                    o = otile[:, ts, :]
                    layer_norm_tm(r, "rtile%d" % ts, o, "otile%d" % ts)
                    if final:
                        if tile >= 1:
                            k.dma("sp", "ost", out_d[(tile - 1) * 128:tile * 128, :], o, reads=["otile%d" % ts])
                    else:
                        k.dma("sp", "xst", xout_d[tile * 128:(tile + 1) * 128, :], o, reads=["otile%d" % ts])
                        to_feature_major(o, "otile%d" % ts, tile, [g_ps[0], g_ps[1]], "g_ps")
                tile0 += nsub
        k.barrier()

    def phase_m1(xin_d, xout_d):
        with contextlib.ExitStack() as ph:
            xtile = sb(ph, "xtile", [128, 1, D], F32)
            rtile = sb(ph, "rtile", [128, 1, D], F32)
            otile = sb(ph, "otile", [128, 1, D], F32)
            wq = sb(ph, "awq", [128, DC, D], BF16)
            wo = sb(ph, "awo", [128, DC, D], BF16)
            KT = sb(ph, "KT", [128, 4, NTOK], BF16)
            VS = sb(ph, "VS", [128, NTL, 256], BF16)
            qt = sb(ph, "qt", [128, D], F32)
            qTs = sb(ph, "qTs", [128, 8, 128], BF16)
            rt = [sb(ph, "rt%d" % i, [128, 16, 8], F32) for i in range(4)]
            posi = sb(ph, "posi", [128, NTL], I32)
            posf = sb(ph, "posf", [128, NTL], F32)
            invf = sb(ph, "invf", [128, 8], F32)
            ang = sb(ph, "ang", [128, NTL, 8], F32)
            sin_t = sb(ph, "sin_t", [128, NTL, 8], F32)
            cos_t = sb(ph, "cos_t", [128, NTL, 8], F32)
            amask = sb(ph, "amask", [128, 2, 256], F32)
            sink_b = sb(ph, "sink_b", [128, 16], F32)
            nsink_b = sb(ph, "nsink_b", [128, 16], F32)
            sm = [sb(ph, "sm%d" % i, [128, 256], F32) for i in range(2)]
            pb = [sb(ph, "pb%d" % i, [128, 256], BF16) for i in range(2)]
            pT = [sb(ph, "pT%d" % i, [128, 2, 128], BF16) for i in range(2)]
            sst = [sb(ph, "sst%d" % i, [128, 8], F32) for i in range(2)]
            osb = sb(ph, "osb", [128, D], F32)
            rden = sb(ph, "rden", [128, 16], F32)
            oT = sb(ph, "oT", [128, 8, 128], BF16)
            pj = [ps(ph, "pj%d" % i, [128, 512]) for i in range(2)]
            s_ps = [ps(ph, "s_ps%d" % i, [128, 512]) for i in range(2)]
            pT_ps = [ps(ph, "pT_ps%d" % i, [128, 1024], BF16) for i in range(2)]
            o_ps = ps(ph, "o_ps", [128, D])

            for (wt, wd, key, ncols) in ((wq, awq_d, "awq", D), (wo, awo_d, "awo", D)):
                wv = wd.rearrange("(c p) n -> p c n", p=128)
                for c in range(DC):
                    k.dma("pool", "wload", wt[:, c, :], wv[:, c, :], writes=[key])
            k.dma("sp", "cload", posi[:], pos_d[:, :], writes=["posi"])
            k.dma("sp", "cload", invf[:], invf_d.partition_broadcast(128), writes=["invf"])
            k.dma("sp", "cload", amask[:], amask_d.rearrange("a p k -> p a k"), writes=["amask"])
            k.dma("sp", "cload", sink_b[:], sinks_d.partition_broadcast(128), writes=["sink_b"])
            load_ln(lnmg_d, lnmb_d, 1)
            k.op("dve", lambda e: e.tensor_scalar(out=nsink_b[:], in0=sink_b[:], scalar1=-1.0, scalar2=None, op0=ALU.mult),
                 reads=["sink_b"], writes=["nsink_b"])
            k.op("dve", lambda e: e.tensor_copy(out=posf[:], in_=posi[:]), reads=["posi"], writes=["posf"])
            k.op("dve", lambda e: e.tensor_tensor(out=ang[:], in0=posf[:].unsqueeze(2).to_broadcast([128, NTL, 8]),
                                                  in1=invf[:].unsqueeze(1).to_broadcast([128, NTL, 8]), op=ALU.mult),
                 reads=["posf", "invf"], writes=["ang"])
            angs = sb(ph, "angs", [128, NTL, 8], F32)
            angk = sb(ph, "angk", [128, NTL, 8], F32)
            angi = sb(ph, "angi", [128, NTL, 8], I32)
            for (dst, dk, shift) in ((sin_t, "sin_t", 0.0), (cos_t, "cos_t", 0.5 * PI)):
                k.op("dve", lambda e: e.tensor_scalar(out=angs[:], in0=ang[:], scalar1=shift, scalar2=None, op0=ALU.add),
                     reads=["ang"], writes=["angs"])
                k.op("dve", lambda e: e.tensor_scalar(out=angk[:], in0=angs[:], scalar1=1.0 / (2.0 * PI), scalar2=None,
                                                      op0=ALU.mult), reads=["angs"], writes=["angk"])
                k.op("dve", lambda e: e.tensor_copy(out=angi[:], in_=angk[:]), reads=["angk"], writes=["angi"])
                k.op("dve", lambda e: e.tensor_copy(out=angk[:], in_=angi[:]), reads=["angi"], writes=["angk"])
                k.op("dve", lambda e: e.scalar_tensor_tensor(out=dst[:], in0=angk[:], scalar=-2.0 * PI, in1=angs[:],
                                                             op0=ALU.mult, op1=ALU.add),
                     reads=["angk", "angs"], writes=[dk])
                k.op("dve", lambda e: e.tensor_scalar(out=angk[:], in0=dst[:], scalar1=PI, scalar2=2.0 * PI,
                                                      op0=ALU.is_gt, op1=ALU.mult), reads=[dk], writes=["angk"])
                k.op("dve", lambda e: e.tensor_tensor(out=dst[:], in0=dst[:], in1=angk[:], op=ALU.subtract),
                     reads=[dk, "angk"], writes=[dk])
                k.op("dve", lambda e: e.tensor_scalar(out=dst[:], in0=dst[:], scalar1=-3.1415925, scalar2=3.1415925,
                                                      op0=ALU.max, op1=ALU.min), reads=[dk], writes=[dk])
                k.op("act", lambda e: e.activation(out=dst[:], in_=dst[:], func=AF.Sin), reads=[dk], writes=[dk])

            def rope(v3, vkey, nh, tile):
                c_b = cos_t[:, tile, :].unsqueeze(1).to_broadcast([128, nh, 8])
                s_b = sin_t[:, tile, :].unsqueeze(1).to_broadcast([128, nh, 8])
                v1 = v3[:, :, 0:8]
                v2 = v3[:, :, 8:16]
                tt = [rt[i][:, 0:nh, :] for i in range(4)]
                k.op("dve", lambda e: e.tensor_tensor(out=tt[0], in0=v1, in1=c_b, op=ALU.mult), reads=[vkey, "cos_t"], writes=["rt0"])
                k.op("dve", lambda e: e.tensor_tensor(out=tt[1], in0=v2, in1=s_b, op=ALU.mult), reads=[vkey, "sin_t"], writes=["rt1"])
                k.op("dve", lambda e: e.tensor_tensor(out=tt[2], in0=v2, in1=c_b, op=ALU.mult), reads=[vkey, "cos_t"], writes=["rt2"])
                k.op("dve", lambda e: e.tensor_tensor(out=tt[3], in0=v1, in1=s_b, op=ALU.mult), reads=[vkey, "sin_t"], writes=["rt3"])
                k.op("dve", lambda e: e.tensor_tensor(out=v1, in0=tt[0], in1=tt[1], op=ALU.subtract), reads=["rt0", "rt1"], writes=[vkey])
                k.op("dve", lambda e: e.tensor_tensor(out=v2, in0=tt[2], in1=tt[3], op=ALU.add), reads=["rt2", "rt3"], writes=[vkey])

            kvst = contextlib.ExitStack()
            kvw = sb(kvst, "kvw", [128, DC, 512], BF16)
            kvt = sb(kvst, "kvt", [128, 512], F32)
            kd = sb(kvst, "kd", [128, 4, 2, 64], F32)
            wv_ = kvw_d.rearrange("(c p) n -> p c n", p=128)
            for c in range(DC):
                k.dma("pool", "wload", kvw[:, c, :], wv_[:, c, :], writes=["kvw"])
            for tile in range(NTL):
                col0 = 32 + tile * 128
                for dc in range(DC):
                    k.op("pe", lambda e, dc=dc: e.matmul(pj[0][:, :], lhsT=xT_res[:, dc, col0:col0 + 128], rhs=kvw[:, dc, :],
                                                         start=(dc == 0), stop=(dc == DC - 1)),
                         reads=["kvw", "xTall", xkey(tile)], writes=["pj0"], sig=(dc == DC - 1))
                k.op("act", lambda e: e.copy(out=kvt[:], in_=pj[0][:]), reads=["pj0"], writes=["kvt"])
                k.op("dve", lambda e: e.tensor_copy(out=VS[:, tile, :], in_=kvt[:, 256:512]), reads=["kvt"], writes=["VS"])
                rope(kvt[:, 0:256].rearrange("p (h d) -> p h d", h=4), "kvt", 4, tile)
                kv3 = kvt[:, 0:256].rearrange("p (h d) -> p h d", h=4)
                for dup in range(2):
                    k.op("dve", lambda e: e.tensor_copy(out=kd[:, :, dup, :], in_=kv3), reads=["kvt"], writes=["kd"])
                for g in range(4):
                    k.op("pe", lambda e: e.transpose(out=pj[1][:, g * 128:(g + 1) * 128],
                                                     in_=kd[:, g, :, :].rearrange("p a b -> p (a b)"), identity=ident[:]),
                         reads=["kd", "ident"], writes=["pj1"], sig=(g == 3))
                k.op("act", lambda e: e.copy(out=KT[:, :, tile * 128:(tile + 1) * 128],
                                             in_=pj[1][:].rearrange("p (a b) -> p a b", a=4)),
                     reads=["pj1"], writes=["KT"])

            cnt = 0
            k.barrier()
            kvst.close()
            tsteps = table_steps(1, ph) if want_tables1 else iter(())
            per_tile = (NJ + NTL - 2) // (NTL - 1)
            qTs2 = [qTs, sb(ph, "qTs1", [128, 8, 128], BF16)]

            def preA(tile):
                col0 = 32 + tile * 128
                for half in range(2):
                    for dc in range(DC):
                        k.op("pe", lambda e, dc=dc: e.matmul(pj[half][:, :], lhsT=xT_res[:, dc, col0:col0 + 128],
                                                             rhs=wq[:, dc, half * 512:(half + 1) * 512],
                                                             start=(dc == 0), stop=(dc == DC - 1)),
                             reads=["awq", "xTall", xkey(tile)], writes=["pj%d" % half], sig=(dc == DC - 1))
                    k.op("act", lambda e: e.copy(out=qt[:, half * 512:(half + 1) * 512], in_=pj[half][:]),
                         reads=["pj%d" % half], writes=["qt"])
                rope(qt[:].rearrange("p (h d) -> p h d", h=16), "qt", 16, tile)

            def preB(tile):
                qd = qTs2[tile % 2]
                for half in range(2):
                    for q in range(4):
                        c = half * 4 + q
                        k.op("pe", lambda e, c=c, q=q: e.transpose(out=pj[half][:, q * 128:(q + 1) * 128],
                                                                   in_=qt[:, c * 128:(c + 1) * 128], identity=ident[:]),
                             reads=["qt", "ident"], writes=["pj%d" % half], sig=(q == 3))
                    k.op("act", lambda e: e.copy(out=qd[:, half * 4:(half + 1) * 4, :],
                                                 in_=pj[half][:].rearrange("p (a b) -> p a b", a=4)),
                         reads=["pj%d" % half], writes=["qTs%d" % (tile % 2)])

            def heads(tile):
                qd = qTs2[tile % 2]
                qk = "qTs%d" % (tile % 2)
                mi = 0 if tile == 1 else 1

                def front(h):
                    b = h % 2
                    g = h // 4
                    hb = 64 * (h % 2)
                    k.op("pe", lambda e: e.matmul(s_ps[b][:, 0:256], lhsT=qd[hb:hb + 64, h // 2, :],
                                                  rhs=KT[hb:hb + 64, g, (tile - 1) * 128:(tile + 1) * 128],
                                                  start=True, stop=True),
                         reads=[qk, "KT"], writes=["s_ps%d" % b])
                    k.op("dve", lambda e: e.tensor_tensor(out=sm[b][:], in0=s_ps[b][:, 0:256], in1=amask[:, mi, :], op=ALU.add),
                         reads=["s_ps%d" % b, "amask"], writes=["sm%d" % b])
                    k.op("dve", lambda e: e.tensor_reduce(out=sst[b][:, 0:1], in_=sm[b][:], axis=AX.X, op=ALU.max),
                         reads=["sm%d" % b], writes=["sstA%d" % b])
                    k.op("dve", lambda e: e.tensor_scalar(out=sst[b][:, 1:2], in0=sst[b][:, 0:1], scalar1=-0.125,
                                                          scalar2=nsink_b[:, h:h + 1], op0=ALU.mult, op1=ALU.min),
                         reads=["sstA%d" % b, "nsink_b"], writes=["sstB%d" % b])
                    k.op("act", lambda e: e.activation(out=pb[b][:], in_=sm[b][:], func=AF.Exp, bias=sst[b][:, 1:2],
                                                       scale=0.125, accum_out=sst[b][:, 2:3]),
                         reads=["sm%d" % b, "sstB%d" % b], writes=["pb%d" % b, "sstC%d" % b])
                    k.op("act", lambda e: e.activation(out=sst[b][:, 3:4], in_=sink_b[:, h:h + 1], func=AF.Exp,
                                                       bias=sst[b][:, 1:2], scale=1.0),
                         reads=["sink_b", "sstB%d" % b], writes=["sstD%d" % b])

                def denom(h):
                    b = h % 2
                    k.op("dve", lambda e: e.tensor_tensor(out=sst[b][:, 4:5], in0=sst[b][:, 2:3], in1=sst[b][:, 3:4], op=ALU.add),
                         reads=["sstC%d" % b, "sstD%d" % b], writes=["sstE%d" % b])
                    k.op("dve", lambda e: e.reciprocal(out=rden[:, h:h + 1], in_=sst[b][:, 4:5]),
                         reads=["sstE%d" % b], writes=["rden"])

                def back(h):
                    b = h % 2
                    g = h // 4
                    for kb_ in range(2):
                        k.op("pe", lambda e, kb_=kb_: e.transpose(out=pT_ps[b][:, kb_ * 128:(kb_ + 1) * 128],
                                                                  in_=pb[b][:, kb_ * 128:(kb_ + 1) * 128], identity=identb[:]),
                             reads=["pb%d" % b, "identb"], writes=["pT_ps%d" % b], sig=(kb_ == 1))
                    k.op("dve", lambda e: e.tensor_copy(out=pT[b][:].rearrange("p a b -> p (a b)"), in_=pT_ps[b][:, 0:256]),
                         reads=["pT_ps%d" % b], writes=["pT%d" % b])
                    for kb_ in range(2):
                        k.op("pe", lambda e, kb_=kb_: e.matmul(o_ps[:, h * 64:(h + 1) * 64], lhsT=pT[b][:, kb_, :],
                                                               rhs=VS[:, tile - 1 + kb_, g * 64:(g + 1) * 64],
                                                               start=(kb_ == 0), stop=(kb_ == 1)),
                             reads=["pT%d" % b, "VS"], writes=["o_ps%d" % (h // 8)], sig=(kb_ == 1))

                def evac(h):
                    if h % 8 == 7:
                        hb8 = h // 8
                        k.op("dve", lambda e: e.tensor_tensor(
                            out=osb[:, hb8 * 512:(hb8 + 1) * 512].rearrange("p (h d) -> p h d", h=8),
                            in0=o_ps[:, hb8 * 512:(hb8 + 1) * 512].rearrange("p (h d) -> p h d", h=8),
                            in1=rden[:, hb8 * 8:(hb8 + 1) * 8].unsqueeze(2).to_broadcast([128, 8, 64]), op=ALU.mult),
                            reads=["o_ps%d" % hb8, "rden"], writes=["osb"])

                front(0)
                front(1)
                denom(0)
                for h in range(16):
                    back(h)
                    if h + 2 < 16:
                        front(h + 2)
                    if h + 1 < 16:
                        denom(h + 1)
                    evac(h)
                    yield h

            def post(tile):
                for half in range(2):
                    for q in range(4):
                        c = half * 4 + q
                        k.op("pe", lambda e, c=c, q=q: e.transpose(out=pj[half][:, q * 128:(q + 1) * 128],
                                                                   in_=osb[:, c * 128:(c + 1) * 128], identity=ident[:]),
                             reads=["osb", "ident"], writes=["pj%d" % half], sig=(q == 3))
                    k.op("act", lambda e: e.copy(out=oT[:, half * 4:(half + 1) * 4, :],
                                                 in_=pj[half][:].rearrange("p (a b) -> p a b", a=4)),
                         reads=["pj%d" % half], writes=["oT"])
                    yield
                for half in range(2):
                    for c in range(8):
                        k.op("pe", lambda e, c=c: e.matmul(pj[half][:, :], lhsT=oT[:, c, :],
                                                           rhs=wo[:, c, half * 512:(half + 1) * 512],
                                                           start=(c == 0), stop=(c == 7)),
                             reads=["oT", "awo"], writes=["pj%d" % half], sig=(c == 7))
                    yield
                k.dma("sp", "xt", xtile[:, 0, :], xin_d[tile * 128:(tile + 1) * 128, :], writes=["xtile"])
                r = rtile[:, 0, :]
                for half in range(2):
                    k.op("dve", lambda e: e.scalar_tensor_tensor(
                        out=r[:, half * 512:(half + 1) * 512], in0=xtile[:, 0, half * 512:(half + 1) * 512], scalar=ALPHA,
                        in1=pj[half][:], op0=ALU.mult, op1=ALU.add),
                        reads=["xtile", "pj%d" % half], writes=["rtile"])
                yield
                o = otile[:, 0, :]
                layer_norm_tm(r, "rtile", o, "otile")
                yield
                k.dma("sp", "xst", xout_d[tile * 128:(tile + 1) * 128, :], o, reads=["otile"])
                to_feature_major(o, "otile", tile, [pj[0], pj[1]], "pj")
                yield

            preA(1)
            preB(1)
            pending = iter(())
            for tile in range(1, NTL):
                if tile + 1 < NTL:
                    preA(tile + 1)
                for h in heads(tile):
                    if h % 2 == 1:
                        for _ in range((per_tile + 7) // 8):
                            next(tsteps, None)
                    next(pending, None)
                    if h == 8 and tile + 1 < NTL:
                        preB(tile + 1)
                for _ in pending:
                    pass
                pending = post(tile)
            for _ in pending:
                pass
            for _ in tsteps:
                pass
        k.barrier()

    allp = ("m0", "s0", "d0", "m1", "s1", "d1")
    phases = allp if phases is None else phases
    want_tables0 = "d0" in phases and "s0" in phases
    want_tables1 = "d1" in phases and "m1" in phases
    if ("d0" in phases and not want_tables0) or ("d1" in phases and not want_tables1):
        phase_w()
    if "m0" in phases:
        phase_m0()
    if "s0" in phases:
        phase_s(0)
    if "d0" in phases:
        phase_d(0, xs_d[0], xs_d[1], False)
    if "m1" in phases:
        phase_m1(xs_d[1], xs_d[2])
    if "s1" in phases:
        phase_s(1)
    if "d1" in phases:
        phase_d(1, xs_d[2], None, True)
    k.finish("sp")
    return nc, k


def _core_inputs(x, positions, S, S_core, core):
    per_seq = S // S_core
    b = core // per_seq
    start = (core % per_seq) * S_core
    NTOK = S_core + 128
    XW = NTOK + 32
    lo = start - 160
    xrows = np.zeros((XW, D), np.float32)
    hm = np.zeros((XW,), np.float32)
    pos = np.zeros((NTOK,), np.int32)
    src_lo = max(lo, 0)
    xrows[src_lo - lo:] = x[b, src_lo:start + S_core]
    hm[src_lo - lo:] = 1.0
    plo = max(start - 128, 0)
    pos[plo - (start - 128):] = positions[b, plo:start + S_core]
    tq = np.arange(128)[:, None]
    tk = np.arange(256)[None, :]
    dd = tq + 128 - tk
    band = (dd >= 0) & (dd < 128)
    first = band & (tk >= 128) if start == 0 else band
    amask = np.stack([np.where(first, 0.0, -1e30), np.where(band, 0.0, -1e30)]).astype(np.float32)
    return {
        "xT": np.ascontiguousarray(xrows.T),
        "xtm": np.ascontiguousarray(xrows[32:]),
        "hm": hm,
        "pos_pn": np.ascontiguousarray(pos.reshape(-1, 128).T),
        "amask": amask,
    }


def _pc(v):
    return np.ascontiguousarray(v.reshape(-1, 128).T)


def _shared_inputs(conv_w_in, conv_b_in, conv_dw, conv_dw_b, conv_ln_g, conv_ln_b, conv_w_out, conv_b_out,
                   kv_w, attn_w_q, attn_sinks, attn_w_o, peer_w_q, peer_sub_keys, peer_u, peer_v,
                   ln_mix_g, ln_mix_b, ln_ffn_g, ln_ffn_b):
    half = 8
    invf = (500000.0 ** (-(np.arange(half, dtype=np.float32) * 2.0 / 16.0))).astype(np.float32)
    UT = np.ascontiguousarray(peer_u.reshape(2, 128, 128, 8, 128).transpose(0, 2, 4, 3, 1))
    keysT = np.ascontiguousarray(peer_sub_keys.transpose(0, 3, 1, 2))
    return {
        "invf": invf,
        "w_in": np.ascontiguousarray(conv_w_in[0]),
        "b_in_pc": _pc(conv_b_in[0]),
        "dw_pc": np.ascontiguousarray(conv_dw[0].T.reshape(8, 128, 31).transpose(1, 0, 2)),
        "dwb_pc": _pc(conv_dw_b[0]),
        "clng_pc": _pc(conv_ln_g[0]),
        "clnb_pc": _pc(conv_ln_b[0]),
        "w_out": np.ascontiguousarray(conv_w_out[0]),
        "b_out": np.ascontiguousarray(conv_b_out[0]),
        "kv_w": np.ascontiguousarray(kv_w),
        "attn_w_q": np.ascontiguousarray(attn_w_q[0]),
        "sinks": np.ascontiguousarray(attn_sinks[0]),
        "attn_w_o": np.ascontiguousarray(attn_w_o[0]),
        "peer_w_q": np.ascontiguousarray(peer_w_q),
        "keysT": keysT,
        "UT": UT,
        "peer_v": np.ascontiguousarray(peer_v),
        "ln_mix_g": np.ascontiguousarray(ln_mix_g),
        "ln_mix_b": np.ascontiguousarray(ln_mix_b),
        "ln_ffn_g": np.ascontiguousarray(ln_ffn_g),
        "ln_ffn_b": np.ascontiguousarray(ln_ffn_b),
    }


def run(x, positions, weights, n_cores=8, dbg=False, phases=None, cores=None):
    x = np.asarray(x, np.float32)
    positions = np.asarray(positions, np.int32)
    B, S, _ = x.shape
    S_core = B * S // n_cores
    shared = _shared_inputs(**{kk: np.asarray(v) for kk, v in weights.items()})
    nc, kb = build(S_core, dbg=dbg, phases=phases)
    print('ninstr', kb.ninstr, {e: kb.cnt[e] for e in ('pe', 'act', 'dve', 'pool')}, flush=True)
    in_maps = []
    cores = list(range(n_cores)) if cores is None else cores
    for c in cores:
        m = dict(shared)
        m.update(_core_inputs(x, positions, S, S_core, c))
        in_maps.append(m)
    res = run_bass_kernel_spmd(nc, in_maps, core_ids=list(range(len(cores))))
    if len(cores) != n_cores:
        return None, res
    out = np.concatenate([r["out"] for r in res.results], axis=0).reshape(B, S, D)
    return out, res


def kernel(x, positions, **weights):
    out, _ = run(x, positions, weights)
    return out.astype(np.float32)
```

```python
import contextlib
import math
import numpy as np
import concourse.bass as bass
import concourse.mybir as mybir
from concourse.bass_utils import run_bass_kernel_spmd

F32 = mybir.dt.float32
BF16 = mybir.dt.bfloat16
I32 = mybir.dt.int32
U32 = mybir.dt.uint32
ALU = mybir.AluOpType
AF = mybir.ActivationFunctionType
AX = mybir.AxisListType

D = 1024
DC = 8
ALPHA = 4.0 ** 0.25
EPS = 1e-5
NJ = 128
NSLOT = 6
PI = math.pi


class KB:
    def __init__(self, nc):
        self.nc = nc
        self.es = contextlib.ExitStack()
        self.eng = {"pe": nc.tensor, "act": nc.scalar, "dve": nc.vector,
                    "pool": nc.gpsimd, "sp": nc.sync}
        self.sem = {}
        self.cnt = {}
        for e in ("pe", "act", "dve", "pool"):
            self.sem[e] = self.es.enter_context(nc.semaphore("s_" + e))
            self.cnt[e] = 0
        self.known = {e: {} for e in self.eng}
        self.res = {}
        self.ninstr = 0
        import os
        self.limit = int(os.environ.get("KB_LIMIT", "1000000000"))

    def dsem(self, name):
        if name not in self.sem:
            self.sem[name] = self.es.enter_context(self.nc.semaphore("d_" + name))
            self.cnt[name] = 0
        return name

    def _need(self, e, reads, writes):
        need = {}

        def add(sv):
            if sv is None:
                return
            s, v = sv
            if need.get(s, 0) < v:
                need[s] = v
        for k in reads:
            r = self.res.get(k)
            if r:
                add(r["w"])
        for k in writes:
            r = self.res.get(k)
            if r:
                add(r["w"])
                for s, v in r["r"].items():
                    add((s, v))
        kn = self.known[e]
        for s, v in need.items():
            if e == "pe" and s == "pe":
                continue
            if s not in ("pe", "act", "dve", "pool"):
                v = self.cnt[s]
            if kn.get(s, 0) < v:
                self.eng[e].wait_ge(self.sem[s], v)
                kn[s] = v

    def _mark(self, s, v, reads, writes):
        for k in writes:
            self.res[k] = {"w": (s, v), "r": {}}
        for k in reads:
            r = self.res.setdefault(k, {"w": None, "r": {}})
            if r["r"].get(s, 0) < v:
                r["r"][s] = v

    def op(self, e, fn, reads=(), writes=(), sig=True):
        if self.ninstr >= self.limit:
            self.ninstr += 1
            return None
        self._need(e, reads, writes)
        ins = fn(self.eng[e])
        self.ninstr += 1
        if sig:
            self.cnt[e] += 1
            ins.then_inc(self.sem[e], 1)
            v = self.cnt[e]
        else:
            v = self.cnt[e] + 1
        self._mark(e, v, reads, writes)
        return ins

    def dma(self, q, semname, out, in_, reads=(), writes=(), **kw):
        if self.ninstr >= self.limit:
            self.ninstr += 1
            return None
        self.dsem(semname)
        self._need(q, reads, writes)
        ins = self.eng[q].dma_start(out=out, in_=in_, **kw)
        self.cnt[semname] += 16
        ins.then_inc(self.sem[semname], 16)
        self._mark(semname, self.cnt[semname], reads, writes)
        self.ninstr += 1
        return ins

    def barrier(self):
        for e in ("pe", "act", "dve", "pool", "sp"):
            kn = self.known[e]
            for s, v in self.cnt.items():
                if v > 0 and kn.get(s, 0) < v:
                    self.eng[e].wait_ge(self.sem[s], v)
                    kn[s] = v
        self.res = {}

    def finish(self, e="sp"):
        kn = self.known[e]
        for s, v in self.cnt.items():
            if v > 0 and kn.get(s, 0) < v:
                self.eng[e].wait_ge(self.sem[s], v)
                kn[s] = v


def build(S_core, dbg=False, phases=None):
    NTL = S_core // 128 + 1
    NTOK = NTL * 128
    XW = NTOK + 32
    nc = bass.Bass("TRN2", target_bir_lowering=False)

    def din(name, shape, dt=F32):
        return nc.dram_tensor(name, list(shape), dt, kind="ExternalInput").ap()

    xT_d = din("xT", [D, XW])
    xtm_d = din("xtm", [NTOK, D])
    hm_d = din("hm", [XW])
    pos_d = din("pos_pn", [128, NTL], I32)
    amask_d = din("amask", [2, 128, 256])
    invf_d = din("invf", [8])
    w_in_d = din("w_in", [D, 2 * D])
    b_in_d = din("b_in_pc", [128, 16])
    dw_d = din("dw_pc", [128, 8, 31])
    dwb_d = din("dwb_pc", [128, 8])
    clng_d = din("clng_pc", [128, 8])
    clnb_d = din("clnb_pc", [128, 8])
    w_out_d = din("w_out", [D, D])
    b_out_d = din("b_out", [D])
    kvw_d = din("kv_w", [D, 512])
    awq_d = din("attn_w_q", [D, D])
    sinks_d = din("sinks", [16])
    awo_d = din("attn_w_o", [D, D])
    pwq_d = din("peer_w_q", [2, D, 2 * D])
    keysT_d = din("keysT", [2, 128, 2, 128])
    UT_d = din("UT", [2, NJ, 128, 8, 128])
    V_d = din("peer_v", [2, 128 * NJ, D])
    lnmg_d = din("ln_mix_g", [2, D])
    lnmb_d = din("ln_mix_b", [2, D])
    lnfg_d = din("ln_ffn_g", [2, D])
    lnfb_d = din("ln_ffn_b", [2, D])
    out_d = nc.dram_tensor("out", [S_core, D], F32, kind="ExternalOutput").ap()
    skind = "ExternalOutput" if dbg else "Internal"
    xs_d = [nc.dram_tensor("xs%d" % i, [NTOK, D], F32, kind=skind).ap() for i in range(3)]

    UTb_d = nc.dram_tensor("UTb", [2, NJ, 128, D], BF16).ap()
    Vb_d = nc.dram_tensor("Vb", [2, NJ, 128, D], BF16).ap()

    k = KB(nc)
    top = k.es

    uid = [0]

    def sb(st, name, shape, dt):
        uid[0] += 1
        return st.enter_context(nc.sbuf_tensor("sb%d_%s" % (uid[0], name), list(shape), dt))

    def ps(st, name, shape, dt=F32):
        uid[0] += 1
        return st.enter_context(nc.psum_tensor("ps%d_%s" % (uid[0], name), list(shape), dt))

    xT_res = sb(top, "xT_res", [128, DC, XW], BF16)
    IDXI = sb(top, "IDXI", [128, NTOK], F32)
    IDXJ = sb(top, "IDXJ", [128, NTOK], F32)
    GATE = sb(top, "GATE", [128, NTOK], F32)
    ident = sb(top, "ident", [128, 128], F32)
    identb = sb(top, "identb", [128, 128], BF16)
    iota_f = sb(top, "iota_f", [128, 128], F32)
    iota_b = sb(top, "iota_b", [128, 128], BF16)
    iota_p = sb(top, "iota_p", [128, 1], F32)
    ones_f = sb(top, "ones_f", [128, 128], F32)
    g_b = sb(top, "g_b", [128, D], F32)
    b_b = sb(top, "b_b", [128, D], F32)
    junk = sb(top, "junk", [128, D], BF16)
    st = sb(top, "st", [128, 16], F32)

    k.op("pool", lambda e: e.iota(iota_f[:], pattern=[[1, 128]], base=0, channel_multiplier=0,
                                  allow_small_or_imprecise_dtypes=True), writes=["iota_f"])
    k.op("pool", lambda e: e.iota(iota_p[:], pattern=[[0, 1]], base=0, channel_multiplier=1,
                                  allow_small_or_imprecise_dtypes=True), writes=["iota_p"])
    k.op("dve", lambda e: e.tensor_copy(out=iota_b[:], in_=iota_f[:]), reads=["iota_f"], writes=["iota_b"])
    k.op("dve", lambda e: e.tensor_scalar(out=ident[:], in0=iota_f[:], scalar1=iota_p[:, 0:1], scalar2=None,
                                          op0=ALU.is_equal), reads=["iota_f", "iota_p"], writes=["ident"])
    k.op("dve", lambda e: e.tensor_copy(out=identb[:], in_=ident[:]), reads=["ident"], writes=["identb"])
    k.op("dve", lambda e: e.memset(ones_f[:], 1.0), writes=["ones_f"])

    xT_v = xT_d.rearrange("(c p) w -> p c w", p=128)
    for c in range(DC):
        k.dma("pool", "xTload", xT_res[:, c, :], xT_v[:, c, :], writes=["xTall"])
    xkeys_all = ["xTall"]

    def xkey(tile):
        return "xT%d" % tile

    def load_ln(g_d, b_d, layer):
        k.dma("sp", "lnp", g_b[:], g_d[layer].partition_broadcast(128), writes=["g_b"])
        k.dma("sp", "lnp", b_b[:], b_d[layer].partition_broadcast(128), writes=["b_b"])

    def layer_norm_tm(r, rkey, o, okey):
        k.op("act", lambda e: e.activation(out=junk[:], in_=r, func=AF.Identity, accum_out=st[:, 0:1]),
             reads=[rkey], writes=["junk", "st0"])
        k.op("act", lambda e: e.activation(out=junk[:], in_=r, func=AF.Square, accum_out=st[:, 1:2]),
             reads=[rkey], writes=["junk", "st1"])
        k.op("dve", lambda e: e.tensor_scalar(out=st[:, 2:3], in0=st[:, 0:1], scalar1=1.0 / D, scalar2=None,
                                              op0=ALU.mult), reads=["st0"], writes=["st2"])
        k.op("dve", lambda e: e.tensor_tensor(out=st[:, 3:4], in0=st[:, 2:3], in1=st[:, 2:3], op=ALU.mult),
             reads=["st2"], writes=["st3"])
        k.op("dve", lambda e: e.scalar_tensor_tensor(out=st[:, 4:5], in0=st[:, 1:2], scalar=1.0 / D,
                                                     in1=st[:, 3:4], op0=ALU.mult, op1=ALU.subtract),
             reads=["st1", "st3"], writes=["st4"])
        k.op("dve", lambda e: e.tensor_scalar(out=st[:, 6:7], in0=st[:, 4:5], scalar1=EPS, scalar2=None,
                                              op0=ALU.add), reads=["st4"], writes=["st6"])
        k.op("act", lambda e: e.activation(out=st[:, 7:8], in_=st[:, 6:7], func=AF.Sqrt), reads=["st6"], writes=["st7"])
        k.op("dve", lambda e: e.reciprocal(out=st[:, 5:6], in_=st[:, 7:8]), reads=["st7"], writes=["st5"])
        k.op("dve", lambda e: e.tensor_scalar(out=o, in0=r, scalar1=st[:, 2:3], scalar2=st[:, 5:6],
                                              op0=ALU.subtract, op1=ALU.mult),
             reads=[rkey, "st2", "st5"], writes=[okey])
        k.op("dve", lambda e: e.tensor_tensor(out=o, in0=o, in1=g_b[:], op=ALU.mult),
             reads=[okey, "g_b"], writes=[okey])
        k.op("dve", lambda e: e.tensor_tensor(out=o, in0=o, in1=b_b[:], op=ALU.add),
             reads=[okey, "b_b"], writes=[okey])

    def to_feature_major(src, skey, tile, tp_ps, tpkey):
        col0 = 32 + tile * 128
        for half in range(2):
            pk = tpkey + str(half)
            for q in range(4):
                c = half * 4 + q
                k.op("pe", lambda e, c=c, q=q: e.transpose(out=tp_ps[half][:, q * 128:(q + 1) * 128],
                                                           in_=src[:, c * 128:(c + 1) * 128], identity=ident[:]),
                     reads=[skey, "ident"], writes=[pk], sig=(q == 3))
            eng = "act" if half == 0 else "dve"
            if eng == "act":
                k.op("act", lambda e: e.copy(out=xT_res[:, half * 4:(half + 1) * 4, col0:col0 + 128],
                                             in_=tp_ps[half][:].rearrange("p (a b) -> p a b", a=4)),
                     reads=[pk], writes=[xkey(tile)] + xkeys_all)
            else:
                k.op("dve", lambda e: e.tensor_copy(out=xT_res[:, half * 4:(half + 1) * 4, col0:col0 + 128],
                                                    in_=tp_ps[half][:].rearrange("p (a b) -> p a b", a=4)),
                     reads=[pk], writes=[xkey(tile)] + xkeys_all)

    def phase_w():
        with contextlib.ExitStack() as ph:
            NB = 3
            su = [sb(ph, "su%d" % i, [128, 2, D], F32) for i in range(NB)]
            sv_ = [sb(ph, "sv%d" % i, [128, 2, D], F32) for i in range(NB)]
            bu = [sb(ph, "bu%d" % i, [128, 2, D], BF16) for i in range(NB)]
            bv = [sb(ph, "bv%d" % i, [128, 2, D], BF16) for i in range(NB)]
            cnt = 0
            for layer in range(2):
                V_v = V_d[layer].rearrange("(i j) d -> j i d", j=NJ)
                for step in range(NJ // 2):
                    b = cnt % NB
                    cnt += 1
                    j0 = 2 * step
                    k.dma("sp", "pw_in%d" % b, su[b][:], UT_d[layer, j0:j0 + 2].rearrange("j p c i -> p j (c i)"),
                          writes=["su%d" % b])
                    k.dma("sp", "pw_in%d" % b, sv_[b][:], V_v[j0:j0 + 2].rearrange("j i d -> i j d"),
                          writes=["sv%d" % b])
                    ue = "dve" if step % 2 == 0 else "pool"
                    k.op(ue, lambda e: e.tensor_copy(out=bu[b][:], in_=su[b][:]), reads=["su%d" % b], writes=["bu%d" % b])
                    k.op("act", lambda e: e.copy(out=bv[b][:], in_=sv_[b][:]), reads=["sv%d" % b], writes=["bv%d" % b])
                    k.dma("sp", "pw_out%d" % b, UTb_d[layer, j0:j0 + 2].rearrange("j p x -> p j x"), bu[b][:],
                          reads=["bu%d" % b])
                    k.dma("sp", "pw_out%d" % b, Vb_d[layer, j0:j0 + 2].rearrange("j p x -> p j x"), bv[b][:],
                          reads=["bv%d" % b])
        k.barrier()

    def table_steps(layer, ph):
        NB = 3
        su = [sb(ph, "tsu%d" % i, [128, D], F32) for i in range(NB)]
        sv_ = [sb(ph, "tsv%d" % i, [128, D], F32) for i in range(NB)]
        bu = [sb(ph, "tbu%d" % i, [128, D], BF16) for i in range(NB)]
        bv = [sb(ph, "tbv%d" % i, [128, D], BF16) for i in range(NB)]
        V_v = V_d[layer].rearrange("(i j) d -> j i d", j=NJ)

        def load(j):
            b = j % NB
            k.dma("sp", "tw_in%d" % b, su[b][:], UT_d[layer, j].rearrange("p c i -> p (c i)"), writes=["tsu%d" % b])
            k.dma("sp", "tw_in%d" % b, sv_[b][:], V_v[j], writes=["tsv%d" % b])

        load(0)
        load(1)
        for j in range(NJ):
            b = j % NB
            if j + 2 < NJ:
                load(j + 2)
            k.op("act", lambda e: e.copy(out=bu[b][:], in_=su[b][:]), reads=["tsu%d" % b], writes=["tbu%d" % b])
            k.op("act", lambda e: e.copy(out=bv[b][:], in_=sv_[b][:]), reads=["tsv%d" % b], writes=["tbv%d" % b])
            k.dma("pool", "tw_out%d" % b, UTb_d[layer, j], bu[b][:], reads=["tbu%d" % b])
            k.dma("pool", "tw_out%d" % b, Vb_d[layer, j], bv[b][:], reads=["tbv%d" % b])
            yield j

    def phase_m0():
        with contextlib.ExitStack() as ph:
            xtile = sb(ph, "xtile", [128, 1, D], F32)
            rtile = sb(ph, "rtile", [128, 1, D], F32)
            otile = sb(ph, "otile", [128, 1, D], F32)
            wi = sb(ph, "wi", [128, DC, 2 * D], BF16)
            wo = sb(ph, "wo", [128, DC, D], BF16)
            hch = sb(ph, "hch", [128, 8, 544], BF16)
            dg = sb(ph, "dg", [128, 2, 31, 128], BF16)
            acc = sb(ph, "acc", [128, 8, 512], F32)
            hs = sb(ph, "hs", [128, 8, 512], BF16)
            sq = [sb(ph, "sq%d" % i, [128, 512], F32) for i in range(2)]
            sig = [sb(ph, "sig%d" % i, [128, 512], F32) for i in range(2)]
            gtmp = [sb(ph, "gtmp%d" % i, [128, 512], F32) for i in range(2)]
            mean_t = sb(ph, "mean_t", [128, 512], F32)
            msq_t = sb(ph, "msq_t", [128, 512], F32)
            rstd_t = sb(ph, "rstd_t", [128, 512], F32)
            hm_b = sb(ph, "hm_b", [128, XW], BF16)
            bout_b = sb(ph, "bout_b", [128, D], F32)
            b_in = sb(ph, "b_in", [128, 16], F32)
            dw = sb(ph, "dw", [128, 8, 31], F32)
            dwb = sb(ph, "dwb", [128, 8], F32)
            clng = sb(ph, "clng", [128, 8], F32)
            clnb = sb(ph, "clnb", [128, 8], F32)
            ag = [ps(ph, "ag%d" % i, [128, 512]) for i in range(4)]
            s1 = ps(ph, "s1", [128, 512])
            s2 = ps(ph, "s2", [128, 512])
            mix = ps(ph, "mix", [128, D])

            wi_v = w_in_d.rearrange("(c p) n -> p c n", p=128)
            wo_v = w_out_d.rearrange("(c p) n -> p c n", p=128)
            for c in range(DC):
                k.dma("pool", "wload", wi[:, c, :], wi_v[:, c, :], writes=["wi"])
            for c in range(DC):
                k.dma("pool", "wload", wo[:, c, :], wo_v[:, c, :], writes=["wo"])
            k.dma("pool", "wload", hm_b[:], hm_d.partition_broadcast(128), writes=["hm_b"])
            k.dma("sp", "cload", bout_b[:], b_out_d.partition_broadcast(128), writes=["bout_b"])
            k.dma("sp", "cload", b_in[:], b_in_d[:, :], writes=["cst"])
            k.dma("sp", "cload", dw[:], dw_d[:, :, :], writes=["cst"])
            k.dma("sp", "cload", dwb[:], dwb_d[:, :], writes=["cst"])
            k.dma("sp", "cload", clng[:], clng_d[:, :], writes=["cst"])
            k.dma("sp", "cload", clnb[:], clnb_d[:, :], writes=["cst"])
            load_ln(lnmg_d, lnmb_d, 0)

            def glu(col0, n, dcol0, rkeys):
                for fc in range(8):
                    b = fc % 2
                    a_ps, g_ps = ag[2 * b], ag[2 * b + 1]
                    for (pt, colsel, pk) in ((a_ps, fc, "ag%d" % (2 * b)), (g_ps, fc + 8, "ag%d" % (2 * b + 1))):
                        for dc in range(DC):
                            k.op("pe", lambda e, pt=pt, colsel=colsel, dc=dc: e.matmul(
                                pt[:, :n], lhsT=wi[:, dc, colsel * 128:(colsel + 1) * 128],
                                rhs=xT_res[:, dc, col0:col0 + n], start=(dc == 0), stop=(dc == DC - 1)),
                                reads=["wi"] + rkeys, writes=[pk], sig=(dc == DC - 1))
                    k.op("act", lambda e: e.activation(out=sig[b][:, :n], in_=g_ps[:, :n], func=AF.Sigmoid,
                                                       bias=b_in[:, fc + 8:fc + 9], scale=1.0),
                         reads=["ag%d" % (2 * b + 1), "cst"], writes=["sig%d" % b])
                    k.op("dve", lambda e: e.scalar_tensor_tensor(out=gtmp[b][:, :n], in0=a_ps[:, :n],
                                                                 scalar=b_in[:, fc:fc + 1], in1=sig[b][:, :n],
                                                                 op0=ALU.add, op1=ALU.mult),
                         reads=["ag%d" % (2 * b), "sig%d" % b, "cst"], writes=["gtmp%d" % b])
                    k.op("dve", lambda e: e.tensor_tensor(out=hch[:, fc, dcol0:dcol0 + n], in0=gtmp[b][:, :n],
                                                          in1=hm_b[:, col0:col0 + n], op=ALU.mult),
                         reads=["gtmp%d" % b, "hm_b"], writes=["hch"])

            glu(0, 32, 0, ["xTall"])
            tok0 = 0
            while tok0 < NTOK:
                n = min(512, NTOK - tok0)
                tiles = list(range(tok0 // 128, (tok0 + n) // 128))
                glu(32 + tok0, n, 32, ["xTall"] + [xkey(t) for t in tiles])
                for fc in range(8):
                    b = fc % 2
                    ak = "acc%d" % fc
                    for kk in range(31):
                        if kk % 3 == 0:
                            k.op("act", lambda e, kk=kk: e.activation(out=dg[:, b, kk, :], in_=identb[:], func=AF.Copy,
                                                                      scale=dw[:, fc, kk:kk + 1]),
                                 reads=["identb", "cst"], writes=["dgA%d" % b])
                        else:
                            k.op("dve", lambda e, kk=kk: e.tensor_scalar(out=dg[:, b, kk, :], in0=identb[:],
                                                                         scalar1=dw[:, fc, kk:kk + 1], scalar2=None,
                                                                         op0=ALU.mult),
                                 reads=["identb", "cst"], writes=["dgD%d" % b])
                    for kk in range(31):
                        k.op("pe", lambda e, kk=kk: e.matmul(ag[2 + b][:, :n], lhsT=dg[:, b, kk, :],
                                                             rhs=hch[:, fc, 2 + kk:2 + kk + n],
                                                             start=(kk == 0), stop=(kk == 30)),
                             reads=["dgA%d" % b, "dgD%d" % b, "hch"], writes=["ag%d" % (2 + b)], sig=(kk == 30))
                    k.op("act", lambda e: e.activation(out=acc[:, fc, :n], in_=ag[2 + b][:, :n], func=AF.Identity,
                                                       bias=dwb[:, fc:fc + 1], scale=1.0),
                         reads=["ag%d" % (2 + b), "cst"], writes=[ak])
                if tok0 + n < NTOK:
                    k.op("dve", lambda e: e.tensor_copy(out=hch[:, :, 0:32], in_=hch[:, :, n:n + 32]),
                         reads=["hch"] + ["acc%d" % f for f in range(8)], writes=["hch"])
                for fc in range(8):
                    b = fc % 2
                    k.op("act", lambda e: e.activation(out=sq[b][:, :n], in_=acc[:, fc, :n], func=AF.Square),
                         reads=["acc%d" % fc], writes=["sq%d" % b])
                    k.op("pe", lambda e: e.matmul(s1[:, :n], lhsT=ones_f[:], rhs=acc[:, fc, :n],
                                                  start=(fc == 0), stop=(fc == 7)),
                         reads=["acc%d" % fc, "ones_f"], writes=["s1"])
                    k.op("pe", lambda e: e.matmul(s2[:, :n], lhsT=ones_f[:], rhs=sq[b][:, :n],
                                                  start=(fc == 0), stop=(fc == 7)),
                         reads=["sq%d" % b, "ones_f"], writes=["s2"])
                k.op("dve", lambda e: e.tensor_scalar(out=mean_t[:, :n], in0=s1[:, :n], scalar1=1.0 / D, scalar2=None,
                                                      op0=ALU.mult), reads=["s1"], writes=["mean_t"])
                k.op("dve", lambda e: e.tensor_tensor(out=msq_t[:, :n], in0=mean_t[:, :n], in1=mean_t[:, :n],
                                                      op=ALU.mult), reads=["mean_t"], writes=["msq_t"])
                k.op("dve", lambda e: e.scalar_tensor_tensor(out=rstd_t[:, :n], in0=s2[:, :n], scalar=1.0 / D,
                                                             in1=msq_t[:, :n], op0=ALU.mult, op1=ALU.subtract),
                     reads=["s2", "msq_t"], writes=["rstd_t"])
                k.op("dve", lambda e: e.tensor_scalar(out=rstd_t[:, :n], in0=rstd_t[:, :n], scalar1=EPS, scalar2=None,
                                                      op0=ALU.add), reads=["rstd_t"], writes=["rstd_t"])
                k.op("act", lambda e: e.activation(out=rstd_t[:, :n], in_=rstd_t[:, :n], func=AF.Sqrt),
                     reads=["rstd_t"], writes=["rstd_t"])
                k.op("dve", lambda e: e.reciprocal(out=rstd_t[:, :n], in_=rstd_t[:, :n]),
                     reads=["rstd_t"], writes=["rstd_t"])
                for m in range(4):
                    pr = (2 * m, 2 * m + 1)
                    for fc in pr:
                        k.op("dve", lambda e, fc=fc: e.tensor_tensor(out=acc[:, fc, :n], in0=acc[:, fc, :n], in1=mean_t[:, :n],
                                                                     op=ALU.subtract),
                             reads=["acc%d" % fc, "mean_t"], writes=["acc%d" % fc])
                    for fc in pr:
                        k.op("dve", lambda e, fc=fc: e.tensor_tensor(out=acc[:, fc, :n], in0=acc[:, fc, :n], in1=rstd_t[:, :n],
                                                                     op=ALU.mult),
                             reads=["acc%d" % fc, "rstd_t"], writes=["acc%d" % fc])
                    for fc in pr:
                        k.op("act", lambda e, fc=fc: e.activation(out=hs[:, fc, :n], in_=acc[:, fc, :n], func=AF.Silu,
                                                                  bias=clnb[:, fc:fc + 1], scale=clng[:, fc:fc + 1]),
                             reads=["acc%d" % fc, "cst"], writes=["hs"])
                for ti, tile in enumerate(tiles):
                    for half in range(2):
                        for fc in range(8):
                            k.op("pe", lambda e, fc=fc: e.matmul(
                                mix[:, half * 512:(half + 1) * 512], lhsT=hs[:, fc, ti * 128:(ti + 1) * 128],
                                rhs=wo[:, fc, half * 512:(half + 1) * 512], start=(fc == 0), stop=(fc == 7)),
                                reads=["hs", "wo"], writes=["mix"], sig=(fc == 7))
                    k.dma("sp", "xt", xtile[:, 0, :], xtm_d[tile * 128:(tile + 1) * 128, :], writes=["xtile"])
                    r = rtile[:, 0, :]
                    k.op("dve", lambda e: e.scalar_tensor_tensor(out=r, in0=xtile[:, 0, :], scalar=ALPHA, in1=mix[:],
                                                                 op0=ALU.mult, op1=ALU.add),
                         reads=["xtile", "mix"], writes=["rtile"])
                    k.op("dve", lambda e: e.tensor_tensor(out=r, in0=r, in1=bout_b[:], op=ALU.add),
                         reads=["rtile", "bout_b"], writes=["rtile"])
                    o = otile[:, 0, :]
                    layer_norm_tm(r, "rtile", o, "otile")
                    k.dma("sp", "xst", xs_d[0][tile * 128:(tile + 1) * 128, :], o, reads=["otile"])
                    to_feature_major(o, "otile", tile, [ag[0], ag[1]], "ag")
                tok0 += n
        k.barrier()

    def phase_s(layer):
        with contextlib.ExitStack() as ph:
            wq = sb(ph, "wq", [128, DC, 2 * D], BF16)
            keysT = sb(ph, "keysT", [128, 2, 128], F32)
            qT = sb(ph, "qT", [128, 16, 256], F32)
            scA = sb(ph, "scA", [128, 2048], F32)
            scB = sb(ph, "scB", [128, 2048], F32)

            class _V:
                def __init__(self, t, a):
                    self.t, self.a = t, a

                def __getitem__(self, idx):
                    return self.t[:].rearrange("p (a b) -> p a b", a=self.a)[idx]
            sc = _V(scA, 16)
            scr = _V(scB, 16)
            sv = sb(ph, "sv", [128, 16, 16], F32)
            si = sb(ph, "si", [128, 16, 16], U32)
            sif = sb(ph, "sif", [128, 16, 16], F32)
            cand = _V(scA, 8)
            cscr = _V(scB, 8)
            best = sb(ph, "best", [128, 8, 16], F32)
            posu = sb(ph, "posu", [128, 8, 16], U32)
            k1u = sb(ph, "k1u", [128, 8, 16], U32)
            k2u = sb(ph, "k2u", [128, 8, 16], U32)
            k1f = sb(ph, "k1f", [128, 8, 16], F32)
            k2f = sb(ph, "k2f", [128, 8, 16], F32)
            class _V4:
                def __getitem__(self, idx):
                    return scB[:].rearrange("p (h a b) -> p h a b", h=8, a=16)[idx]
            E1 = _V4()
            E2 = E1
            idi = sb(ph, "idi", [128, 128], F32)
            idj = sb(ph, "idj", [128, 128], F32)
            gt = sb(ph, "gt", [128, 8, 16], F32)
            zs = sb(ph, "zs", [128, 8], F32)
            q_ps = [ps(ph, "q_ps%d" % i, [128, 512]) for i in range(2)]
            sc_ps = ps(ph, "sc_ps", [128, 2048])
            tp = ps(ph, "tp", [128, 512])

            wq_v = pwq_d[layer].rearrange("(c p) n -> p c n", p=128)
            for c in range(DC):
                k.dma("pool", "wload", wq[:, c, :], wq_v[:, c, :], writes=["wq"])
            k.dma("sp", "cload", keysT[:], keysT_d[layer], writes=["keysT"])
            tsteps = table_steps(0, ph) if (layer == 0 and want_tables0) else iter(())
            per_tile = (NJ + NTL - 1) // NTL

            tok0 = 128 if layer == 1 else 0
            while tok0 < NTOK:
                n = min(256, NTOK - tok0)
                tiles = list(range(tok0 // 128, (tok0 + n) // 128))
                col0 = 32 + tok0
                for hp in range(16):
                    b = hp % 2
                    for dc in range(DC):
                        k.op("pe", lambda e, dc=dc: e.matmul(q_ps[b][:, :n], lhsT=wq[:, dc, hp * 128:(hp + 1) * 128],
                                                             rhs=xT_res[:, dc, col0:col0 + n],
                                                             start=(dc == 0), stop=(dc == DC - 1)),
                             reads=["wq", "xTall"] + [xkey(t) for t in tiles], writes=["q_ps%d" % b],
                             sig=(dc == DC - 1))
                    if b == 0:
                        k.op("act", lambda e: e.copy(out=qT[:, hp, :n], in_=q_ps[b][:, :n]),
                             reads=["q_ps%d" % b], writes=["qT"])
                    else:
                        k.op("dve", lambda e: e.tensor_copy(out=qT[:, hp, :n], in_=q_ps[b][:, :n]),
                             reads=["q_ps%d" % b], writes=["qT"])
                for ti, tile in enumerate(tiles):
                    for hp in range(16):
                        k.op("pe", lambda e: e.matmul(sc_ps[:, hp * 128:(hp + 1) * 128],
                                                      lhsT=qT[:, hp, ti * 128:(ti + 1) * 128],
                                                      rhs=keysT[:, hp % 2, :], start=True, stop=True),
                             reads=["qT", "keysT"], writes=["sc_ps"], sig=(hp == 15))
                    k.op("act", lambda e: e.copy(out=sc[:].rearrange("p a b -> p (a b)"), in_=sc_ps[:]),
                         reads=["sc_ps"], writes=["sc"])
                    SCRK = ["scr%d" % i for i in range(16)]
                    for m in range(8):
                        for _ in range((per_tile + 7) // 8):
                            next(tsteps, None)
                        pr = (2 * m, 2 * m + 1)
                        for hp in pr:
                            k.op("dve", lambda e, hp=hp: e.max(out=sv[:, hp, 0:8], in_=sc[:, hp, :]),
                                 reads=["sc"], writes=["svA%d" % hp])
                        for hp in pr:
                            k.op("dve", lambda e, hp=hp: e.max_index(out=si[:, hp, 0:8], in_max=sv[:, hp, 0:8],
                                                                     in_values=sc[:, hp, :]),
                                 reads=["sc", "svA%d" % hp], writes=["siA%d" % hp])
                        for hp in pr:
                            k.op("dve", lambda e, hp=hp: e.match_replace(out=scr[:, hp, :], in_to_replace=sv[:, hp, 0:8],
                                                                         in_values=sc[:, hp, :], imm_value=-1e30),
                                 reads=["sc", "svA%d" % hp], writes=["scr%d" % hp])
                        for hp in pr:
                            k.op("dve", lambda e, hp=hp: e.max(out=sv[:, hp, 8:16], in_=scr[:, hp, :]),
                                 reads=["scr%d" % hp], writes=["svB%d" % hp])
                        for hp in pr:
                            k.op("dve", lambda e, hp=hp: e.max_index(out=si[:, hp, 8:16], in_max=sv[:, hp, 8:16],
                                                                     in_values=scr[:, hp, :]),
                                 reads=["scr%d" % hp, "svB%d" % hp], writes=["siB%d" % hp])
                    SVK = ["svA%d" % i for i in range(16)] + ["svB%d" % i for i in range(16)]
                    SIK = ["siA%d" % i for i in range(16)] + ["siB%d" % i for i in range(16)]
                    k.op("dve", lambda e: e.tensor_copy(out=sif[:], in_=si[:]), reads=SIK, writes=["sif"])
                    sv4 = sv[:].rearrange("p (h s) k -> p h s k", s=2)
                    sif4 = sif[:].rearrange("p (h s) k -> p h s k", s=2)
                    k.op("dve", lambda e: e.tensor_tensor(
                        out=cand[:].rearrange("p h (a b) -> p h a b", a=16),
                        in0=sv4[:, :, 0, :].unsqueeze(3).to_broadcast([128, 8, 16, 16]),
                        in1=sv4[:, :, 1, :].unsqueeze(2).to_broadcast([128, 8, 16, 16]), op=ALU.add),
                        reads=SVK, writes=["sc"])
                    for m in range(4):
                        pr = (2 * m, 2 * m + 1)
                        ck = lambda h: ["scr%d" % (2 * h), "scr%d" % (2 * h + 1)]
                        for h in pr:
                            k.op("dve", lambda e, h=h: e.max(out=best[:, h, 0:8], in_=cand[:, h, :]),
                                 reads=["sc"], writes=["bA%d" % h])
                        for h in pr:
                            k.op("dve", lambda e, h=h: e.max_index(out=posu[:, h, 0:8], in_max=best[:, h, 0:8],
                                                                   in_values=cand[:, h, :]),
                                 reads=["sc", "bA%d" % h], writes=["pA%d" % h])
                        for h in pr:
                            k.op("dve", lambda e, h=h: e.match_replace(out=cscr[:, h, :], in_to_replace=best[:, h, 0:8],
                                                                       in_values=cand[:, h, :], imm_value=-1e30),
                                 reads=["sc", "bA%d" % h], writes=ck(h))
                        for h in pr:
                            k.op("dve", lambda e, h=h: e.max(out=best[:, h, 8:16], in_=cscr[:, h, :]),
                                 reads=ck(h), writes=["bB%d" % h])
                        for h in pr:
                            k.op("dve", lambda e, h=h: e.max_index(out=posu[:, h, 8:16], in_max=best[:, h, 8:16],
                                                                   in_values=cscr[:, h, :]),
                                 reads=ck(h) + ["bB%d" % h], writes=["pB%d" % h])
                    BK = ["bA%d" % i for i in range(8)] + ["bB%d" % i for i in range(8)]
                    PK = ["pA%d" % i for i in range(8)] + ["pB%d" % i for i in range(8)]
                    k.op("dve", lambda e: e.tensor_single_scalar(out=k1u[:], in_=posu[:], scalar=4,
                                                                 op=ALU.logical_shift_right), reads=PK, writes=["k1u"])
                    k.op("dve", lambda e: e.tensor_single_scalar(out=k2u[:], in_=posu[:], scalar=15,
                                                                 op=ALU.bitwise_and), reads=PK, writes=["k2u"])
                    k.op("dve", lambda e: e.tensor_copy(out=k1f[:], in_=k1u[:]), reads=["k1u"], writes=["k1f"])
                    k.op("dve", lambda e: e.tensor_copy(out=k2f[:], in_=k2u[:]), reads=["k2u"], writes=["k2f"])
                    io16 = iota_f[:, 0:16].unsqueeze(1).unsqueeze(1).to_broadcast([128, 8, 16, 16])
                    for (kf, kfk, Et, Ek, side, dst, dk) in ((k1f, "k1f", E1, SCRK, 0, idi, "idi"),
                                                           (k2f, "k2f", E2, SCRK, 1, idj, "idj")):
                        k.op("dve", lambda e: e.tensor_tensor(out=Et[:], in0=kf[:].unsqueeze(3).to_broadcast([128, 8, 16, 16]),
                                                              in1=io16, op=ALU.is_equal),
                             reads=[kfk, "iota_f"], writes=Ek)
                        k.op("dve", lambda e: e.tensor_tensor(
                            out=Et[:], in0=Et[:], in1=sif4[:, :, side, :].unsqueeze(2).to_broadcast([128, 8, 16, 16]),
                            op=ALU.mult), reads=Ek + ["sif"], writes=Ek)
                        k.op("dve", lambda e: e.tensor_reduce(out=dst[:].rearrange("p (h k) -> p h k", h=8), in_=Et[:],
                                                              axis=AX.X, op=ALU.add), reads=Ek, writes=[dk])
                    k.op("dve", lambda e: e.tensor_tensor(out=gt[:], in0=best[:],
                                                          in1=best[:, :, 0:1].to_broadcast([128, 8, 16]), op=ALU.subtract),
                         reads=BK, writes=["gt"])
                    k.op("act", lambda e: e.activation(out=gt[:], in_=gt[:], func=AF.Exp), reads=["gt"], writes=["gt"])
                    k.op("dve", lambda e: e.tensor_reduce(out=zs[:], in_=gt[:], axis=AX.X, op=ALU.add),
                         reads=["gt"], writes=["zs"])
                    k.op("dve", lambda e: e.reciprocal(out=zs[:], in_=zs[:]), reads=["zs"], writes=["zs"])
                    k.op("dve", lambda e: e.tensor_tensor(out=gt[:], in0=gt[:],
                                                          in1=zs[:].unsqueeze(2).to_broadcast([128, 8, 16]), op=ALU.mult),
                         reads=["gt", "zs"], writes=["gt"])
                    srcs = ((idi[:], "idi", IDXI, "IDXI"), (idj[:], "idj", IDXJ, "IDXJ"),
                            (gt[:].rearrange("p h k -> p (h k)"), "gt", GATE, "GATE"))
                    for qi, (s_ap, s_k, dst, d_k) in enumerate(srcs):
                        k.op("pe", lambda e: e.transpose(out=tp[:, qi * 128:(qi + 1) * 128], in_=s_ap, identity=ident[:]),
                             reads=[s_k, "ident"], writes=["tp"])
                    for qi, (s_ap, s_k, dst, d_k) in enumerate(srcs):
                        k.op("act", lambda e: e.copy(out=dst[:, tile * 128:(tile + 1) * 128],
                                                     in_=tp[:, qi * 128:(qi + 1) * 128]),
                             reads=["tp"], writes=[d_k])
                tok0 += n
            for _ in tsteps:
                pass
        k.barrier()

    def phase_d(layer, xin_d, xout_d, final):
        with contextlib.ExitStack() as ph:
            xtile = sb(ph, "xtile", [128, 2, D], F32)
            rtile = sb(ph, "rtile", [128, 2, D], F32)
            otile = sb(ph, "otile", [128, 2, D], F32)
            G = sb(ph, "G", [128, 256, 128], BF16)
            ub = [sb(ph, "ub%d" % i, [128, DC, 128], BF16) for i in range(NSLOT)]
            vb = [sb(ph, "vb%d" % i, [128, D], BF16) for i in range(NSLOT)]
            TB = 16
            Pb = [sb(ph, "Pb%d" % i, [128, TB, 128], BF16) for i in range(2)]
            Qb = [sb(ph, "Qb%d" % i, [128, TB, 128], BF16) for i in range(2)]
            gl = [sb(ph, "gl%d" % i, [128, 256], F32) for i in range(3)]
            ab = [sb(ph, "ab%d" % i, [128, 256], BF16) for i in range(3)]
            out_ps = [ps(ph, "out_ps%d" % i, [128, D]) for i in range(2)]
            h_ps = [ps(ph, "h_ps%d" % i, [128, 512]) for i in range(2)]
            g_ps = [ps(ph, "g_ps%d" % i, [128, 512]) for i in range(2)]
            hbuf = [(h_ps[0], "h_ps0"), (h_ps[1], "h_ps1"), (g_ps[0], "g_ps0"), (g_ps[1], "g_ps1")]
            load_ln(lnfg_d, lnfb_d, layer)

            tile0 = 1 if final else 0
            while tile0 < NTL:
                nsub = min(2, NTL - tile0)
                T = nsub * 128
                tiles = list(range(tile0, tile0 + nsub))
                t0 = tile0 * 128
                col0 = 32 + t0
                xk = ["xTall"] + [xkey(t) for t in tiles]
                io_bc = iota_b[:].unsqueeze(1).to_broadcast([128, TB, 128])
                for tb in range(T // TB):
                    b = tb % 2
                    ta = t0 + tb * TB
                    k.op("dve", lambda e: e.tensor_tensor(
                        out=Pb[b][:], in0=io_bc, in1=IDXI[:, ta:ta + TB].unsqueeze(2).to_broadcast([128, TB, 128]),
                        op=ALU.is_equal), reads=["iota_b", "IDXI"], writes=["Pb%d" % b])
                    k.op("dve", lambda e: e.tensor_tensor(
                        out=Qb[b][:], in0=io_bc, in1=IDXJ[:, ta:ta + TB].unsqueeze(2).to_broadcast([128, TB, 128]),
                        op=ALU.is_equal), reads=["iota_b", "IDXJ"], writes=["Qb%d" % b])
                    k.op("dve", lambda e: e.tensor_tensor(
                        out=Pb[b][:], in0=Pb[b][:], in1=GATE[:, ta:ta + TB].unsqueeze(2).to_broadcast([128, TB, 128]),
                        op=ALU.mult), reads=["Pb%d" % b, "GATE"], writes=["Pb%d" % b])
                    for q4 in range(TB // 4):
                        gb = (tb * (TB // 4) + q4) % 2
                        for u in range(4):
                            uu = q4 * 4 + u
                            k.op("pe", lambda e, u=u, uu=uu: e.matmul(
                                g_ps[gb][:, u * 128:(u + 1) * 128], lhsT=Pb[b][:, uu, :], rhs=Qb[b][:, uu, :],
                                start=True, stop=True),
                                reads=["Pb%d" % b, "Qb%d" % b], writes=["g_ps%d" % gb], sig=(u == 3))
                        tq = tb * TB + q4 * 4
                        k.op("act", lambda e: e.copy(out=G[:, tq:tq + 4, :],
                                                     in_=g_ps[gb][:].rearrange("p (a b) -> p a b", a=4)),
                             reads=["g_ps%d" % gb], writes=["G"])
                def mm1(j):
                    s_ = j % NSLOT
                    hp_, hk = hbuf[j % 4]
                    k.dma("sp", "wu%d" % s_, ub[s_][:].rearrange("p c i -> p (c i)"), UTb_d[layer, j], writes=["wslotU%d" % s_])
                    k.dma("sp", "wv%d" % s_, vb[s_][:], Vb_d[layer, j], writes=["wslotV%d" % s_])
                    for dc in range(DC):
                        k.op("pe", lambda e, dc=dc: e.matmul(hp_[:, :T], lhsT=ub[s_][:, dc, :],
                                                             rhs=xT_res[:, dc, col0:col0 + T],
                                                             start=(dc == 0), stop=(dc == DC - 1)),
                             reads=["wslotU%d" % s_] + xk, writes=[hk], sig=(dc == DC - 1))

                def mm2(j):
                    s_ = j % NSLOT
                    b = j % 3
                    hp_, hk = hbuf[j % 4]
                    k.op("act", lambda e: e.activation(out=gl[b][:, :T], in_=hp_[:, :T], func=AF.Gelu),
                         reads=[hk], writes=["gl%d" % b])
                    k.op("dve", lambda e: e.tensor_tensor(out=ab[b][:, :T], in0=gl[b][:, :T], in1=G[:, 0:T, j],
                                                          op=ALU.mult), reads=["gl%d" % b, "G"], writes=["ab%d" % b])
                    for ts in range(nsub):
                        for dh in range(2):
                            k.op("pe", lambda e, ts=ts, dh=dh: e.matmul(
                                out_ps[ts][:, dh * 512:(dh + 1) * 512], lhsT=ab[b][:, ts * 128:(ts + 1) * 128],
                                rhs=vb[s_][:, dh * 512:(dh + 1) * 512], start=(j == 0), stop=(j == NJ - 1)),
                                reads=["ab%d" % b, "wslotV%d" % s_], writes=["out_ps"],
                                sig=(ts == nsub - 1 and dh == 1))

                for ts, tile in enumerate(tiles):
                    k.dma("sp", "xt", xtile[:, ts, :], xin_d[tile * 128:(tile + 1) * 128, :], writes=["xtile%d" % ts])
                mm1(0)
                mm1(1)
                for j in range(NJ):
                    if j + 2 < NJ:
                        mm1(j + 2)
                    mm2(j)
                for ts, tile in enumerate(tiles):
                    r = rtile[:, ts, :]
                    k.op("dve", lambda e: e.scalar_tensor_tensor(out=r, in0=xtile[:, ts, :], scalar=ALPHA,
                                                                 in1=out_ps[ts][:], op0=ALU.mult, op1=ALU.add),
                         reads=["xtile%d" % ts, "out_ps"], writes=["rtile%d" % ts])
                    o = otile[:, ts, :]
                    layer_norm_tm(r, "rtile%d" % ts, o, "otile%d" % ts)
                    if final:
                        if tile >= 1:
                            k.dma("sp", "ost", out_d[(tile - 1) * 128:tile * 128, :], o, reads=["otile%d" % ts])
                    else:
                        k.dma("sp", "xst", xout_d[tile * 128:(tile + 1) * 128, :], o, reads=["otile%d" % ts])
                        to_feature_major(o, "otile%d" % ts, tile, [g_ps[0], g_ps[1]], "g_ps")
                tile0 += nsub
        k.barrier()

    def phase_m1(xin_d, xout_d):
        with contextlib.ExitStack() as ph:
            xtile = sb(ph, "xtile", [128, 1, D], F32)
            rtile = sb(ph, "rtile", [128, 1, D], F32)
            otile = sb(ph, "otile", [128, 1, D], F32)
            wq = sb(ph, "awq", [128, DC, D], BF16)
            wo = sb(ph, "awo", [128, DC, D], BF16)
            KT = sb(ph, "KT", [128, 4, NTOK], BF16)
            VS = sb(ph, "VS", [128, NTL, 256], BF16)
            qt = sb(ph, "qt", [128, D], F32)
            qTs = sb(ph, "qTs", [128, 8, 128], BF16)
            rt = [sb(ph, "rt%d" % i, [128, 16, 8], F32) for i in range(4)]
            posi = sb(ph, "posi", [128, NTL], I32)
            posf = sb(ph, "posf", [128, NTL], F32)
            invf = sb(ph, "invf", [128, 8], F32)
            ang = sb(ph, "ang", [128, NTL, 8], F32)
            sin_t = sb(ph, "sin_t", [128, NTL, 8], F32)
            cos_t = sb(ph, "cos_t", [128, NTL, 8], F32)
            amask = sb(ph, "amask", [128, 2, 256], F32)
            amask_b = sb(ph, "amask_b", [128, 2, 256], BF16)
            sink_b = sb(ph, "sink_b", [128, 16], F32)
            nsink_b = sb(ph, "nsink_b", [128, 16], F32)
            sm = [sb(ph, "sm%d" % i, [128, 256], F32) for i in range(2)]
            pb = [sb(ph, "pb%d" % i, [128, 256], BF16) for i in range(2)]
            pT = [sb(ph, "pT%d" % i, [128, 2, 128], BF16) for i in range(2)]
            sst = [sb(ph, "sst%d" % i, [128, 8], F32) for i in range(2)]
            osb = sb(ph, "osb", [128, D], F32)
            rden = sb(ph, "rden", [128, 16], F32)
            oT = sb(ph, "oT", [128, 8, 128], BF16)
            pj = [ps(ph, "pj%d" % i, [128, 512]) for i in range(2)]
            s_ps = [ps(ph, "s_ps%d" % i, [128, 512]) for i in range(2)]
            pT_ps = [ps(ph, "pT_ps%d" % i, [128, 1024], BF16) for i in range(2)]
            o_ps = ps(ph, "o_ps", [128, D])

            for (wt, wd, key, ncols) in ((wq, awq_d, "awq", D), (wo, awo_d, "awo", D)):
                wv = wd.rearrange("(c p) n -> p c n", p=128)
                for c in range(DC):
                    k.dma("pool", "wload", wt[:, c, :], wv[:, c, :], writes=[key])
            k.dma("sp", "cload", posi[:], pos_d[:, :], writes=["posi"])
            k.dma("sp", "cload", invf[:], invf_d.partition_broadcast(128), writes=["invf"])
            k.dma("sp", "cload", amask[:], amask_d.rearrange("a p k -> p a k"), writes=["amask"])
            k.dma("sp", "cload", sink_b[:], sinks_d.partition_broadcast(128), writes=["sink_b"])
            load_ln(lnmg_d, lnmb_d, 1)
            k.op("dve", lambda e: e.tensor_scalar(out=nsink_b[:], in0=sink_b[:], scalar1=-1.0, scalar2=None, op0=ALU.mult),
                 reads=["sink_b"], writes=["nsink_b"])
            k.op("dve", lambda e: e.tensor_copy(out=amask_b[:], in_=amask[:]), reads=["amask"], writes=["amask_b"])
            k.op("dve", lambda e: e.tensor_copy(out=posf[:], in_=posi[:]), reads=["posi"], writes=["posf"])
            k.op("dve", lambda e: e.tensor_tensor(out=ang[:], in0=posf[:].unsqueeze(2).to_broadcast([128, NTL, 8]),
                                                  in1=invf[:].unsqueeze(1).to_broadcast([128, NTL, 8]), op=ALU.mult),
                 reads=["posf", "invf"], writes=["ang"])
            angs = sb(ph, "angs", [128, NTL, 8], F32)
            angk = sb(ph, "angk", [128, NTL, 8], F32)
            angi = sb(ph, "angi", [128, NTL, 8], I32)
            for (dst, dk, shift) in ((sin_t, "sin_t", 0.0), (cos_t, "cos_t", 0.5 * PI)):
                k.op("dve", lambda e: e.tensor_scalar(out=angs[:], in0=ang[:], scalar1=shift, scalar2=None, op0=ALU.add),
                     reads=["ang"], writes=["angs"])
                k.op("dve", lambda e: e.tensor_scalar(out=angk[:], in0=angs[:], scalar1=1.0 / (2.0 * PI), scalar2=None,
                                                      op0=ALU.mult), reads=["angs"], writes=["angk"])
                k.op("dve", lambda e: e.tensor_copy(out=angi[:], in_=angk[:]), reads=["angk"], writes=["angi"])
                k.op("dve", lambda e: e.tensor_copy(out=angk[:], in_=angi[:]), reads=["angi"], writes=["angk"])
                k.op("dve", lambda e: e.scalar_tensor_tensor(out=dst[:], in0=angk[:], scalar=-2.0 * PI, in1=angs[:],
                                                             op0=ALU.mult, op1=ALU.add),
                     reads=["angk", "angs"], writes=[dk])
                k.op("dve", lambda e: e.tensor_scalar(out=angk[:], in0=dst[:], scalar1=PI, scalar2=2.0 * PI,
                                                      op0=ALU.is_gt, op1=ALU.mult), reads=[dk], writes=["angk"])
                k.op("dve", lambda e: e.tensor_tensor(out=dst[:], in0=dst[:], in1=angk[:], op=ALU.subtract),
                     reads=[dk, "angk"], writes=[dk])
                k.op("dve", lambda e: e.tensor_scalar(out=dst[:], in0=dst[:], scalar1=-3.1415925, scalar2=3.1415925,
                                                      op0=ALU.max, op1=ALU.min), reads=[dk], writes=[dk])
                k.op("act", lambda e: e.activation(out=dst[:], in_=dst[:], func=AF.Sin), reads=[dk], writes=[dk])

            def rope(v3, vkey, nh, tile):
                c_b = cos_t[:, tile, :].unsqueeze(1).to_broadcast([128, nh, 8])
                s_b = sin_t[:, tile, :].unsqueeze(1).to_broadcast([128, nh, 8])
                v1 = v3[:, :, 0:8]
                v2 = v3[:, :, 8:16]
                tt = [rt[i][:, 0:nh, :] for i in range(4)]
                k.op("dve", lambda e: e.tensor_tensor(out=tt[0], in0=v1, in1=c_b, op=ALU.mult), reads=[vkey, "cos_t"], writes=["rt0"])
                k.op("dve", lambda e: e.tensor_tensor(out=tt[1], in0=v2, in1=s_b, op=ALU.mult), reads=[vkey, "sin_t"], writes=["rt1"])
                k.op("dve", lambda e: e.tensor_tensor(out=tt[2], in0=v2, in1=c_b, op=ALU.mult), reads=[vkey, "cos_t"], writes=["rt2"])
                k.op("dve", lambda e: e.tensor_tensor(out=tt[3], in0=v1, in1=s_b, op=ALU.mult), reads=[vkey, "sin_t"], writes=["rt3"])
                k.op("dve", lambda e: e.tensor_tensor(out=v1, in0=tt[0], in1=tt[1], op=ALU.subtract), reads=["rt0", "rt1"], writes=[vkey])
                k.op("dve", lambda e: e.tensor_tensor(out=v2, in0=tt[2], in1=tt[3], op=ALU.add), reads=["rt2", "rt3"], writes=[vkey])

            kvst = contextlib.ExitStack()
            kvw = sb(kvst, "kvw", [128, DC, 512], BF16)
            kvt = sb(kvst, "kvt", [128, 512], F32)
            kd = sb(kvst, "kd", [128, 4, 2, 64], F32)
            wv_ = kvw_d.rearrange("(c p) n -> p c n", p=128)
            for c in range(DC):
                k.dma("pool", "wload", kvw[:, c, :], wv_[:, c, :], writes=["kvw"])
            for tile in range(NTL):
                col0 = 32 + tile * 128
                for dc in range(DC):
                    k.op("pe", lambda e, dc=dc: e.matmul(pj[0][:, :], lhsT=xT_res[:, dc, col0:col0 + 128], rhs=kvw[:, dc, :],
                                                         start=(dc == 0), stop=(dc == DC - 1)),
                         reads=["kvw", "xTall", xkey(tile)], writes=["pj0"], sig=(dc == DC - 1))
                k.op("act", lambda e: e.copy(out=kvt[:], in_=pj[0][:]), reads=["pj0"], writes=["kvt"])
                k.op("dve", lambda e: e.tensor_copy(out=VS[:, tile, :], in_=kvt[:, 256:512]), reads=["kvt"], writes=["VS"])
                rope(kvt[:, 0:256].rearrange("p (h d) -> p h d", h=4), "kvt", 4, tile)
                kv3 = kvt[:, 0:256].rearrange("p (h d) -> p h d", h=4)
                for dup in range(2):
                    k.op("dve", lambda e: e.tensor_copy(out=kd[:, :, dup, :], in_=kv3), reads=["kvt"], writes=["kd"])
                for g in range(4):
                    k.op("pe", lambda e: e.transpose(out=pj[1][:, g * 128:(g + 1) * 128],
                                                     in_=kd[:, g, :, :].rearrange("p a b -> p (a b)"), identity=ident[:]),
                         reads=["kd", "ident"], writes=["pj1"], sig=(g == 3))
                k.op("act", lambda e: e.copy(out=KT[:, :, tile * 128:(tile + 1) * 128],
                                             in_=pj[1][:].rearrange("p (a b) -> p a b", a=4)),
                     reads=["pj1"], writes=["KT"])

            cnt = 0
            k.barrier()
            kvst.close()
            tsteps = table_steps(1, ph) if want_tables1 else iter(())
            per_tile = (NJ + NTL - 2) // (NTL - 1)
            qTs2 = [qTs, sb(ph, "qTs1", [128, 8, 128], BF16)]

            def preA(tile):
                col0 = 32 + tile * 128
                for half in range(2):
                    for dc in range(DC):
                        k.op("pe", lambda e, dc=dc: e.matmul(pj[half][:, :], lhsT=xT_res[:, dc, col0:col0 + 128],
                                                             rhs=wq[:, dc, half * 512:(half + 1) * 512],
                                                             start=(dc == 0), stop=(dc == DC - 1)),
                             reads=["awq", "xTall", xkey(tile)], writes=["pj%d" % half], sig=(dc == DC - 1))
                    k.op("act", lambda e: e.copy(out=qt[:, half * 512:(half + 1) * 512], in_=pj[half][:]),
                         reads=["pj%d" % half], writes=["qt"])
                rope(qt[:].rearrange("p (h d) -> p h d", h=16), "qt", 16, tile)

            def preB(tile):
                qd = qTs2[tile % 2]
                for half in range(2):
                    for q in range(4):
                        c = half * 4 + q
                        k.op("pe", lambda e, c=c, q=q: e.transpose(out=pj[half][:, q * 128:(q + 1) * 128],
                                                                   in_=qt[:, c * 128:(c + 1) * 128], identity=ident[:]),
                             reads=["qt", "ident"], writes=["pj%d" % half], sig=(q == 3))
                    k.op("act", lambda e: e.copy(out=qd[:, half * 4:(half + 1) * 4, :],
                                                 in_=pj[half][:].rearrange("p (a b) -> p a b", a=4)),
                         reads=["pj%d" % half], writes=["qTs%d" % (tile % 2)])

            def heads(tile):
                qd = qTs2[tile % 2]
                qk = "qTs%d" % (tile % 2)
                mi = 0 if tile == 1 else 1

                def front(h):
                    b = h % 2
                    g = h // 4
                    hb = 64 * (h % 2)
                    k.op("pe", lambda e: e.matmul(s_ps[b][:, 0:256], lhsT=qd[hb:hb + 64, h // 2, :],
                                                  rhs=KT[hb:hb + 64, g, (tile - 1) * 128:(tile + 1) * 128],
                                                  start=True, stop=False),
                         reads=[qk, "KT"], writes=["s_ps%d" % b], sig=False)
                    k.op("pe", lambda e: e.matmul(s_ps[b][:, 0:256], lhsT=identb[:], rhs=amask_b[:, mi, :],
                                                  start=False, stop=True),
                         reads=["identb", "amask_b"], writes=["s_ps%d" % b])
                    k.op("dve", lambda e: e.tensor_reduce(out=sst[b][:, 0:1], in_=s_ps[b][:, 0:256], axis=AX.X, op=ALU.max),
                         reads=["s_ps%d" % b], writes=["sstA%d" % b])
                    k.op("dve", lambda e: e.tensor_scalar(out=sst[b][:, 1:2], in0=sst[b][:, 0:1], scalar1=-0.125,
                                                          scalar2=nsink_b[:, h:h + 1], op0=ALU.mult, op1=ALU.min),
                         reads=["sstA%d" % b, "nsink_b"], writes=["sstB%d" % b])
                    k.op("act", lambda e: e.activation(out=pb[b][:], in_=s_ps[b][:, 0:256], func=AF.Exp, bias=sst[b][:, 1:2],
                                                       scale=0.125, accum_out=sst[b][:, 2:3]),
                         reads=["s_ps%d" % b, "sstB%d" % b], writes=["pb%d" % b, "sstC%d" % b])
                    k.op("act", lambda e: e.activation(out=sst[b][:, 3:4], in_=sink_b[:, h:h + 1], func=AF.Exp,
                                                       bias=sst[b][:, 1:2], scale=1.0),
                         reads=["sink_b", "sstB%d" % b], writes=["sstD%d" % b])

                def denom(h):
                    b = h % 2
                    k.op("dve", lambda e: e.tensor_tensor(out=sst[b][:, 4:5], in0=sst[b][:, 2:3], in1=sst[b][:, 3:4], op=ALU.add),
                         reads=["sstC%d" % b, "sstD%d" % b], writes=["sstE%d" % b])
                    k.op("dve", lambda e: e.reciprocal(out=rden[:, h:h + 1], in_=sst[b][:, 4:5]),
                         reads=["sstE%d" % b], writes=["rden"])

                def back(h):
                    b = h % 2
                    g = h // 4
                    for kb_ in range(2):
                        k.op("pe", lambda e, kb_=kb_: e.transpose(out=pT_ps[b][:, kb_ * 128:(kb_ + 1) * 128],
                                                                  in_=pb[b][:, kb_ * 128:(kb_ + 1) * 128], identity=identb[:]),
                             reads=["pb%d" % b, "identb"], writes=["pT_ps%d" % b], sig=(kb_ == 1))
                    k.op("dve", lambda e: e.tensor_copy(out=pT[b][:].rearrange("p a b -> p (a b)"), in_=pT_ps[b][:, 0:256]),
                         reads=["pT_ps%d" % b], writes=["pT%d" % b])
                    for kb_ in range(2):
                        k.op("pe", lambda e, kb_=kb_: e.matmul(o_ps[:, h * 64:(h + 1) * 64], lhsT=pT[b][:, kb_, :],
                                                               rhs=VS[:, tile - 1 + kb_, g * 64:(g + 1) * 64],
                                                               start=(kb_ == 0), stop=(kb_ == 1)),
                             reads=["pT%d" % b, "VS"], writes=["o_ps%d" % (h // 8)], sig=(kb_ == 1))

                def evac(h):
                    if h % 8 == 7:
                        hb8 = h // 8
                        k.op("dve", lambda e: e.tensor_tensor(
                            out=osb[:, hb8 * 512:(hb8 + 1) * 512].rearrange("p (h d) -> p h d", h=8),
                            in0=o_ps[:, hb8 * 512:(hb8 + 1) * 512].rearrange("p (h d) -> p h d", h=8),
                            in1=rden[:, hb8 * 8:(hb8 + 1) * 8].unsqueeze(2).to_broadcast([128, 8, 64]), op=ALU.mult),
                            reads=["o_ps%d" % hb8, "rden"], writes=["osb"])

                front(0)
                front(1)
                denom(0)
                for h in range(16):
                    back(h)
                    if h + 2 < 16:
                        front(h + 2)
                    if h + 1 < 16:
                        denom(h + 1)
                    evac(h)
                    yield h

            def post(tile):
                for half in range(2):
                    for q in range(4):
                        c = half * 4 + q
                        k.op("pe", lambda e, c=c, q=q: e.transpose(out=pj[half][:, q * 128:(q + 1) * 128],
                                                                   in_=osb[:, c * 128:(c + 1) * 128], identity=ident[:]),
                             reads=["osb", "ident"], writes=["pj%d" % half], sig=(q == 3))
                    k.op("act", lambda e: e.copy(out=oT[:, half * 4:(half + 1) * 4, :],
                                                 in_=pj[half][:].rearrange("p (a b) -> p a b", a=4)),
                         reads=["pj%d" % half], writes=["oT"])
                    yield
                for half in range(2):
                    for c in range(8):
                        k.op("pe", lambda e, c=c: e.matmul(pj[half][:, :], lhsT=oT[:, c, :],
                                                           rhs=wo[:, c, half * 512:(half + 1) * 512],
                                                           start=(c == 0), stop=(c == 7)),
                             reads=["oT", "awo"], writes=["pj%d" % half], sig=(c == 7))
                    yield
                k.dma("sp", "xt", xtile[:, 0, :], xin_d[tile * 128:(tile + 1) * 128, :], writes=["xtile"])
                r = rtile[:, 0, :]
                for half in range(2):
                    k.op("dve", lambda e: e.scalar_tensor_tensor(
                        out=r[:, half * 512:(half + 1) * 512], in0=xtile[:, 0, half * 512:(half + 1) * 512], scalar=ALPHA,
                        in1=pj[half][:], op0=ALU.mult, op1=ALU.add),
                        reads=["xtile", "pj%d" % half], writes=["rtile"])
                yield
                o = otile[:, 0, :]
                layer_norm_tm(r, "rtile", o, "otile")
                yield
                k.dma("sp", "xst", xout_d[tile * 128:(tile + 1) * 128, :], o, reads=["otile"])
                to_feature_major(o, "otile", tile, [pj[0], pj[1]], "pj")
                yield

            preA(1)
            preB(1)
            pending = iter(())
            for tile in range(1, NTL):
                if tile + 1 < NTL:
                    preA(tile + 1)
                for h in heads(tile):
                    if h % 2 == 1:
                        for _ in range((per_tile + 7) // 8):
                            next(tsteps, None)
                    next(pending, None)
                    if h == 8 and tile + 1 < NTL:
                        preB(tile + 1)
                for _ in pending:
                    pass
                pending = post(tile)
            for _ in pending:
                pass
            for _ in tsteps:
                pass
        k.barrier()

    allp = ("m0", "s0", "d0", "m1", "s1", "d1")
    phases = allp if phases is None else phases
    want_tables0 = "d0" in phases and "s0" in phases
    want_tables1 = "d1" in phases and "m1" in phases
    if ("d0" in phases and not want_tables0) or ("d1" in phases and not want_tables1):
        phase_w()
    if "m0" in phases:
        phase_m0()
    if "s0" in phases:
        phase_s(0)
    if "d0" in phases:
        phase_d(0, xs_d[0], xs_d[1], False)
    if "m1" in phases:
        phase_m1(xs_d[1], xs_d[2])
    if "s1" in phases:
        phase_s(1)
    if "d1" in phases:
        phase_d(1, xs_d[2], None, True)
    k.finish("sp")
    return nc, k


def _core_inputs(x, positions, S, S_core, core):
    per_seq = S // S_core
    b = core // per_seq
    start = (core % per_seq) * S_core
    NTOK = S_core + 128
    XW = NTOK + 32
    lo = start - 160
    xrows = np.zeros((XW, D), np.float32)
    hm = np.zeros((XW,), np.float32)
    pos = np.zeros((NTOK,), np.int32)
    src_lo = max(lo, 0)
    xrows[src_lo - lo:] = x[b, src_lo:start + S_core]
    hm[src_lo - lo:] = 1.0
    plo = max(start - 128, 0)
    pos[plo - (start - 128):] = positions[b, plo:start + S_core]
    tq = np.arange(128)[:, None]
    tk = np.arange(256)[None, :]
    dd = tq + 128 - tk
    band = (dd >= 0) & (dd < 128)
    first = band & (tk >= 128) if start == 0 else band
    amask = np.stack([np.where(first, 0.0, -1e30), np.where(band, 0.0, -1e30)]).astype(np.float32)
    return {
        "xT": np.ascontiguousarray(xrows.T),
        "xtm": np.ascontiguousarray(xrows[32:]),
        "hm": hm,
        "pos_pn": np.ascontiguousarray(pos.reshape(-1, 128).T),
        "amask": amask,
    }


def _pc(v):
    return np.ascontiguousarray(v.reshape(-1, 128).T)


def _shared_inputs(conv_w_in, conv_b_in, conv_dw, conv_dw_b, conv_ln_g, conv_ln_b, conv_w_out, conv_b_out,
                   kv_w, attn_w_q, attn_sinks, attn_w_o, peer_w_q, peer_sub_keys, peer_u, peer_v,
                   ln_mix_g, ln_mix_b, ln_ffn_g, ln_ffn_b):
    half = 8
    invf = (500000.0 ** (-(np.arange(half, dtype=np.float32) * 2.0 / 16.0))).astype(np.float32)
    UT = np.ascontiguousarray(peer_u.reshape(2, 128, 128, 8, 128).transpose(0, 2, 4, 3, 1))
    keysT = np.ascontiguousarray(peer_sub_keys.transpose(0, 3, 1, 2))
    return {
        "invf": invf,
        "w_in": np.ascontiguousarray(conv_w_in[0]),
        "b_in_pc": _pc(conv_b_in[0]),
        "dw_pc": np.ascontiguousarray(conv_dw[0].T.reshape(8, 128, 31).transpose(1, 0, 2)),
        "dwb_pc": _pc(conv_dw_b[0]),
        "clng_pc": _pc(conv_ln_g[0]),
        "clnb_pc": _pc(conv_ln_b[0]),
        "w_out": np.ascontiguousarray(conv_w_out[0]),
        "b_out": np.ascontiguousarray(conv_b_out[0]),
        "kv_w": np.ascontiguousarray(kv_w),
        "attn_w_q": np.ascontiguousarray(attn_w_q[0]),
        "sinks": np.ascontiguousarray(attn_sinks[0]),
        "attn_w_o": np.ascontiguousarray(attn_w_o[0]),
        "peer_w_q": np.ascontiguousarray(peer_w_q),
        "keysT": keysT,
        "UT": UT,
        "peer_v": np.ascontiguousarray(peer_v),
        "ln_mix_g": np.ascontiguousarray(ln_mix_g),
        "ln_mix_b": np.ascontiguousarray(ln_mix_b),
        "ln_ffn_g": np.ascontiguousarray(ln_ffn_g),
        "ln_ffn_b": np.ascontiguousarray(ln_ffn_b),
    }


def run(x, positions, weights, n_cores=8, dbg=False, phases=None, cores=None):
    x = np.asarray(x, np.float32)
    positions = np.asarray(positions, np.int32)
    B, S, _ = x.shape
    S_core = B * S // n_cores
    shared = _shared_inputs(**{kk: np.asarray(v) for kk, v in weights.items()})
    nc, kb = build(S_core, dbg=dbg, phases=phases)
    print('ninstr', kb.ninstr, {e: kb.cnt[e] for e in ('pe', 'act', 'dve', 'pool')}, flush=True)
    in_maps = []
    cores = list(range(n_cores)) if cores is None else cores
    for c in cores:
        m = dict(shared)
        m.update(_core_inputs(x, positions, S, S_core, c))
        in_maps.append(m)
    res = run_bass_kernel_spmd(nc, in_maps, core_ids=list(range(len(cores))))
    if len(cores) != n_cores:
        return None, res
    out = np.concatenate([r["out"] for r in res.results], axis=0).reshape(B, S, D)
    return out, res


def kernel(x, positions, **weights):
    out, _ = run(x, positions, weights)
    return out.astype(np.float32)
```
